# Optimizing a Trainium2 kernel written in Bass

```python
import jax, jax.numpy as jnp
from jax import lax
import numpy as np

D_MODEL = 1024
BATCH = 2
SEQ = 8192
DEPTH = 1
DEC_BATCH = 32
DEC_SEQ = 32
PAST_LEN = 4096

CHUNK = 64
MIX_WIDTH = D_MODEL
GLA_WIDTH = MIX_WIDTH // 2
GLA_HEADS = 4
GLA_DV = GLA_WIDTH // GLA_HEADS
GLA_DK = GLA_DV // 2
GLA_KEY_WIDTH = GLA_HEADS * GLA_DK
GLA_RANK = 16
GLA_GATE_TEMP = 16.0
GMLP_WIDTH = MIX_WIDTH - GLA_WIDTH
GMLP_GROUPS = 4
GMLP_GROUP_DIM = GMLP_WIDTH // GMLP_GROUPS
GMLP_CHUNK = 128
D_FF = 4 * D_MODEL
N_IN = 2 * GLA_KEY_WIDTH + 2 * GLA_WIDTH + GLA_RANK + 2 * GMLP_WIDTH
DN_ALPHA = (2 * DEPTH) ** 0.25
DN_BETA = (8 * DEPTH) ** -0.25
LN_EPS = 1e-5
RMS_EPS = 1e-6

kernel_name = "hybrid_gla_gmlp_streaming_encoder_step"


def layer_norm(x, g, b):
    xf = x.astype(jnp.float32)
    mu = jnp.mean(xf, axis=-1, keepdims=True)
    var = jnp.mean(jnp.square(xf - mu), axis=-1, keepdims=True)
    y = (xf - mu) * lax.rsqrt(var + LN_EPS) * g.astype(jnp.float32) + b.astype(jnp.float32)
    return y.astype(x.dtype)


def gla_block(S, q, k, v, g):
    C = q.shape[2]
    b = jnp.cumsum(g, axis=2)
    o_inter = jnp.einsum('bhid,bhde->bhie', q * jnp.exp(b), S)
    causal = jnp.tril(jnp.ones((C, C), dtype=bool))
    diff = b[:, :, :, None, :] - b[:, :, None, :, :]
    decay = jnp.exp(jnp.where(causal[None, None, :, :, None], diff, -jnp.inf))
    att = jnp.einsum('bhid,bhjd,bhijd->bhij', q, k, decay)
    o = o_inter + jnp.einsum('bhij,bhje->bhie', att, v)
    b_last = b[:, :, -1:, :]
    S_new = jnp.exp(b_last[:, :, 0, :])[..., None] * S + jnp.einsum(
        'bhjd,bhje->bhde', k * jnp.exp(b_last - b), v)
    return S_new, o


def token_mixers(h, s0, w_in, w_gate2, b_gate2, gla_norm_g, gmlp_ln_g, gmlp_ln_b, w_spatial, b_spatial):
    Bn, T, _ = h.shape
    f32 = jnp.float32
    proj = h @ w_in
    o1 = GLA_KEY_WIDTH
    o2 = o1 + GLA_KEY_WIDTH
    o3 = o2 + GLA_WIDTH
    o4 = o3 + GLA_WIDTH
    o5 = o4 + GLA_RANK
    o6 = o5 + GMLP_WIDTH
    q, k, v, r, glr, zu, zv = jnp.split(proj, [o1, o2, o3, o4, o5, o6], axis=-1)

    q = q.reshape(Bn, T, GLA_HEADS, GLA_DK).astype(f32) * (GLA_DK ** -0.5)
    k = k.reshape(Bn, T, GLA_HEADS, GLA_DK).astype(f32)
    v = v.reshape(Bn, T, GLA_HEADS, GLA_DV).astype(f32)
    glog = jax.nn.log_sigmoid((glr @ w_gate2 + b_gate2).astype(f32)) / GLA_GATE_TEMP
    glog = glog.reshape(Bn, T, GLA_HEADS, GLA_DK)
    c = min(T, CHUNK)
    n = T // c

    def to_blocks(a):
        return a.reshape(Bn, n, c, GLA_HEADS, a.shape[-1]).transpose(1, 0, 3, 2, 4)

    s_fin, o = lax.scan(lambda S, xs: gla_block(S, *xs), s0.astype(f32),
                        (to_blocks(q), to_blocks(k), to_blocks(v), to_blocks(glog)))
    o = o.transpose(1, 0, 3, 2, 4).reshape(Bn, T, GLA_HEADS, GLA_DV)
    o = o * lax.rsqrt(jnp.mean(jnp.square(o), axis=-1, keepdims=True) + RMS_EPS) * gla_norm_g.astype(f32)
    o_gla = o.reshape(Bn, T, GLA_WIDTH).astype(h.dtype) * jax.nn.silu(r)

    zu = jax.nn.gelu(zu, approximate=False)
    zv = layer_norm(jax.nn.gelu(zv, approximate=False), gmlp_ln_g, gmlp_ln_b)
    L = min(T, GMLP_CHUNK)
    m = T // L
    mask = jnp.tril(jnp.ones((L, L), dtype=bool))
    ws = jnp.where(mask[None], w_spatial[:, :L, :L], 0.0)
    vb = zv.reshape(Bn, m, L, GMLP_GROUPS, GMLP_GROUP_DIM)
    s = jnp.einsum('gij,bmjgc->bmigc', ws, vb) + b_spatial[:, :L].T[None, None, :, :, None]
    o_gmlp = zu * s.reshape(Bn, T, GMLP_WIDTH)

    mix = jnp.concatenate([o_gla, o_gmlp], axis=-1)
    return mix, s_fin.astype(s0.dtype), zv


def trunk_layer(x, c, s0, w_ada, b_ada, w_in, w_gate2, b_gate2, gla_norm_g, gmlp_ln_g, gmlp_ln_b,
                w_spatial, b_spatial, w_o, b_o, ln1_g, ln1_b, w_up, b_up, w_down, b_down, ln2_g, ln2_b):
    mod = jax.nn.silu(c) @ w_ada + b_ada
    shift1, scale1, gate1, shift2, scale2, gate2 = [t[:, None, :] for t in jnp.split(mod, 6, axis=-1)]
    h = x * (1.0 + scale1) + shift1
    mix, s_fin, v_rows = token_mixers(h, s0, w_in, w_gate2, b_gate2, gla_norm_g, gmlp_ln_g, gmlp_ln_b,
                                      w_spatial, b_spatial)
    x = layer_norm(DN_ALPHA * x + (1.0 + gate1) * (mix @ w_o + b_o), ln1_g, ln1_b)
    h2 = x * (1.0 + scale2) + shift2
    f = jnp.square(jax.nn.relu(h2 @ w_up + b_up)) @ w_down + b_down
    x = layer_norm(DN_ALPHA * x + (1.0 + gate2) * f, ln2_g, ln2_b)
    return x, s_fin, v_rows


def setup_inputs(seed: int = 0) -> dict:
    key = jax.random.key(seed)
    ks = jax.random.split(key, 32)
    nrm = lambda k, shape, s: jax.random.normal(k, shape, jnp.float32) * s
    D = D_MODEL
    return {
        "x_prompt": nrm(ks[0], (BATCH, SEQ, D), 1.0),
        "x_sample": nrm(ks[1], (DEC_BATCH, DEC_SEQ, D), 1.0),
        "state_gla": nrm(ks[2], (DEPTH, DEC_BATCH, GLA_HEADS, GLA_DK, GLA_DV), 0.5),
        "c_prompt": nrm(ks[3], (BATCH, D), 1.0),
        "c_sample": nrm(ks[4], (DEC_BATCH, D), 1.0),
        "ln_in_g": 1.0 + nrm(ks[5], (D,), 0.02),
        "ln_in_b": nrm(ks[6], (D,), 0.02),
        "w_ada": nrm(ks[7], (DEPTH, D, 6 * D), 0.1 * D ** -0.5),
        "b_ada": nrm(ks[8], (DEPTH, 6 * D), 0.01),
        "w_in": nrm(ks[9], (DEPTH, D, N_IN), D ** -0.5),
        "w_gate2": nrm(ks[10], (DEPTH, GLA_RANK, GLA_KEY_WIDTH), GLA_RANK ** -0.5),
        "b_gate2": nrm(ks[11], (DEPTH, GLA_KEY_WIDTH), 0.1),
        "gla_norm_g": 1.0 + nrm(ks[12], (DEPTH, GLA_DV), 0.02),
        "gmlp_ln_g": 1.0 + nrm(ks[13], (DEPTH, GMLP_WIDTH), 0.02),
        "gmlp_ln_b": nrm(ks[14], (DEPTH, GMLP_WIDTH), 0.02),
        "w_spatial": nrm(ks[15], (DEPTH, GMLP_GROUPS, GMLP_CHUNK, GMLP_CHUNK), 0.5 * GMLP_CHUNK ** -0.5),
        "b_spatial": 1.0 + nrm(ks[16], (DEPTH, GMLP_GROUPS, GMLP_CHUNK), 0.02),
        "w_o": nrm(ks[17], (DEPTH, MIX_WIDTH, D), DN_BETA * MIX_WIDTH ** -0.5),
        "b_o": nrm(ks[18], (DEPTH, D), 0.01),
        "ln1_g": 1.0 + nrm(ks[19], (DEPTH, D), 0.02),
        "ln1_b": nrm(ks[20], (DEPTH, D), 0.02),
        "w_up": nrm(ks[21], (DEPTH, D, D_FF), D ** -0.5),
        "b_up": nrm(ks[22], (DEPTH, D_FF), 0.01),
        "w_down": nrm(ks[23], (DEPTH, D_FF, D), DN_BETA * D_FF ** -0.5),
        "b_down": nrm(ks[24], (DEPTH, D), 0.01),
        "ln2_g": 1.0 + nrm(ks[25], (DEPTH, D), 0.02),
        "ln2_b": nrm(ks[26], (DEPTH, D), 0.02),
    }


def reference(x_prompt, x_sample, state_gla, c_prompt, c_sample, ln_in_g, ln_in_b, w_ada, b_ada, w_in,
              w_gate2, b_gate2, gla_norm_g, gmlp_ln_g, gmlp_ln_b, w_spatial, b_spatial, w_o, b_o,
              ln1_g, ln1_b, w_up, b_up, w_down, b_down, ln2_g, ln2_b):
    xp = layer_norm(x_prompt, ln_in_g, ln_in_b)
    xs = layer_norm(x_sample, ln_in_g, ln_in_b)
    s_zero = jnp.zeros((x_prompt.shape[0], GLA_HEADS, GLA_DK, GLA_DV), state_gla.dtype)
    new_p, new_s, new_v = [], [], []
    for l in range(DEPTH):
        params = (w_ada[l], b_ada[l], w_in[l], w_gate2[l], b_gate2[l], gla_norm_g[l], gmlp_ln_g[l],
                  gmlp_ln_b[l], w_spatial[l], b_spatial[l], w_o[l], b_o[l], ln1_g[l], ln1_b[l],
                  w_up[l], b_up[l], w_down[l], b_down[l], ln2_g[l], ln2_b[l])
        xp, sp, _ = trunk_layer(xp, c_prompt, s_zero, *params)
        xs, ss, vs = trunk_layer(xs, c_sample, state_gla[l], *params)
        new_p.append(sp)
        new_s.append(ss)
        new_v.append(vs)
    new_state_gla_prompt = jnp.stack(new_p)
    new_state_gla_sample = jnp.stack(new_s)
    new_gmlp_v_sample = jnp.stack(new_v)
    return (xp, xs, new_state_gla_prompt, new_state_gla_sample, new_gmlp_v_sample)
```

```python
import numpy as np
import concourse.bass as bass
import concourse.mybir as mybir
from concourse.bass_utils import run_bass_kernel_spmd

F32 = mybir.dt.float32
BF16 = mybir.dt.bfloat16
AF = mybir.ActivationFunctionType
ALU = mybir.AluOpType

D = 1024
NIN = 2576
DFF = 4096
NT = 2
NB = NT * 128
NPB = 2048 // NB
NPRE = NPB
ALPHA = 2.0 ** 0.25
LN_EPS = 1e-5
RMS_EPS = 1e-6
RING = 5
DO_SAMPLE = True
XLAT = 600.0
USE_CC = True
FIXED = ()
STOP = 99
V_GIN, V_BIN, V_G1, V_B1, V_SH1, V_SC1, V_SH2, V_SC2, V_BUP, V_GN = 0, 8, 16, 24, 32, 40, 48, 56, 64, 96
NV = 97


class Buf:
    def __init__(self, name, multi=False, excl=False):
        self.name = name
        self.excl = excl
        self.multi = multi
        self.full = None
        self.partials = []
        self.readers = []
        self.prev_readers = []


class Op:
    __slots__ = ("idx", "ek", "fn", "deps", "is_dma", "semname", "dur", "issue", "t_end", "pos", "nsucc", "users", "nbytes")

    def __init__(self, idx, ek, fn, deps, is_dma, semname, dur, issue):
        self.idx, self.ek, self.fn, self.deps = idx, ek, fn, deps
        self.is_dma, self.semname, self.dur, self.issue = is_dma, semname, dur, issue
        self.t_end = None
        self.pos = None
        self.users = []
        self.nbytes = 0.0


class Eng:
    def __init__(self, key, sem):
        self.key = key
        self.sem = sem
        self.cmds = []


class Prog:
    def __init__(self, nc, sems):
        self.nc = nc
        self.eng = {k: Eng(k, sems[k]) for k in ("pe", "act", "dve", "pool", "sp")}
        self.dsem = {}
        self.ops = []

    def _deps(self, reads, writes):
        deps = set()
        for b in reads:
            if b.full is not None:
                deps.add(b.full)
            deps.update(b.partials)
            if b.excl:
                deps.update(b.readers)
        for b in writes:
            if b.full is not None:
                deps.add(b.full)
            deps.update(b.readers)
            if b.multi:
                deps.update(b.prev_readers)
            else:
                deps.update(b.partials)
        return deps

    def _record(self, op, reads, writes):
        for b in reads:
            b.readers.append(op)
        for b in writes:
            if b.multi:
                if b.readers:
                    b.partials = [op]
                    b.prev_readers = b.readers
                    b.readers = []
                else:
                    b.partials.append(op)
            else:
                b.full = op
                b.partials = []
                b.readers = []
                b.prev_readers = []
        self.ops.append(op)

    def op(self, ek, reads, writes, fn, dur=150.0):
        op = Op(len(self.ops), ek, fn, self._deps(reads, writes), False, None, dur, dur)
        self._record(op, reads, writes)
        return op

    def dma(self, qk, semname, reads, writes, fn, dur=2000.0, nbytes=65536.0):
        issue = 1000.0 if qk == "pool" else 80.0
        op = Op(len(self.ops), qk, fn, self._deps(reads, writes), True, semname, dur, issue)
        op.nbytes = nbytes
        self._record(op, reads, writes)
        return op

    def schedule(self):
        ops = self.ops
        for o in ops:
            o.nsucc = len(o.deps)
            for d in o.deps:
                d.users.append(o)
        fixed = getattr(self, "fixed", set())
        nxt_fixed = {k: 0 for k in self.eng}
        per_eng = {k: [o for o in ops if o.ek == k] for k in self.eng}
        ready = {k: [] for k in self.eng}
        free = {k: 0.0 for k in self.eng}
        order = {k: [] for k in self.eng}
        rt = {}
        for o in ops:
            if o.nsucc == 0:
                ready[o.ek].append(o)
                rt[o] = 0.0
        n_done = 0
        dma_free = 0.0
        while n_done < len(ops):
            best = None
            for k, lst in ready.items():
                if not lst:
                    continue
                f = free[k]
                cand = None
                if k in fixed:
                    lst = [o for o in lst if o is per_eng[k][nxt_fixed[k]]]
                    if not lst:
                        continue
                for o in lst:
                    st = max(f, rt[o])
                    key = (st, o.idx) if rt[o] > f else (f, o.idx)
                    if cand is None or key < cand[0]:
                        cand = (key, o, st)
                if best is None or cand[0] < best[0]:
                    best = cand
            key, o, st = best
            k = o.ek
            ready[k].remove(o)
            nxt_fixed[k] += 1
            free[k] = st + o.issue
            if o.is_dma:
                t0 = max(st + o.issue, dma_free)
                dma_free = t0 + o.nbytes / 300.0
                o.t_end = dma_free + 2000.0
            else:
                o.t_end = st + o.dur
            order[k].append(o)
            n_done += 1
            for u in o.users:
                u.nsucc -= 1
                r = max(rt.get(u, 0.0), o.t_end + (XLAT if u.ek != o.ek else 0.0))
                rt[u] = r
                if u.nsucc == 0:
                    ready[u.ek].append(u)
        self.makespan = max(o.t_end for o in ops)
        return order

    def codegen(self):
        order = self.schedule()
        cnt = {k: 0 for k in self.eng}
        dcnt = {n: 0 for n in self.dsem}
        for k, lst in order.items():
            for o in lst:
                if o.is_dma:
                    dcnt[o.semname] += 1
                    o.pos = (self.dsem[o.semname], 16 * dcnt[o.semname], "dma:" + o.semname)
                else:
                    cnt[k] += 1
                    o.pos = (self.eng[k].sem, cnt[k], k)
        for k, lst in order.items():
            E = self.eng[k]
            waited = {}
            for o in lst:
                need = {}
                for d in o.deps:
                    sem, val, key = d.pos
                    if key == k and k == "pe":
                        continue
                    if key not in need or need[key][1] < val:
                        need[key] = (sem, val)
                for key, (sem, val) in need.items():
                    if waited.get(key, 0) < val:
                        E.cmds.append(("wait", sem, val))
                        waited[key] = val
                if o.is_dma:
                    E.cmds.append(("ins", o.fn, self.dsem[o.semname], 16))
                else:
                    E.cmds.append(("ins", o.fn, E.sem, 1))
        E = self.eng["sp"]
        for name, n in dcnt.items():
            if n > 0:
                E.cmds.append(("wait", self.dsem[name], 16 * n))

    def replay(self, ek, handle):
        for c in self.eng[ek].cmds:
            if c[0] == "wait":
                handle.wait_ge(c[1], c[2])
            else:
                ins = c[1](handle)
                ins.then_inc(c[2], c[3])


def build_program():
    nc = bass.Bass("TRN2", target_bir_lowering=False)

    def din(name, shape):
        return nc.dram_tensor(name, list(shape), F32, kind="ExternalInput").ap()

    def dout(name, shape):
        return nc.dram_tensor(name, list(shape), F32, kind="ExternalOutput").ap()

    xmain = din("xmain", [2176, D])
    qmask_d = din("qmask", [128, 4])
    cT_p = din("cT_p", [128, 8, 128])
    cT_s = din("cT_s", [128, 8, 128])
    st0 = din("st0", [4, 128, 2, 128])
    w_ada = din("w_ada", [D, 6 * D])
    w_in = din("w_in", [D, NIN])
    w_o = din("w_o", [D, D])
    w_up = din("w_up", [D, DFF])
    w_down = din("w_down", [DFF, D])
    vecT_d = din("vecT", [128, NV])
    bc_names = ["g_in", "b_in", "b_o", "g1", "b1", "b_down", "g2", "b2", "ba_g1", "ba_g2"]
    bc_d = {n: din("bc_" + n, [128, D]) for n in bc_names}
    bg2_d = din("bc_bg2", [128, 256])
    gg_d = din("bc_gg", [128, 512])
    gb_d = din("bc_gb", [128, 512])
    wg2_d = din("w_gate2", [16, 256])
    wsp_p_d = din("wsp_p", [4, 128, 128])
    wsp_s_d = din("wsp_s", [4, 128, 128])
    bsp_p_d = din("bsp_p", [1, 512])
    bsp_s_d = din("bsp_s", [1, 512])
    ident_d = din("ident", [128, 128])
    maskp_d = din("maskp", [128, 128])
    masks_d = din("masks", [128, 128])
    up_d = din("U_p", [128, 128])
    us_d = din("U_s", [128, 128])
    upre_d = din("U_pre", [128, 128])
    tril_d = din("trilT", [128, 128])
    cmask_p_d = din("cmask_p", [128, 4])
    cmask_s_d = din("cmask_s", [128, 4])

    y_out = dout("y", [2176, D])
    sp_out = dout("s_out_p", [128, 2, 128])
    ss_out = dout("s_out_s", [4, 128, 2, 128])
    zv_out = dout("zv_out", [128, 512])

    from contextlib import ExitStack
    es = ExitStack()

    def sb(name, shape, dt=F32):
        return es.enter_context(nc.sbuf_tensor("sb_" + name, list(shape), dt))

    def sem(name):
        return es.enter_context(nc.semaphore(name))

    with es:
        sems = {k: sem("s_" + k) for k in ("pe", "act", "dve", "pool", "sp")}
        P = Prog(nc, sems)
        for n in ["const", "R0", "R1", "yst", "misc", "st_zv", "st_ss", "st_sp", "m_a", "m_b", "m_c", "ccw", "ccr"] + ["ring%d" % i for i in range(RING)] + ["ws%d" % i for i in range(RING)]:
            P.dsem[n] = sem("d_" + n)

        MULTI = {"hT", "mixT", "h2T", "big", "srT", "guT", "qsel", "ktT"}

        def T(name, shape, dt=F32):
            t = sb(name, shape, dt)
            return t, Buf(name, multi=(name in MULTI))

        ident, b_ident = T("ident", [128, 128])
        maskp, b_maskp = T("maskp", [128, 128])
        masks, b_masks = T("masks", [128, 128])
        trilT, b_tril = T("trilT", [128, 128])
        vecT, b_vecT = T("vecT", [128, NV])
        qmask, b_qmask = T("qmask", [128, 4])
        Dt, b_Dt = T("Dt", [128, 2])
        Dp, b_Dp = T("Dp", [128, 2])
        Gt, b_Gt = T("Gt", [128, 258])
        ones_bf, b_ones = T("ones_bf", [128, 128], BF16)
        gn128, b_gn = T("gn128", [128, 1])
        bg2, b_bg2 = T("bg2", [128, 256])
        gg, b_gg = T("gg", [128, 512])
        gb, b_gb = T("gb", [128, 512])
        wg2, b_wg2 = T("wg2", [16, 256], BF16)
        wg, b_wg = T("wg", [128, 8, 16], BF16)
        wsT = {}
        for st_ in ("p", "s"):
            wsT[st_] = T("wsT_" + st_, [128, 4, 128], BF16)
        bsp = {}
        for st_ in ("p", "s"):
            bsp[st_] = T("bsp_" + st_, [128, 2, 512], BF16)
        SC5, b_SC5 = T("SC5", [128, 8, 5], BF16)
        MT, b_MT = T("MT", [128, 4, 8, 5])
        AB, b_AB = T("AB", [128, 4, 8, 5])
        modP = {n: T("mod_" + n, [128, D]) for n in ("P1", "P2", "g2", "b2")}
        modS = {st_: {n: T("mod_%s_%s" % (n, st_), [128, D]) for n in ("G1", "C1", "G2", "C2")}
                for st_ in ("p", "s")}
        ring = [T("ring%d" % i, [128, 8, 512], BF16) for i in range(RING)]
        Rb = [T("R%d" % i, [128, NT, D]) for i in range(2)]
        hT, b_hT = T("hT", [128, 8, NB], BF16)
        SCT = {"p": (hT[:, :, 0:128], b_hT), "s": (hT[:, :, 128:256], b_hT)}
        mixT, b_mixT = T("mixT", [128, 8, NB], BF16)
        h2T, b_h2T = T("h2T", [128, 8, NB], BF16)
        glrT, b_glrT = T("glrT", [16, NB], BF16)
        spt, b_spt = T("sp", [128, NT, 256])
        ebT, b_ebT = T("ebT", [128, 2, NB])
        enbT, b_enbT = T("enbT", [128, 2, NB])
        Et, b_Et = T("Et", [128, 2, 8])
        ktT, b_ktT = T("ktT", [128, 2, NB], BF16)
        ktm, b_ktm = T("ktm", [128, NT, 256], BF16)
        vbf, b_vbf = T("vbf", [128, NT, 512], BF16)
        srT, b_srT = T("srT", [128, 4, NB], BF16)
        guT, b_guT = T("guT", [128, 4, NB], BF16)
        zvn, b_zvn = T("zvn", [128, NT, 512], BF16)
        attm2 = [T("attm%d" % i, [128, 4, 128], BF16) for i in range(2)]
        osq2 = [T("osq%d" % i, [128, 512], BF16) for i in range(2)]
        qsel, b_qsel = T("qsel", [128, 4, NB], BF16)
        vsel, b_vsel = T("vsel", [128, 4, 512], BF16)
        ones0, b_ones0 = T("ones0", [128, 128], BF16)
        cmask = {"p": T("cmask_p", [128, 4]), "s": T("cmask_s", [128, 4])}
        S, b_S = T("S", [128, 2, 128])
        Se, b_Se = T("Se", [128, 2, 128])
        Sb, b_Sb = T("Sb", [128, 4, 2, 128], BF16)
        S0s, b_S0s = T("S0s", [128, 4, 2, 128])
        stat, b_stat = T("stat", [128, 2, 6])
        mv, b_mv = T("mv", [128, 2])
        rs, b_rs = T("rs", [128, 2])
        tA, b_tA = T("tA", [128, D])
        enbtm, b_enbtm = tA[:, 768:1024], b_tA
        tB, b_tB = T("tB", [128, D])
        tC, b_tC = T("tC", [128, D])
        tZ, b_tZ = T("tZ", [128, D])
        Ubf = {}
        for k_, nm in enumerate(("p", "s", "pre")):
            Ubf[nm] = T("Ubf_" + nm, [128, 128], BF16)
        big, b_big = T("big", [128, 32, NB], BF16)
        yst, b_yst = T("yst", [128, D])

        banks = []
        for i in range(8):
            t = es.enter_context(nc.psum_tensor("bank%d" % i, [128, 512], F32))
            banks.append((t, Buf("bank%d" % i, excl=True)))
        bank_ctr = [0]

        pinned = set()

        def bank(pin=False):
            for _ in range(8):
                i = bank_ctr[0] % 8
                bank_ctr[0] += 1
                if i not in pinned:
                    if pin:
                        pinned.add(i)
                    return banks[i]
            raise RuntimeError("all PSUM banks pinned")

        def unpin(pbuf):
            for i, (t, b) in enumerate(banks):
                if b is pbuf:
                    pinned.discard(i)


        def fsz(ap):
            n = 1
            for d in ap.shape[1:]:
                n *= int(d)
            return n

        def act(reads, writes, out, in_, func, bias=None, scale=None):
            kw = {}
            if bias is not None:
                kw["bias"] = bias
            if scale is not None:
                kw["scale"] = scale
            P.op("act", reads, writes, lambda e: e.activation(out=out, in_=in_, func=func, **kw),
                 dur=250.0 + fsz(in_) / 1.1)

        def tt(reads, writes, out, in0, in1, op, eng="dve"):
            P.op(eng, reads, writes, lambda e: e.tensor_tensor(out=out, in0=in0, in1=in1, op=op),
                 dur=120.0 + fsz(in0) / 0.9)

        def ts(reads, writes, out, in0, s1, s2, op0, op1=None, eng="dve"):
            d = 120.0 + fsz(in0) / 0.9
            if op1 is None:
                P.op(eng, reads, writes, lambda e: e.tensor_scalar(out=out, in0=in0, scalar1=s1, scalar2=None, op0=op0), dur=d)
            else:
                P.op(eng, reads, writes, lambda e: e.tensor_scalar(out=out, in0=in0, scalar1=s1, scalar2=s2, op0=op0, op1=op1), dur=d)

        def stt(reads, writes, out, in0, scalar, in1, op0, op1, eng="dve"):
            P.op(eng, reads, writes, lambda e: e.scalar_tensor_tensor(out=out, in0=in0, scalar=scalar, in1=in1, op0=op0, op1=op1),
                 dur=120.0 + fsz(in0) / 0.9)

        def cpy(reads, writes, out, in_, eng="dve"):
            P.op(eng, reads, writes, lambda e: e.tensor_copy(out=out, in_=in_), dur=120.0 + fsz(in_) / 0.9)

        def rsqrt_(sbuf_, src, eps, dbuf, dst):
            act([sbuf_], [dbuf], dst, src, AF.Ln, bias=float(eps))
            act([dbuf], [dbuf], dst, dst, AF.Exp, scale=-0.5)

        def mm(reads, writes, out, lhsT, rhs, start=True, stop=True, tp=None):
            kw = {}
            if tp is not None:
                kw["tile_position"] = tp
            n = max(64, fsz(rhs))
            d = n / 2.0 * (4.0 if rhs.dtype == F32 else 1.0)
            P.op("pe", reads, writes, lambda e: e.matmul(out, lhsT, rhs, start=start, stop=stop,
                                                         skip_group_check=True, **kw), dur=d)

        def tr(reads, writes, out, in_):
            P.op("pe", reads, writes, lambda e: e.transpose(out, in_, ident[:, :]), dur=215.0)

        def ld(q, semname, writes, out, in_, reads=()):
            nbytes = (4.0 if in_.dtype == F32 else 2.0) * int(out.shape[0]) * fsz(out)
            return P.dma(q, semname, list(reads), writes, lambda e: e.dma_start(out=out, in_=in_), nbytes=nbytes)

        Rb[0][1].multi = True
        Rb[1][1].multi = True
        cl = [(ident, b_ident, ident_d), (maskp, b_maskp, maskp_d), (masks, b_masks, masks_d),
              (trilT, b_tril, tril_d),
              (vecT, b_vecT, vecT_d), (qmask, b_qmask, qmask_d), (bg2, b_bg2, bg2_d), (gg, b_gg, gg_d),
              (gb, b_gb, gb_d), (cmask["p"][0], cmask["p"][1], cmask_p_d), (cmask["s"][0], cmask["s"][1], cmask_s_d)]
        bufs_c = []
        for t, b, d in cl:
            ld("sp", "const", [b], t[:], d)
            bufs_c.append(b)
        for n, src in (("P1", "g_in"), ("P2", "g1"), ("g2", "g2"), ("b2", "b2")):
            t, b = modP[n]
            ld("sp", "const", [b], t[:], bc_d[src])
            bufs_c.append(b)
        for st_ in ("p", "s"):
            t, b = modS[st_]["C1"]
            ld("sp", "const", [b], t[:], bc_d["b_in"])
            bufs_c.append(b)
            t, b = modS[st_]["C2"]
            ld("sp", "const", [b], t[:], bc_d["b1"])
            bufs_c.append(b)
        ld("sp", "const", [b_tA], tA[:], bc_d["ba_g1"]); bufs_c.append(b_tA)
        ld("sp", "const", [b_tB], tB[:], bc_d["ba_g2"]); bufs_c.append(b_tB)
        ld("sp", "const", [Rb[0][1]], Rb[0][0][:, 0, :].rearrange("p (k m) -> p k m", k=8), cT_p); bufs_c.append(Rb[0][1])
        ld("sp", "const", [Rb[1][1]], Rb[1][0][:, 0, :].rearrange("p (k m) -> p k m", k=8), cT_s); bufs_c.append(Rb[1][1])
        ld("sp", "const", [b_S0s], S0s[:], st0.rearrange("s p a e -> p s a e")); bufs_c.append(b_S0s)
        ld("sp", "const", [b_yst], yst[:], bc_d["b_o"]); bufs_c.append(b_yst)
        ld("sp", "const", [b_tZ], tZ[:], bc_d["b_down"]); bufs_c.append(b_tZ)
        for ri, (wsrc, bsrc) in enumerate(((wsp_p_d, bsp_p_d), (wsp_s_d, bsp_s_d))):
            ld("sp", "const", [Rb[ri][1]], Rb[ri][0][:, 1, 0:512].rearrange("p (g j) -> p g j", g=4), wsrc.rearrange("g i j -> i g j"))
            ld("sp", "const", [Rb[ri][1]], Rb[ri][0][0:1, 1, 512:1024], bsrc)
        for k_, ud in enumerate((up_d, us_d, upre_d)):
            ld("sp", "const", [b_tC], tC[:, k_ * 128:(k_ + 1) * 128], ud)
        bufs_c.append(b_tC)
        const_ops = [o for o in P.ops if o.is_dma and o.semname == "const"]
        Rb[0][1].multi = False
        Rb[1][1].multi = False
        b_cd = Buf("const_done")
        bufs_c.append(b_cd)
        for b in bufs_c:
            b.full, b.partials, b.readers, b.prev_readers = const_ops[-1], list(const_ops), [], []
        ld("pool", "misc", [b_wg2], wg2[:], wg2_d, reads=[b_cd])
        ld("pool", "misc", [b_wg], wg[:], w_in[:, 1536:1552].rearrange("(k p) n -> p k n", p=128), reads=[b_cd])
        wpk = big[:, 0:8, :]
        wpv = big[:, 8:24, :].rearrange("p (k a) n -> p k (a n)", a=2)
        ld("pool", "misc", [b_big], wpk, w_in[:, 256:512].rearrange("(k p) n -> p k n", p=128), reads=[b_cd])
        ld("pool", "misc", [b_big], wpv, w_in[:, 512:1024].rearrange("(k p) n -> p k n", p=128), reads=[b_cd])
        misc_ops = [o for o in P.ops if o.is_dma and o.semname == "misc"]
        for b in (b_wg2, b_wg, b_big):
            b.full, b.partials, b.readers = misc_ops[-1], list(misc_ops), []

        tiles = []

        def wtile(src2d, r0, c0, ncols=512):
            return src2d[r0:r0 + 1024, c0:c0 + ncols].rearrange("(k p) n -> p k n", p=128)

        for j in range(12):
            tiles.append(wtile(w_ada, 0, 512 * j))

        def block_tiles():
            tl = [wtile(w_in, 0, 0), wtile(w_in, 0, 512), wtile(w_in, 0, 2064),
                  wtile(w_in, 0, 1024), wtile(w_in, 0, 1552),
                  wtile(w_o, 0, 0), wtile(w_o, 0, 512)]
            for j in range(8):
                tl.append(wtile(w_up, 0, 512 * j))
            for half in range(2):
                for g in range(4):
                    tl.append(wtile(w_down, 1024 * g, 512 * half))
            return tl
        NBT = 23
        wscr = nc.dram_tensor("wscr", [NBT, 128, 8 * 512], BF16, kind="Internal").ap()
        scr_b = [Buf("wscr%d" % j) for j in range(NBT)]
        nblocks_chain = NPB + (1 if DO_SAMPLE else 0)
        wstate = {"next_load": 0, "next_use": 0}

        def convert_weights():
            base = wstate["next_load"]
            for j, tsrc in enumerate(block_tiles()):
                slot = (base + j) % RING
                t, b = ring[slot]
                ld("pool", "ring%d" % slot, [b], t[:], tsrc)
                P.dma("pool", "ws%d" % slot, [b], [scr_b[j]],
                      lambda e, j=j, t=t: e.dma_start(out=wscr[j].rearrange("p (k n) -> p k n", k=8), in_=t[:]),
                      dur=6000.0, nbytes=1048576.0)
            wstate["conv_base"] = base
            for bi_ in range(nblocks_chain):
                for j in range(NBT):
                    tiles.append(("plain", wscr[j].rearrange("p (k n) -> p k n", k=8), j))

        def w_issue(upto):
            while wstate["next_load"] < min(upto, len(tiles)):
                i = wstate["next_load"]
                t, b = ring[i % RING]
                ent = tiles[i]
                if not isinstance(ent, tuple):
                    ld("pool", "ring%d" % (i % RING), [b], t[:], ent, reads=[b_cd])
                elif ent[0] == "plain":
                    ld("pool", "ring%d" % (i % RING), [b], t[:], ent[1], reads=[scr_b[ent[2]]])
                wstate["next_load"] += 1

        def w_acquire():
            i = wstate["next_use"]
            wstate["next_use"] += 1
            assert i < wstate["next_load"]
            return ring[i % RING]

        def w_release(n=1):
            w_issue(wstate["next_load"] + n)

        w_issue(RING)

        P.op("dve", [], [b_ones], lambda e: e.memset(ones_bf[:], 1.0))
        for k_, nm in enumerate(("p", "s", "pre")):
            cpy([b_tC], [Ubf[nm][1]], Ubf[nm][0][:, :], tC[:, k_ * 128:(k_ + 1) * 128])
        P.op("dve", [], [b_ones0], lambda e: e.memset(ones0[:], 0.0))
        P.op("dve", [b_ones0], [b_ones0], lambda e: e.memset(ones0[0:1, :], 1.0))
        b_qsel.multi = False
        P.op("dve", [], [b_qsel], lambda e: e.memset(qsel[:], 0.0))
        b_qsel.multi = True
        for st__ in ("p", "s"):
            P.op("dve", [], [bsp[st__][1]], lambda e, st__=st__: e.memset(bsp[st__][0][:], 0.0))
        ts([b_vecT], [b_gn], gn128[:], vecT[:, V_GN:V_GN + 1], float(np.sqrt(128.0)), None, ALU.mult)
        for st_, rb in (("p", Rb[0]), ("s", Rb[1])):
            t, b = SCT[st_]
            act([rb[1]], [b], t, rb[0][:, 0, :].rearrange("p (k m) -> p k m", k=8), AF.Silu)
        cpy([SCT["p"][1]], [b_SC5], SC5[:, :, 0:1], SCT["p"][0][:, :, 0:1])
        cpy([SCT["s"][1]], [b_SC5], SC5[:, :, 1:5], SCT["s"][0][:, :, 0:128:32])
        mt_slot = {0: (0, 0), 1: (0, 4), 2: (1, 0), 3: (1, 4), 6: (2, 0), 7: (2, 4), 8: (3, 0), 9: (3, 4)}
        vcol = {0: V_SH1, 1: V_SC1, 2: V_SH2, 3: V_SC2}
        for j in range(12):
            wt, wb = w_acquire()
            if j in (4, 5, 10, 11):
                gname = "G1" if j < 6 else "G2"
                bat, bab = (tA, b_tA) if j < 6 else (tB, b_tB)
                hf = j % 2 if j < 6 else (j - 10)
                for st_ in ("p", "s"):
                    pt, pb = bank()
                    for kc in range(8):
                        mm([SCT[st_][1], wb], [pb], pt[:, :], SCT[st_][0][:, kc, :], wt[:, kc, :],
                           start=(kc == 0), stop=(kc == 7))
                    gt, gbuf = modS[st_][gname]
                    stt([pb, bab], [gbuf], gt[:, hf * 512:(hf + 1) * 512], pt[:, :], 1.0,
                        bat[:, hf * 512:(hf + 1) * 512], ALU.add, ALU.add)
            else:
                v, c0 = mt_slot[j]
                for cc in range(4):
                    pt, pb = bank()
                    for kc in range(8):
                        mm([b_SC5, wb], [pb], pt[:, 0:5], wt[:, kc, cc * 128:(cc + 1) * 128], SC5[:, kc, :],
                           start=(kc == 0), stop=(kc == 7))
                    col = vcol[v] + c0 + cc
                    ts([pb, b_vecT], [b_MT], MT[:, v, c0 + cc, :], pt[:, 0:5], vecT[:, col:col + 1], None, ALU.add)
            w_release()
        convert_weights()
        w_issue(wstate["next_load"] + RING)
        for kc in range(8):
            for (ai, bi_, sc, sh, gcol, bcol) in ((0, 1, 1, 0, V_GIN, V_BIN), (2, 3, 3, 2, V_G1, V_B1)):
                ts([b_MT, b_vecT], [b_AB], AB[:, ai, kc, :], MT[:, sc, kc, :], 1.0,
                   vecT[:, gcol + kc:gcol + kc + 1], ALU.add, ALU.mult)
                ts([b_MT, b_vecT], [b_AB], AB[:, bi_, kc, :], MT[:, sc, kc, :], 1.0,
                   vecT[:, bcol + kc:bcol + kc + 1], ALU.add, ALU.mult)
                tt([b_AB, b_MT], [b_AB], AB[:, bi_, kc, :], AB[:, bi_, kc, :], MT[:, sh, kc, :], ALU.add)
        for st_ in ("p", "s"):
            for (gn_, cn_, bt, bb) in (("G1", "C1", yst, b_yst), ("G2", "C2", tZ, b_tZ)):
                gt, gbuf = modS[st_][gn_]
                ct, cbuf = modS[st_][cn_]
                tt([gbuf, bb], [b_tC], tC[:], gt[:], bt[:], ALU.mult)
                stt([cbuf, b_tC], [cbuf], ct[:], ct[:], ALPHA, tC[:], ALU.mult, ALU.add)
        for n in ("P1", "P2"):
            t, b = modP[n]
            ts([b], [b], t[:], t[:], ALPHA, None, ALU.mult)
        for ri, (st_, mk, mkb) in enumerate((("p", trilT, b_tril), ("s", masks, b_masks))):
            Rt_, Rb_ = Rb[ri]
            for g in range(4):
                pt, pb = bank()
                tr([Rb_, b_ident], [pb], pt[:, 0:128], Rt_[:, 1, g * 128:(g + 1) * 128])
                tt([pb, mkb], [wsT[st_][1]], wsT[st_][0][:, g, :], pt[:, 0:128], mk[:, :], ALU.mult)
            t, b = bsp[st_]
            src = Rt_[0:1, 1, 512:1024]
            cpy([Rb_], [b], t[0:1, 0, :], src)
            cpy([b], [Rb_], Rt_[0:1, 1, 0:512], t[0:1, 0, :])
            tt([Rb_], [b], t[0:1, 1, :], src, Rt_[0:1, 1, 0:512], ALU.subtract)

        stat2, b_stat2 = T("stat2", [128, 2, 6])
        mv2, b_mv2 = T("mv2", [128, 2])
        rs2, b_rs2 = T("rs2", [128, 2])
        Sos, b_Sos = tZ[:, :].rearrange("p (s a e) -> p s a e", s=4, a=2), b_tZ
        sph, b_sph = T("sph", [128, NT, 2, 256], BF16)
        SS = {"a": (stat, b_stat, mv, b_mv, rs, b_rs), "b": (stat2, b_stat2, mv2, b_mv2, rs2, b_rs2)}
        def layer_norm_rows(Rt, Rbuf, t_, out_ap, out_bufs, ss="a"):
            st, bst, mv_, bmv, rs_, brs = SS[ss]
            for hh in range(2):
                P.op("dve", [Rbuf], [bst], lambda e, hh=hh: e.bn_stats(out=st[:, hh, :], in_=Rt[:, t_, hh * 512:(hh + 1) * 512]), dur=700.0)
            P.op("dve", [bst], [bmv], lambda e: e.bn_aggr(out=mv_[:, :], in_=st[:, :, :].rearrange("p a b -> p (a b)")))
            rsqrt_(bmv, mv_[:, 1:2], LN_EPS, brs, rs_[:, 0:1])
            stt([bmv, brs], [brs], rs_[:, 1:2], mv_[:, 0:1], -1.0, rs_[:, 0:1], ALU.mult, ALU.mult)
            act([Rbuf, brs], out_bufs, out_ap, Rt[:, t_, :], AF.Identity, bias=rs_[:, 1:2], scale=rs_[:, 0:1])

        def transpose_affine(Rt, Rbuf, t_, dst, dstbuf, ai, groups):
            for _ in transpose_affine_g(Rt, Rbuf, t_, dst, dstbuf, ai, groups):
                pass

        def transpose_affine_g(Rt, Rbuf, t_, dst, dstbuf, ai, groups):
            for hb in range(2):
                pt, pb = bank()
                for k4 in range(4):
                    kc = hb * 4 + k4
                    tr([Rbuf, b_ident], [pb], pt[:, k4 * 128:(k4 + 1) * 128], Rt[:, t_, kc * 128:(kc + 1) * 128])
                for k4 in range(4):
                    kc = hb * 4 + k4
                    for (c0, c1, bi) in groups:
                        o = dst[:, kc, t_ * 128 + c0:t_ * 128 + c1]
                        i_ = pt[:, k4 * 128 + c0:k4 * 128 + c1]
                        if hb == 0:
                            act([pb, b_AB], [dstbuf], o, i_, AF.Identity, bias=AB[:, ai + 1, kc, bi:bi + 1],
                                scale=AB[:, ai, kc, bi:bi + 1])
                        else:
                            ts([pb, b_AB], [dstbuf], o, i_, AB[:, ai, kc, bi:bi + 1], AB[:, ai + 1, kc, bi:bi + 1],
                               ALU.mult, ALU.add)
                yield

        def gate_glr(ntl, H, Hbuf):
            pt, pb = bank()
            for kc in range(8):
                mm([b_wg, Hbuf], [pb], pt[0:16, 0:ntl * 128], wg[:, kc, :], H[:, kc, 0:ntl * 128],
                   start=(kc == 0), stop=(kc == 7))
            act([pb], [b_glrT], glrT[:, 0:ntl * 128], pt[0:16, 0:ntl * 128], AF.Copy)

        def gate_sp(t_):
            pt, pb = bank()
            mm([b_glrT, b_wg2], [pb], pt[:, 0:256], glrT[:, t_ * 128:(t_ + 1) * 128], wg2[:, :])
            tt([pb, b_bg2], [b_spt], spt[:, t_, :], pt[:, 0:256], bg2[:, :], ALU.add)
            act([b_spt], [b_spt], spt[:, t_, :], spt[:, t_, :], AF.Exp, scale=-1.0)
            act([b_spt], [b_spt], spt[:, t_, :], spt[:, t_, :], AF.Ln, bias=1.0)
            act([b_spt], [b_sph], sph[:, t_, 0, :], spt[:, t_, :], AF.Copy)
            tt([b_spt, b_sph], [b_sph], sph[:, t_, 1, :], spt[:, t_, :], sph[:, t_, 0, :], ALU.subtract)

        def state_update(pt, pb, col0, e_idx, snap_idx):
            for pr in range(2):
                ev = Et[:, pr, e_idx:e_idx + 1]
                ts([b_S, b_Et], [b_Se], Se[:, pr, :], S[:, pr, :], ev, None, ALU.mult)
                stt([pb, b_Et, b_Se], [b_S], S[:, pr, :], pt[:, col0 + pr * 128:col0 + (pr + 1) * 128], ev, Se[:, pr, :],
                    ALU.mult, ALU.add)
            if snap_idx is not None:
                act([b_S], [b_Sb], Sb[:, snap_idx, :, :], S[:, :, :], AF.Copy)

        Hs = [(hT, b_hT), (mixT, b_mixT), (h2T, b_h2T)]
        ENB = [(tA, b_tA), (tB, b_tB)]

        def preA(i):
            Rt, Rbuf = Rb[i % 2]
            H, Hbuf = Hs[i % 3]
            ld("sp", "R%d" % (i % 2), [Rbuf], Rt[:, 0:NT, :],
               xmain[i * NB:(i + 1) * NB, :].rearrange("(t p) d -> p t d", p=128))
            for t_ in range(NT):
                layer_norm_rows(Rt, Rbuf, t_, Rt[:, t_, :], [Rbuf], "b")
                yield
                for _ in transpose_affine_g(Rt, Rbuf, t_, H, Hbuf, 0, [(0, 128, 0)]):
                    yield

        def preB(i):
            H, Hbuf = Hs[i % 3]
            en, enb_ = ENB[i % 2]
            ut, ub = Ubf["pre"]
            gate_glr(NT, H, Hbuf)
            yield
            for t_ in range(NT):
                gate_sp(t_)
                yield
                pt, pb = bank()
                for hl in range(2):
                    mm([ub, b_sph], [pb], pt[:, 0:256], ut[:, :], sph[:, t_, hl, :], start=(hl == 0), stop=(hl == 1))
                act([pb], [enb_], en[:, t_ * 256:(t_ + 1) * 256], pt[:, 0:256], AF.Exp, scale=-1.0)
                pe_, peb = bank()
                for pr in range(2):
                    for hl in range(2):
                        mm([ub, b_sph], [peb], pe_[:, pr:pr + 1], sph[:, t_, hl, pr * 128:(pr + 1) * 128], ut[:, 127:128],
                           start=(hl == 0), stop=(hl == 1))
                sl = (i % 2) * 2 + t_
                act([peb], [b_Et], Et[:, :, sl:sl + 1], pe_[:, 0:2].rearrange("p (a b) -> p a b", b=1), AF.Exp)
                yield

        def preC(i):
            H, Hbuf = Hs[i % 3]
            en, enb_ = ENB[i % 2]
            for t_ in range(NT):
                pk, pkb = bank()
                for kc in range(8):
                    mm([Hbuf, b_big], [pkb], pk[:, 0:256], H[:, kc, t_ * 128:(t_ + 1) * 128], wpk[:, kc, :],
                       start=(kc == 0), stop=(kc == 7))
                tt([pkb, enb_], [b_ktm], ktm[:, t_, :], pk[:, 0:256], en[:, t_ * 256:(t_ + 1) * 256], ALU.mult)
                yield
                pv, pvb = bank()
                for kc in range(8):
                    mm([Hbuf, b_big], [pvb], pv[:, :], H[:, kc, t_ * 128:(t_ + 1) * 128], wpv[:, kc, :],
                       start=(kc == 0), stop=(kc == 7))
                act([pvb], [b_vbf], vbf[:, t_, :], pv[:, :], AF.Copy)
                yield
                pp, ppb = bank()
                for h in range(4):
                    pr, hf = h // 2, h % 2
                    mm([b_ktm, b_vbf], [ppb], pp[hf * 64:(hf + 1) * 64, pr * 128:(pr + 1) * 128],
                       ktm[:, t_, h * 64:(h + 1) * 64], vbf[:, t_, h * 128:(h + 1) * 128], tp=(0, hf * 64))
                sl = (i % 2) * 2 + t_
                state_update(pp, ppb, 0, sl, None)
                tt([b_Dt, b_Et], [b_Dt], Dt[:, :], Dt[:, :], Et[:, :, sl], ALU.mult)
                yield

        def round_robin(gens):
            gens = [g for g in gens if g is not None]
            while gens:
                alive = []
                for g in gens:
                    try:
                        next(g)
                        alive.append(g)
                    except StopIteration:
                        pass
                gens = alive

        class Ctx:
            pass

        def make_ctx(mode, blk, rbi):
            c = Ctx()
            c.sample = (mode == "sample")
            c.blk = blk
            c.ntl = 1 if c.sample else NT
            c.n = c.ntl * 128
            c.st = "s" if c.sample else "p"
            c.cs = 32 if c.sample else 64
            c.cpt = 128 // c.cs
            c.Rt, c.Rbuf = Rb[rbi]
            c.rbi = rbi
            c.src = xmain[2048:2176, :] if c.sample else xmain[blk * NB:(blk + 1) * NB, :]
            c.groups = [(32 * s_, 32 * s_ + 32, 1 + s_) for s_ in range(4)] if c.sample else [(0, 128, 0)]
            return c

        def head1(c):
            ld("sp", "R%d" % c.rbi, [c.Rbuf], c.Rt[:, 0:c.ntl, :], c.src.rearrange("(t p) d -> p t d", p=128))
            for t_ in range(c.ntl):
                layer_norm_rows(c.Rt, c.Rbuf, t_, c.Rt[:, t_, :], [c.Rbuf], "b")
                yield
                for _ in transpose_affine_g(c.Rt, c.Rbuf, t_, hT, b_hT, 0, c.groups):
                    yield
            gate_glr(c.ntl, hT, b_hT)
            yield
            ut, ub = Ubf[c.st]
            for t_ in range(c.ntl):
                gate_sp(t_)
                yield
                pt, pb = bank()
                for hl in range(2):
                    mm([ub, b_sph], [pb], pt[:, 0:256], ut[:, :], sph[:, t_, hl, :], start=(hl == 0), stop=(hl == 1))
                act([pb], [b_tC], tC[:, t_ * 256:(t_ + 1) * 256], pt[:, 0:256], AF.Exp, scale=-1.0)
                pt2, pb2 = bank()
                for pr in range(2):
                    for hl in range(2):
                        mm([ub, b_sph], [pb2], pt2[:, pr * 128:(pr + 1) * 128], sph[:, t_, hl, pr * 128:(pr + 1) * 128],
                           ut[:, :], start=(hl == 0), stop=(hl == 1))
                v3 = pt2[:, 0:256].rearrange("p (a b) -> p a b", a=2)
                act([pb2], [b_ebT], ebT[:, :, t_ * 128:(t_ + 1) * 128], v3, AF.Exp)
                act([pb2], [b_enbT], enbT[:, :, t_ * 128:(t_ + 1) * 128], v3, AF.Exp, scale=-1.0)
                for cl_ in range(c.cpt):
                    cend = cl_ * c.cs + c.cs - 1
                    cidx = t_ * c.cpt + cl_
                    act([pb2], [b_Et], Et[:, :, cidx:cidx + 1], v3[:, :, cend:cend + 1], AF.Exp)
                yield

        def head2(c):
            n, ntl = c.n, c.ntl
            wt, wb = w_acquire()
            for pr in range(2):
                pt, pb = bank()
                for kc in range(8):
                    mm([wb, b_hT], [pb], pt[:, 0:n], wt[:, kc, pr * 128:(pr + 1) * 128], hT[:, kc, 0:n],
                       start=(kc == 0), stop=(kc == 7))
                for hf in range(2):
                    hs_ = slice(hf * 64, (hf + 1) * 64)
                    stt([pb, b_ebT], [b_qsel], qsel[hs_, 2 * pr + hf, 0:n], pt[hs_, 0:n], 0.125, ebT[hs_, pr, 0:n],
                        ALU.mult, ALU.mult)
                pt, pb = bank()
                for kc in range(8):
                    mm([wb, b_hT], [pb], pt[:, 0:n], wt[:, kc, 256 + pr * 128:256 + (pr + 1) * 128], hT[:, kc, 0:n],
                       start=(kc == 0), stop=(kc == 7))
                tt([pb, b_enbT], [b_ktT], ktT[:, pr, 0:n], pt[:, 0:n], enbT[:, pr, 0:n], ALU.mult)
                yield
            for t_ in range(ntl):
                pt, pb = bank()
                for kc in range(8):
                    mm([wb, b_hT], [pb], pt[:, 0:256], hT[:, kc, t_ * 128:(t_ + 1) * 128], wt[:, kc, 256:512],
                       start=(kc == 0), stop=(kc == 7))
                tt([pb, b_tC], [b_ktm], ktm[:, t_, :], pt[:, 0:256], tC[:, t_ * 256:(t_ + 1) * 256], ALU.mult)
            w_release()
            yield
            wt, wb = w_acquire()
            for t_ in range(ntl):
                pt, pb = bank()
                for kc in range(8):
                    mm([wb, b_hT], [pb], pt[:, :], hT[:, kc, t_ * 128:(t_ + 1) * 128], wt[:, kc, :],
                       start=(kc == 0), stop=(kc == 7))
                act([pb], [b_vbf], vbf[:, t_, :], pt[:, :], AF.Copy)
                for cl_ in range(c.cpt):
                    act([pb, cmask[c.st][1]], [b_vsel], vsel[:, t_ * c.cpt + cl_, :], pt[:, :], AF.Copy,
                        scale=cmask[c.st][0][:, cl_:cl_ + 1])
                yield
            w_release()
            for _ in gla_early(c):
                yield
            wt, wb = w_acquire()
            st, bst, mv_, bmv, rs_, brs = SS["b"]
            for t_ in range(ntl):
                pt, pb = bank()
                for kc in range(8):
                    mm([wb, b_hT], [pb], pt[:, :], hT[:, kc, t_ * 128:(t_ + 1) * 128], wt[:, kc, :],
                       start=(kc == 0), stop=(kc == 7))
                act([pb], [b_tZ], tZ[:, 0:512], pt[:, :], AF.Gelu)
                P.op("dve", [b_tZ], [bst], lambda e: e.bn_stats(out=st[:, 0, :], in_=tZ[:, 0:512]), dur=700.0)
                P.op("dve", [bst], [bmv], lambda e: e.bn_aggr(out=mv_[:, :], in_=st[:, 0, 0:6]))
                rsqrt_(bmv, mv_[:, 1:2], LN_EPS, brs, rs_[:, 0:1])
                stt([bmv, brs], [brs], rs_[:, 1:2], mv_[:, 0:1], -1.0, rs_[:, 0:1], ALU.mult, ALU.mult)
                stt([b_gg, brs, b_gb], [b_tZ], tZ[:, 512:1024], gg[:, :], rs_[:, 1:2], gb[:, :], ALU.mult, ALU.add)
                stt([b_tZ, brs, b_gg], [b_tZ], tZ[:, 0:512], tZ[:, 0:512], rs_[:, 0:1], gg[:, :], ALU.mult, ALU.mult)
                if c.sample:
                    tt([b_tZ], [b_tZ], tZ[:, 512:1024], tZ[:, 0:512], tZ[:, 512:1024], ALU.add)
                    act([b_tZ], [b_zvn], zvn[:, t_, :], tZ[:, 512:1024], AF.Copy)
                    P.dma("sp", "st_zv", [b_tZ], [], lambda e: e.dma_start(out=zv_out, in_=tZ[:, 512:1024]))
                else:
                    tt([b_tZ], [b_zvn], zvn[:, t_, :], tZ[:, 0:512], tZ[:, 512:1024], ALU.add)
                yield
            w_release()

            for (func, dst, dbuf) in ((AF.Silu, srT, b_srT), (AF.Gelu, guT, b_guT)):
                wt, wb = w_acquire()
                for cc in range(4):
                    pt, pb = bank()
                    for kc in range(8):
                        mm([wb, b_hT], [pb], pt[:, 0:n], wt[:, kc, cc * 128:(cc + 1) * 128], hT[:, kc, 0:n],
                           start=(kc == 0), stop=(kc == 7))
                    act([pb], [dbuf], dst[:, cc, 0:n], pt[:, 0:n], func)
                    if cc % 2 == 1:
                        yield
                w_release()
        def gla_early(c):
            ntl, cs, cpt = c.ntl, c.cs, c.cpt
            nchunks = ntl * cpt
            if c.sample:
                act([b_S0s], [b_Sb], Sb[:, 0:4, :, :], S0s[:, :, :, :], AF.Copy)
            else:
                act([b_S], [b_Sb], Sb[:, 0, :, :], S[:, :, :], AF.Copy)
            Mk, Mkb = (masks, b_masks) if c.sample else (maskp, b_maskp)
            for t_ in range(ntl):
                tk = slice(t_ * 128, (t_ + 1) * 128)
                attm, b_attm = attm2[t_ % 2]
                pbanks = []
                for cl_ in range(cpt):
                    if cl_ % 2 == 0:
                        pbanks.append(bank())
                    pp, ppb = pbanks[-1]
                    for h in range(4):
                        pr, hf = h // 2, h % 2
                        c0 = (cl_ % 2) * 256 + pr * 128
                        mm([b_ktm, b_vsel], [ppb], pp[hf * 64:(hf + 1) * 64, c0:c0 + 128],
                           ktm[:, t_, h * 64:(h + 1) * 64], vsel[:, t_ * cpt + cl_, h * 128:(h + 1) * 128],
                           tp=(0, hf * 64))
                pa, pab = bank()
                for h in range(4):
                    pr = h // 2
                    mm([b_ktT, b_qsel], [pab], pa[:, h * 128:(h + 1) * 128], ktT[:, pr, tk], qsel[:, h, tk])
                for h in range(4):
                    tt([pab, Mkb], [b_attm], attm[:, h, :], pa[:, h * 128:(h + 1) * 128], Mk[:, :], ALU.mult)
                for cl_ in range(cpt):
                    cidx = t_ * cpt + cl_
                    pp, ppb = pbanks[cl_ // 2]
                    if c.sample:
                        for pr in range(2):
                            ev = Et[:, pr, cidx:cidx + 1]
                            ts([b_S0s, b_Et], [b_Se], Se[:, pr, :], S0s[:, cidx, pr, :], ev, None, ALU.mult)
                            c0 = (cl_ % 2) * 256 + pr * 128
                            stt([ppb, b_Et, b_Se], [b_Sos], Sos[:, cidx, pr, :], pp[:, c0:c0 + 128], ev, Se[:, pr, :],
                                ALU.mult, ALU.add)
                    else:
                        snap = cidx + 1 if cidx + 1 < nchunks else None
                        state_update(pp, ppb, (cl_ % 2) * 256, cidx, snap)
                yield
            if c.sample:
                P.dma("sp", "st_ss", [b_Sos], [], lambda e: e.dma_start(out=ss_out.rearrange("s p a e -> p s a e"), in_=Sos[:, :, :, :]))

        def gla_late(c):
            ntl, cs, cpt, st_ = c.ntl, c.cs, c.cpt, c.st
            for t_ in range(ntl):
                tk = slice(t_ * 128, (t_ + 1) * 128)
                attm, b_attm = attm2[t_ % 2]
                osq, b_osq = osq2[t_ % 2]
                tR, b_tR = (tC, b_tC) if t_ % 2 == 0 else (tA, b_tA)
                po, pob = bank()
                for h in range(4):
                    pr = h // 2
                    mm([b_vbf, b_attm], [pob], po[:, h * 128:(h + 1) * 128], vbf[:, t_, h * 128:(h + 1) * 128],
                       attm[:, h, :], start=True, stop=False)
                    for cl_ in range(cpt):
                        cidx = t_ * cpt + cl_
                        mm([b_Sb, b_qsel], [pob], po[:, h * 128 + cl_ * cs:h * 128 + (cl_ + 1) * cs],
                           Sb[:, cidx, pr, :], qsel[:, h, t_ * 128 + cl_ * cs:t_ * 128 + (cl_ + 1) * cs],
                           start=False, stop=(cl_ == cpt - 1))
                act([pob], [b_osq], osq[:, :], po[:, :], AF.Square)
                ps, psb = bank()
                mm([b_ones, b_osq], [psb], ps[:, :], ones_bf[:, :], osq[:, :])
                rsqrt_(psb, ps[:, :], 128.0 * RMS_EPS, b_tR, tR[:, 0:512])
                tt([pob, b_tR], [b_tR], tR[:, 512:1024], po[:, :], tR[:, 0:512], ALU.mult)
                stt([b_tR, b_gn, b_srT], [b_mixT], mixT[:, 0:4, tk], tR[:, 512:1024].rearrange("p (h i) -> p h i", h=4),
                    gn128[:, 0:1], srT[:, :, tk], ALU.mult, ALU.mult)
                pg, pgb = bank()
                wst, wsb = wsT[st_]
                bt_, bbuf = bsp[st_]
                for g in range(4):
                    o = pg[:, g * 128:(g + 1) * 128]
                    mm([b_zvn, wsb], [pgb], o, zvn[:, t_, g * 128:(g + 1) * 128], wst[:, g, :], start=True, stop=False)
                    mm([b_ones0, bbuf], [pgb], o, ones0[:, :], bt_[:, 0, g * 128:(g + 1) * 128], start=False, stop=False)
                    mm([b_ones0, bbuf], [pgb], o, ones0[:, :], bt_[:, 1, g * 128:(g + 1) * 128], start=False, stop=True)
                tt([pgb, b_guT], [b_mixT], mixT[:, 4:8, tk], pg[:, :].rearrange("p (g i) -> p g i", g=4), guT[:, :, tk], ALU.mult)

        def pull(gen, k):
            if gen is None:
                return
            for _ in range(k):
                try:
                    next(gen)
                except StopIteration:
                    return

        def drain(gen):
            if gen is None:
                return
            for _ in gen:
                pass

        def sub_matmuls(c, lhs, lhsbuf, nk, ntiles_per_half, gen, tile_major):
            ybanks = {}

            def group(t_, half, wts):
                pt, pb = bank(pin=True)
                ybanks[(t_, half)] = (pt, pb)
                k = 0
                for (wt, wb) in wts:
                    for kk in range(8):
                        mm([wb, lhsbuf], [pb], pt[:, :], lhs[:, k, t_ * 128:(t_ + 1) * 128], wt[:, kk, :],
                           start=(k == 0), stop=(k == nk - 1))
                        k += 1
            if tile_major:
                w2 = [[w_acquire() for _ in range(ntiles_per_half)] for _ in range(2)]
                for t_ in range(c.ntl):
                    for half in range(2):
                        group(t_, half, w2[half])
                w_release(2 * ntiles_per_half)
            else:
                for half in range(2):
                    wts = [w_acquire() for _ in range(ntiles_per_half)]
                    for t_ in range(c.ntl):
                        group(t_, half, wts)
                        pull(gen, 2)
                    w_release(ntiles_per_half)
            return ybanks

        def residual_ln(c, ybanks, t_, Gt, Gb, Pt, Pb_, Ct, Cb, final):
            Rt, Rbuf = c.Rt, c.Rbuf
            tt([Rbuf, Pb_], [b_tA], tA[:, :], Rt[:, t_, :], Pt[:, :], ALU.mult)
            tt([b_tA, Cb], [b_tA], tA[:, :], tA[:, :], Ct[:, :], ALU.add)
            for half in range(2):
                pt, pb = ybanks[(t_, half)]
                hs = slice(half * 512, (half + 1) * 512)
                tt([pb, Gb], [b_tB], tB[:, hs], pt[:, :], Gt[:, hs], ALU.mult)
                unpin(pb)
            tt([b_tA, b_tB], [Rbuf], Rt[:, t_, :], tA[:, :], tB[:, :], ALU.add)
            if not final:
                layer_norm_rows(Rt, Rbuf, t_, Rt[:, t_, :], [Rbuf], "a")
                transpose_affine(Rt, Rbuf, t_, h2T, b_h2T, 2, c.groups)
            else:
                layer_norm_rows(Rt, Rbuf, t_, tA[:, :], [b_tA], "a")
                tt([b_tA, modP["g2"][1]], [b_tB], tB[:, :], tA[:, :], modP["g2"][0][:, :], ALU.mult)
                tt([b_tB, modP["b2"][1]], [b_yst], yst[:, :], tB[:, :], modP["b2"][0][:, :], ALU.add)
                r0 = 2048 + t_ * 128 if c.sample else c.blk * NB + t_ * 128
                P.dma("sp", "yst", [b_yst], [], lambda e, r0=r0: e.dma_start(out=y_out[r0:r0 + 128, :], in_=yst[:, :]), nbytes=524288.0)

        def mid(c, g1):
            st_ = c.st
            gla_late(c)
            G1t, G1b = modS[st_]["G1"]; C1t, C1b = modS[st_]["C1"]
            P1t, P1b = modP["P1"]
            yb = sub_matmuls(c, mixT, b_mixT, 8, 1, None, True)
            for t_ in range(c.ntl):
                residual_ln(c, yb, t_, G1t, G1b, P1t, P1b, C1t, C1b, False)
            n = c.n
            for j in range(8):
                wt, wb = w_acquire()
                for cc in range(4):
                    ch = j * 4 + cc
                    pt, pb = bank()
                    for kc in range(8):
                        mm([wb, b_h2T], [pb], pt[:, 0:n], wt[:, kc, cc * 128:(cc + 1) * 128], h2T[:, kc, 0:n],
                           start=(kc == 0), stop=(kc == 7))
                    act([pb, b_vecT], [b_tZ], tZ[:, 0:n], pt[:, 0:n], AF.Relu, bias=vecT[:, V_BUP + ch:V_BUP + ch + 1])
                    tt([b_tZ], [b_big], big[:, ch, 0:n], tZ[:, 0:n], tZ[:, 0:n], ALU.mult)
                w_release()
                pull(g1, 1)

        def tail(c, nxt, g1):
            st_ = c.st
            G2t, G2b = modS[st_]["G2"]; C2t, C2b = modS[st_]["C2"]
            P2t, P2b = modP["P2"]
            yb = sub_matmuls(c, big, b_big, 32, 4, g1, False)
            drain(g1)
            g2 = head2(nxt) if nxt is not None else None
            for t_ in range(c.ntl):
                residual_ln(c, yb, t_, G2t, G2b, P2t, P2b, C2t, C2b, True)
                pull(g2, 7)
            drain(g2)

        P.op("dve", [], [b_S], lambda e: e.memset(S[:], 0.0))
        P.op("dve", [], [b_Dt], lambda e: e.memset(Dt[:], 1.0))
        for step in range(NPRE + 2):
            round_robin([preA(step) if step < NPRE else None,
                         preB(step - 1) if 0 <= step - 1 < NPRE else None,
                         preC(step - 2) if 0 <= step - 2 < NPRE else None])
        cc_in = nc.dram_tensor("cc_in", [128, 258], F32, kind="Internal").ap()
        cc_out = nc.dram_tensor("cc_out", [512, 258], F32, kind="Internal", addr_space="Local").ap()
        b_ccin, b_ccout = Buf("cc_in"), Buf("cc_out")
        P.dma("sp", "ccw", [b_S], [b_ccin], lambda e: e.dma_start(out=cc_in[:, 0:256], in_=S[:, :, :].rearrange("p a e -> p (a e)")))
        P.dma("sp", "ccw", [b_Dt], [b_ccin], lambda e: e.dma_start(out=cc_in[:, 256:258], in_=Dt[:, :]))
        if USE_CC:
            P.op("pool", [b_ccin], [b_ccout], lambda e: e.collective_compute(
                "AllGather", ALU.bypass, replica_groups=[[0, 1, 2, 3], [4, 5, 6, 7]], ins=[cc_in], outs=[cc_out]),
                dur=30000.0)
        else:
            for r_ in range(4):
                P.dma("sp", "ccw", [b_ccin], [b_ccout], lambda e, r_=r_: e.dma_start(out=cc_out[r_ * 128:(r_ + 1) * 128, :], in_=cc_in))
        P.op("dve", [b_ccin], [b_S], lambda e: e.memset(S[:], 0.0))
        for p_ in range(4):
            mcol = qmask[:, p_:p_ + 1]
            P.dma("sp", "ccr", [b_ccout], [b_Gt], lambda e, p_=p_: e.dma_start(out=Gt[:, :], in_=cc_out[p_ * 128:(p_ + 1) * 128, :]))
            ts([b_Gt, b_qmask], [b_Dp], Dp[:, :], Gt[:, 256:258], -1.0, mcol, ALU.add, ALU.mult)
            ts([b_Dp], [b_Dp], Dp[:, :], Dp[:, :], 1.0, None, ALU.add)
            for pr in range(2):
                ts([b_S, b_Dp], [b_Se], Se[:, pr, :], S[:, pr, :], Dp[:, pr:pr + 1], None, ALU.mult)
                stt([b_Gt, b_qmask, b_Se], [b_S], S[:, pr, :], Gt[:, pr * 128:(pr + 1) * 128], mcol, Se[:, pr, :],
                    ALU.mult, ALU.add)
        ctxs = []
        if DO_SAMPLE:
            ctxs.append(make_ctx("sample", 0, len(ctxs) % 2))
        for blk in range(NPB):
            ctxs.append(make_ctx("main", blk, len(ctxs) % 2))
        if ctxs:
            drain(head1(ctxs[0]))
            drain(head2(ctxs[0]))
        for idx, c in enumerate(ctxs):
            nxt = ctxs[idx + 1] if idx + 1 < len(ctxs) else None
            g1 = head1(nxt) if nxt is not None else None
            mid(c, g1)
            tail(c, nxt, g1)
        P.dma("sp", "st_sp", [b_S], [], lambda e: e.dma_start(out=sp_out, in_=S[:, :, :]))

        P.fixed = set(FIXED)
        P.codegen()

        with nc.Block() as block:
            @block.sync
            def _(e):
                P.replay("sp", e)

            @block.gpsimd
            def _(e):
                P.replay("pool", e)

            @block.tensor
            def _(e):
                P.replay("pe", e)

            @block.scalar
            def _(e):
                P.replay("act", e)

            @block.vector
            def _(e):
                P.replay("dve", e)
    return nc


_NC = None


def _consts():
    j = np.arange(128)[:, None]
    i = np.arange(128)[None, :]
    c = {}
    c["ident"] = np.eye(128, dtype=np.float32)
    c["maskp"] = ((j <= i) & (j // 64 == i // 64)).astype(np.float32)
    c["masks"] = ((j <= i) & (j // 32 == i // 32)).astype(np.float32)
    c["U_p"] = (c["maskp"] * np.float32(-1.0 / 16.0)).astype(np.float32)
    c["U_s"] = (c["masks"] * np.float32(-1.0 / 16.0)).astype(np.float32)
    c["U_pre"] = ((j <= i).astype(np.float32) * np.float32(-1.0 / 16.0)).astype(np.float32)
    c["trilT"] = (j <= i).astype(np.float32)
    p = np.arange(128)[:, None]
    cc = np.arange(4)[None, :]
    c["cmask_p"] = (p // 64 == cc).astype(np.float32)
    c["cmask_s"] = (p // 32 == cc).astype(np.float32)
    return c


def prepare(x_prompt, x_sample, state_gla, c_prompt, c_sample, ln_in_g, ln_in_b, w_ada, b_ada, w_in,
            w_gate2, b_gate2, gla_norm_g, gmlp_ln_g, gmlp_ln_b, w_spatial, b_spatial, w_o, b_o,
            ln1_g, ln1_b, w_up, b_up, w_down, b_down, ln2_g, ln2_b):
    f = lambda a: np.ascontiguousarray(np.asarray(a, dtype=np.float32))
    x_prompt, x_sample, state_gla = f(x_prompt), f(x_sample), f(state_gla)
    c_prompt, c_sample = f(c_prompt), f(c_sample)
    b_ada0 = f(b_ada)[0]
    consts = _consts()

    def chunksT(v):
        return f(v).reshape(-1, 128).T

    vecT = np.concatenate([
        chunksT(ln_in_g), chunksT(ln_in_b), chunksT(f(ln1_g)[0]), chunksT(f(ln1_b)[0]),
        chunksT(b_ada0[0:1024]), chunksT(b_ada0[1024:2048]), chunksT(b_ada0[3072:4096]), chunksT(b_ada0[4096:5120]),
        chunksT(f(b_up)[0]), chunksT(f(gla_norm_g)[0])], axis=1)
    bcast = lambda v: np.ascontiguousarray(np.broadcast_to(f(v).reshape(1, -1), (128, f(v).size)))
    shared = {
        "w_ada": f(w_ada)[0], "w_in": f(w_in)[0], "w_o": f(w_o)[0], "w_up": f(w_up)[0], "w_down": f(w_down)[0],
        "vecT": np.ascontiguousarray(vecT),
        "bc_g_in": bcast(ln_in_g), "bc_b_in": bcast(ln_in_b), "bc_b_o": bcast(f(b_o)[0]),
        "bc_g1": bcast(f(ln1_g)[0]), "bc_b1": bcast(f(ln1_b)[0]), "bc_b_down": bcast(f(b_down)[0]),
        "bc_g2": bcast(f(ln2_g)[0]), "bc_b2": bcast(f(ln2_b)[0]),
        "bc_ba_g1": bcast(b_ada0[2048:3072]), "bc_ba_g2": bcast(b_ada0[5120:6144]),
        "bc_bg2": bcast(f(b_gate2)[0]), "bc_gg": bcast(f(gmlp_ln_g)[0]), "bc_gb": bcast(f(gmlp_ln_b)[0]),
        "w_gate2": f(w_gate2)[0],
        "wsp_p": f(w_spatial)[0],
        "bsp_p": f(b_spatial)[0].reshape(1, 512),
        "bsp_s": np.ascontiguousarray(np.tile(f(b_spatial)[0][:, :32], (1, 4)).reshape(1, 512)),
    }
    wsp_s = np.zeros((4, 128, 128), np.float32)
    for s_ in range(4):
        wsp_s[:, 32 * s_:32 * s_ + 32, 32 * s_:32 * s_ + 32] = f(w_spatial)[0][:, :32, :32]
    shared["wsp_s"] = wsp_s
    shared.update(consts)

    in_maps = []
    for ci in range(8):
        b, q = ci // 4, ci % 4
        m = dict(shared)
        xs = x_sample[4 * ci:4 * ci + 4].reshape(128, D)
        m["xmain"] = np.ascontiguousarray(np.concatenate([x_prompt[b, 2048 * q:2048 * (q + 1)], xs], axis=0))
        m["qmask"] = np.ascontiguousarray(np.broadcast_to((np.arange(4) < q).astype(np.float32)[None, :], (128, 4)))
        ctp = c_prompt[b].reshape(8, 128).T
        m["cT_p"] = np.ascontiguousarray(np.broadcast_to(ctp[:, :, None], (128, 8, 128)))
        cs4 = c_sample[4 * ci:4 * ci + 4]
        cts = cs4.reshape(4, 8, 128).transpose(2, 1, 0)
        m["cT_s"] = np.ascontiguousarray(np.repeat(cts, 32, axis=2))
        st = state_gla[0, 4 * ci:4 * ci + 4]
        st = st.reshape(4, 2, 2, 64, 128).transpose(0, 2, 3, 1, 4)
        m["st0"] = np.ascontiguousarray(st.reshape(4, 128, 2, 128))
        in_maps.append(m)

    return in_maps


def kernel(**inputs):
    global _NC
    if _NC is None:
        _NC = build_program()
    in_maps = prepare(**inputs)
    res = run_bass_kernel_spmd(_NC, in_maps, core_ids=list(range(8)))
    return assemble(res.results)


def assemble(R):
    y_prompt = np.zeros((2, 8192, D), np.float32)
    y_sample = np.zeros((32, 32, D), np.float32)
    ns_p = np.zeros((1, 2, 4, 64, 128), np.float32)
    ns_s = np.zeros((1, 32, 4, 64, 128), np.float32)
    nv_s = np.zeros((1, 32, 32, 512), np.float32)

    def unstate(a):
        return a.reshape(2, 64, 2, 128).transpose(2, 0, 1, 3).reshape(4, 64, 128)
    for ci in range(8):
        b, q = ci // 4, ci % 4
        y = R[ci]["y"]
        y_prompt[b, 2048 * q:2048 * (q + 1)] = y[0:2048]
        y_sample[4 * ci:4 * ci + 4] = y[2048:2176].reshape(4, 32, D)
        if q == 3:
            ns_p[0, b] = unstate(R[ci]["s_out_p"])
        so = R[ci]["s_out_s"]
        for s_ in range(4):
            ns_s[0, 4 * ci + s_] = unstate(so[s_])
        nv_s[0, 4 * ci:4 * ci + 4] = R[ci]["zv_out"].reshape(4, 32, 512)
    return (y_prompt, y_sample, ns_p, ns_s, nv_s)
```

```python
import numpy as np
import concourse.bass as bass
import concourse.mybir as mybir
from concourse.bass_utils import run_bass_kernel_spmd

F32 = mybir.dt.float32
BF16 = mybir.dt.bfloat16
AF = mybir.ActivationFunctionType
ALU = mybir.AluOpType

D = 1024
NIN = 2576
DFF = 4096
NT = 2
NB = NT * 128
NPB = 2048 // NB
NPRE = NPB
ALPHA = 2.0 ** 0.25
LN_EPS = 1e-5
RMS_EPS = 1e-6
RING = 5
DO_SAMPLE = True
FILL = True
FILL_MIN_GAP = 2000.0
FILL_FRAC = 0.6
XLAT = 250.0
USE_CC = True
FIXED = ()
STOP = 99
V_GIN, V_BIN, V_G1, V_B1, V_SH1, V_SC1, V_SH2, V_SC2, V_BUP, V_GN = 0, 8, 16, 24, 32, 40, 48, 56, 64, 96
NV = 97


class Buf:
    def __init__(self, name, multi=False, excl=False):
        self.name = name
        self.excl = excl
        self.multi = multi
        self.full = None
        self.partials = []
        self.readers = []
        self.prev_readers = []


class Op:
    __slots__ = ("idx", "ek", "fn", "deps", "is_dma", "semname", "dur", "issue", "t_end", "pos", "nsucc", "users", "nbytes")

    def __init__(self, idx, ek, fn, deps, is_dma, semname, dur, issue):
        self.idx, self.ek, self.fn, self.deps = idx, ek, fn, deps
        self.is_dma, self.semname, self.dur, self.issue = is_dma, semname, dur, issue
        self.t_end = None
        self.pos = None
        self.users = []
        self.nbytes = 0.0


class Eng:
    def __init__(self, key, sem):
        self.key = key
        self.sem = sem
        self.cmds = []


class Prog:
    def __init__(self, nc, sems):
        self.nc = nc
        self.eng = {k: Eng(k, sems[k]) for k in ("pe", "act", "dve", "pool", "sp")}
        self.dsem = {}
        self.ops = []

    def _deps(self, reads, writes):
        deps = set()
        for b in reads:
            if b.full is not None:
                deps.add(b.full)
            deps.update(b.partials)
            if b.excl:
                deps.update(b.readers)
        for b in writes:
            if b.full is not None:
                deps.add(b.full)
            deps.update(b.readers)
            if b.multi:
                deps.update(b.prev_readers)
            else:
                deps.update(b.partials)
        return deps

    def _record(self, op, reads, writes):
        for b in reads:
            b.readers.append(op)
        for b in writes:
            if b.multi:
                if b.readers:
                    b.partials = [op]
                    b.prev_readers = b.readers
                    b.readers = []
                else:
                    b.partials.append(op)
            else:
                b.full = op
                b.partials = []
                b.readers = []
                b.prev_readers = []
        self.ops.append(op)

    def op(self, ek, reads, writes, fn, dur=150.0):
        op = Op(len(self.ops), ek, fn, self._deps(reads, writes), False, None, dur, dur)
        self._record(op, reads, writes)
        return op

    def dma(self, qk, semname, reads, writes, fn, dur=2000.0, nbytes=65536.0):
        issue = 1000.0 if qk == "pool" else 80.0
        op = Op(len(self.ops), qk, fn, self._deps(reads, writes), True, semname, dur, issue)
        op.nbytes = nbytes
        self._record(op, reads, writes)
        return op

    def schedule(self):
        ops = self.ops
        for o in ops:
            o.nsucc = len(o.deps)
            for d in o.deps:
                d.users.append(o)
        fixed = getattr(self, "fixed", set())
        nxt_fixed = {k: 0 for k in self.eng}
        per_eng = {k: [o for o in ops if o.ek == k] for k in self.eng}
        ready = {k: [] for k in self.eng}
        free = {k: 0.0 for k in self.eng}
        order = {k: [] for k in self.eng}
        rt = {}
        for o in ops:
            if o.nsucc == 0:
                ready[o.ek].append(o)
                rt[o] = 0.0
        n_done = 0
        dma_free = 0.0
        while n_done < len(ops):
            best = None
            for k, lst in ready.items():
                if not lst:
                    continue
                f = free[k]
                cand = None
                if k in fixed:
                    lst = [o for o in lst if o is per_eng[k][nxt_fixed[k]]]
                    if not lst:
                        continue
                for o in lst:
                    st = max(f, rt[o])
                    key = (st, o.idx) if rt[o] > f else (f, o.idx)
                    if cand is None or key < cand[0]:
                        cand = (key, o, st)
                if best is None or cand[0] < best[0]:
                    best = cand
            key, o, st = best
            k = o.ek
            filler = getattr(self, "filler", None)
            if filler is not None and k == "pe" and st > filler["t_min"]:
                gap = st - free["pe"]
                if gap > filler["min_gap"]:
                    nf = int(filler["frac"] * gap / filler["dur"])
                    t_ = free["pe"]
                    for _ in range(nf):
                        fo = Op(-1, "pe", filler["fn"], set(filler["deps"]), False, None, filler["dur"], filler["dur"])
                        fo.t_end = t_ + filler["dur"]
                        t_ = fo.t_end
                        order["pe"].append(fo)
                        self.n_fill = getattr(self, "n_fill", 0) + 1
                    free["pe"] = t_
                    st = max(st, t_)
            ready[k].remove(o)
            nxt_fixed[k] += 1
            free[k] = st + o.issue
            if o.is_dma:
                t0 = max(st + o.issue, dma_free)
                dma_free = t0 + o.nbytes / 300.0
                o.t_end = dma_free + 2000.0
            else:
                o.t_end = st + o.dur
            order[k].append(o)
            n_done += 1
            for u in o.users:
                u.nsucc -= 1
                r = max(rt.get(u, 0.0), o.t_end + (XLAT if u.ek != o.ek else 0.0))
                rt[u] = r
                if u.nsucc == 0:
                    ready[u.ek].append(u)
        self.makespan = max(o.t_end for o in ops)
        return order

    def codegen(self):
        order = self.schedule()
        cnt = {k: 0 for k in self.eng}
        dcnt = {n: 0 for n in self.dsem}
        for k, lst in order.items():
            for o in lst:
                if o.is_dma:
                    dcnt[o.semname] += 1
                    o.pos = (self.dsem[o.semname], 16 * dcnt[o.semname], "dma:" + o.semname)
                else:
                    cnt[k] += 1
                    o.pos = (self.eng[k].sem, cnt[k], k)
        for k, lst in order.items():
            E = self.eng[k]
            waited = {}
            for o in lst:
                need = {}
                for d in o.deps:
                    sem, val, key = d.pos
                    if key == k and k == "pe":
                        continue
                    if key not in need or need[key][1] < val:
                        need[key] = (sem, val)
                for key, (sem, val) in need.items():
                    if waited.get(key, 0) < val:
                        E.cmds.append(("wait", sem, val))
                        waited[key] = val
                if o.is_dma:
                    E.cmds.append(("ins", o.fn, self.dsem[o.semname], 16))
                else:
                    E.cmds.append(("ins", o.fn, E.sem, 1))
        E = self.eng["sp"]
        for name, n in dcnt.items():
            if n > 0:
                E.cmds.append(("wait", self.dsem[name], 16 * n))

    def replay(self, ek, handle):
        for c in self.eng[ek].cmds:
            if c[0] == "wait":
                handle.wait_ge(c[1], c[2])
            else:
                ins = c[1](handle)
                ins.then_inc(c[2], c[3])


def build_program():
    nc = bass.Bass("TRN2", target_bir_lowering=False)

    def din(name, shape):
        return nc.dram_tensor(name, list(shape), F32, kind="ExternalInput").ap()

    def dout(name, shape):
        return nc.dram_tensor(name, list(shape), F32, kind="ExternalOutput").ap()

    xmain = din("xmain", [2176, D])
    qmask_d = din("qmask", [128, 4])
    cT_p = din("cT_p", [128, 8, 128])
    cT_s = din("cT_s", [128, 8, 128])
    st0 = din("st0", [4, 128, 2, 128])
    w_ada = din("w_ada", [D, 6 * D])
    w_in = din("w_in", [D, NIN])
    w_o = din("w_o", [D, D])
    w_up = din("w_up", [D, DFF])
    w_down = din("w_down", [DFF, D])
    vecT_d = din("vecT", [128, NV])
    bc_names = ["g_in", "b_in", "b_o", "g1", "b1", "b_down", "g2", "b2", "ba_g1", "ba_g2"]
    bc_d = {n: din("bc_" + n, [128, D]) for n in bc_names}
    bg2_d = din("bc_bg2", [128, 256])
    gg_d = din("bc_gg", [128, 512])
    gb_d = din("bc_gb", [128, 512])
    wg2_d = din("w_gate2", [16, 256])
    wsp_p_d = din("wsp_p", [4, 128, 128])
    wsp_s_d = din("wsp_s", [4, 128, 128])
    bsp_p_d = din("bsp_p", [1, 512])
    bsp_s_d = din("bsp_s", [1, 512])
    ident_d = din("ident", [128, 128])
    maskp_d = din("maskp", [128, 128])
    masks_d = din("masks", [128, 128])
    up_d = din("U_p", [128, 128])
    us_d = din("U_s", [128, 128])
    upre_d = din("U_pre", [128, 128])
    tril_d = din("trilT", [128, 128])
    cmask_p_d = din("cmask_p", [128, 4])
    cmask_s_d = din("cmask_s", [128, 4])

    y_out = dout("y", [2176, D])
    sp_out = dout("s_out_p", [128, 2, 128])
    ss_out = dout("s_out_s", [4, 128, 2, 128])
    zv_out = dout("zv_out", [128, 512])

    from contextlib import ExitStack
    es = ExitStack()

    def sb(name, shape, dt=F32):
        return es.enter_context(nc.sbuf_tensor("sb_" + name, list(shape), dt))

    def sem(name):
        return es.enter_context(nc.semaphore(name))

    with es:
        sems = {k: sem("s_" + k) for k in ("pe", "act", "dve", "pool", "sp")}
        P = Prog(nc, sems)
        for n in ["const", "R0", "R1", "yst", "misc", "st_zv", "st_ss", "st_sp", "m_a", "m_b", "m_c", "ccw", "ccr"] + ["ring%d" % i for i in range(RING)] + ["ws%d" % i for i in range(RING)]:
            P.dsem[n] = sem("d_" + n)

        MULTI = {"hT", "mixT", "h2T", "big", "srT", "guT", "qsel", "ktT"}

        def T(name, shape, dt=F32):
            t = sb(name, shape, dt)
            return t, Buf(name, multi=(name in MULTI))

        ident, b_ident = T("ident", [128, 128])
        maskp, b_maskp = T("maskp", [128, 128])
        masks, b_masks = T("masks", [128, 128])
        trilT, b_tril = T("trilT", [128, 128])
        vecT, b_vecT = T("vecT", [128, NV])
        qmask, b_qmask = T("qmask", [128, 4])
        Dt, b_Dt = T("Dt", [128, 2])
        Dp, b_Dp = T("Dp", [128, 2])
        Gt, b_Gt = T("Gt", [128, 258])
        ones_bf, b_ones = T("ones_bf", [128, 128], BF16)
        gn128, b_gn = T("gn128", [128, 1])
        bg2, b_bg2 = T("bg2", [128, 256])
        gg, b_gg = T("gg", [128, 512])
        gb, b_gb = T("gb", [128, 512])
        wg2, b_wg2 = T("wg2", [16, 256], BF16)
        wg, b_wg = T("wg", [128, 8, 16], BF16)
        wsT = {}
        for st_ in ("p", "s"):
            wsT[st_] = T("wsT_" + st_, [128, 4, 128], BF16)
        bsp = {}
        for st_ in ("p", "s"):
            bsp[st_] = T("bsp_" + st_, [128, 2, 512], BF16)
        SC5, b_SC5 = T("SC5", [128, 8, 5], BF16)
        MT, b_MT = T("MT", [128, 4, 8, 5])
        AB, b_AB = T("AB", [128, 4, 8, 5])
        modP = {n: T("mod_" + n, [128, D]) for n in ("P1", "P2", "g2", "b2")}
        modS = {st_: {n: T("mod_%s_%s" % (n, st_), [128, D]) for n in ("G1", "C1", "G2", "C2")}
                for st_ in ("p", "s")}
        ring = [T("ring%d" % i, [128, 8, 512], BF16) for i in range(RING)]
        Rb = [T("R%d" % i, [128, NT, D]) for i in range(2)]
        hT, b_hT = T("hT", [128, 8, NB], BF16)
        SCT = {"p": (hT[:, :, 0:128], b_hT), "s": (hT[:, :, 128:256], b_hT)}
        mixT, b_mixT = T("mixT", [128, 8, NB], BF16)
        h2T, b_h2T = T("h2T", [128, 8, NB], BF16)
        glrT, b_glrT = T("glrT", [16, NB], BF16)
        spt, b_spt = T("sp", [128, NT, 256])
        ebT, b_ebT = T("ebT", [128, 2, NB])
        enbT, b_enbT = T("enbT", [128, 2, NB])
        Et, b_Et = T("Et", [128, 2, 8])
        ktT, b_ktT = T("ktT", [128, 2, NB], BF16)
        ktm, b_ktm = T("ktm", [128, NT, 256], BF16)
        vbf, b_vbf = T("vbf", [128, NT, 512], BF16)
        srT, b_srT = T("srT", [128, 4, NB], BF16)
        guT, b_guT = T("guT", [128, 4, NB], BF16)
        zvn, b_zvn = T("zvn", [128, NT, 512], BF16)
        attm2 = [T("attm%d" % i, [128, 4, 128], BF16) for i in range(2)]
        osq2 = [T("osq%d" % i, [128, 512], BF16) for i in range(2)]
        qsel, b_qsel = T("qsel", [128, 4, NB], BF16)
        vsel, b_vsel = T("vsel", [128, 4, 512], BF16)
        ones0, b_ones0 = T("ones0", [128, 128], BF16)
        cmask = {"p": T("cmask_p", [128, 4]), "s": T("cmask_s", [128, 4])}
        S, b_S = T("S", [128, 2, 128])
        Se, b_Se = T("Se", [128, 2, 128])
        Sb, b_Sb = T("Sb", [128, 4, 2, 128], BF16)
        S0s, b_S0s = T("S0s", [128, 4, 2, 128])
        stat, b_stat = T("stat", [128, 2, 6])
        mv, b_mv = T("mv", [128, 2])
        rs, b_rs = T("rs", [128, 2])
        tA, b_tA = T("tA", [128, D])
        enbtm, b_enbtm = tA[:, 768:1024], b_tA
        tB, b_tB = T("tB", [128, D])
        tC, b_tC = T("tC", [128, D])
        tZ, b_tZ = T("tZ", [128, D])
        Ubf = {}
        for k_, nm in enumerate(("p", "s", "pre")):
            Ubf[nm] = T("Ubf_" + nm, [128, 128], BF16)
        big, b_big = T("big", [128, 32, NB], BF16)
        yst, b_yst = T("yst", [128, D])

        banks = []
        for i in range(8):
            t = es.enter_context(nc.psum_tensor("bank%d" % i, [128, 512], F32))
            banks.append((t, Buf("bank%d" % i, excl=True)))
        bank_ctr = [0]

        pinned = set()
        if FILL:
            pinned.add(7)

        def bank(pin=False):
            for _ in range(8):
                i = bank_ctr[0] % 8
                bank_ctr[0] += 1
                if i not in pinned:
                    if pin:
                        pinned.add(i)
                    return banks[i]
            raise RuntimeError("all PSUM banks pinned")

        def unpin(pbuf):
            for i, (t, b) in enumerate(banks):
                if b is pbuf:
                    pinned.discard(i)


        def fsz(ap):
            n = 1
            for d in ap.shape[1:]:
                n *= int(d)
            return n

        def act(reads, writes, out, in_, func, bias=None, scale=None):
            kw = {}
            if bias is not None:
                kw["bias"] = bias
            if scale is not None:
                kw["scale"] = scale
            P.op("act", reads, writes, lambda e: e.activation(out=out, in_=in_, func=func, **kw),
                 dur=250.0 + fsz(in_) / 1.1)

        def tt(reads, writes, out, in0, in1, op, eng="dve"):
            P.op(eng, reads, writes, lambda e: e.tensor_tensor(out=out, in0=in0, in1=in1, op=op),
                 dur=120.0 + fsz(in0) / 0.9)

        def ts(reads, writes, out, in0, s1, s2, op0, op1=None, eng="dve"):
            d = 120.0 + fsz(in0) / 0.9
            if op1 is None:
                P.op(eng, reads, writes, lambda e: e.tensor_scalar(out=out, in0=in0, scalar1=s1, scalar2=None, op0=op0), dur=d)
            else:
                P.op(eng, reads, writes, lambda e: e.tensor_scalar(out=out, in0=in0, scalar1=s1, scalar2=s2, op0=op0, op1=op1), dur=d)

        def stt(reads, writes, out, in0, scalar, in1, op0, op1, eng="dve"):
            P.op(eng, reads, writes, lambda e: e.scalar_tensor_tensor(out=out, in0=in0, scalar=scalar, in1=in1, op0=op0, op1=op1),
                 dur=120.0 + fsz(in0) / 0.9)

        def cpy(reads, writes, out, in_, eng="dve"):
            P.op(eng, reads, writes, lambda e: e.tensor_copy(out=out, in_=in_), dur=120.0 + fsz(in_) / 0.9)

        def rsqrt_(sbuf_, src, eps, dbuf, dst):
            act([sbuf_], [dbuf], dst, src, AF.Ln, bias=float(eps))
            act([dbuf], [dbuf], dst, dst, AF.Exp, scale=-0.5)

        def mm(reads, writes, out, lhsT, rhs, start=True, stop=True, tp=None):
            kw = {}
            if tp is not None:
                kw["tile_position"] = tp
            n = max(64, fsz(rhs))
            d = n / 2.0 * (4.0 if rhs.dtype == F32 else 1.0)
            P.op("pe", reads, writes, lambda e: e.matmul(out, lhsT, rhs, start=start, stop=stop,
                                                         skip_group_check=True, **kw), dur=d)

        def tr(reads, writes, out, in_):
            P.op("pe", reads, writes, lambda e: e.transpose(out, in_, ident[:, :]), dur=215.0)

        def ld(q, semname, writes, out, in_, reads=()):
            nbytes = (4.0 if in_.dtype == F32 else 2.0) * int(out.shape[0]) * fsz(out)
            return P.dma(q, semname, list(reads), writes, lambda e: e.dma_start(out=out, in_=in_), nbytes=nbytes)

        Rb[0][1].multi = True
        Rb[1][1].multi = True
        cl = [(ident, b_ident, ident_d), (maskp, b_maskp, maskp_d), (masks, b_masks, masks_d),
              (trilT, b_tril, tril_d),
              (vecT, b_vecT, vecT_d), (qmask, b_qmask, qmask_d), (bg2, b_bg2, bg2_d), (gg, b_gg, gg_d),
              (gb, b_gb, gb_d), (cmask["p"][0], cmask["p"][1], cmask_p_d), (cmask["s"][0], cmask["s"][1], cmask_s_d)]
        bufs_c = []
        for t, b, d in cl:
            ld("sp", "const", [b], t[:], d)
            bufs_c.append(b)
        for n, src in (("P1", "g_in"), ("P2", "g1"), ("g2", "g2"), ("b2", "b2")):
            t, b = modP[n]
            ld("sp", "const", [b], t[:], bc_d[src])
            bufs_c.append(b)
        for st_ in ("p", "s"):
            t, b = modS[st_]["C1"]
            ld("sp", "const", [b], t[:], bc_d["b_in"])
            bufs_c.append(b)
            t, b = modS[st_]["C2"]
            ld("sp", "const", [b], t[:], bc_d["b1"])
            bufs_c.append(b)
        ld("sp", "const", [b_tA], tA[:], bc_d["ba_g1"]); bufs_c.append(b_tA)
        ld("sp", "const", [b_tB], tB[:], bc_d["ba_g2"]); bufs_c.append(b_tB)
        ld("sp", "const", [Rb[0][1]], Rb[0][0][:, 0, :].rearrange("p (k m) -> p k m", k=8), cT_p); bufs_c.append(Rb[0][1])
        ld("sp", "const", [Rb[1][1]], Rb[1][0][:, 0, :].rearrange("p (k m) -> p k m", k=8), cT_s); bufs_c.append(Rb[1][1])
        ld("sp", "const", [b_S0s], S0s[:], st0.rearrange("s p a e -> p s a e")); bufs_c.append(b_S0s)
        ld("sp", "const", [b_yst], yst[:], bc_d["b_o"]); bufs_c.append(b_yst)
        ld("sp", "const", [b_tZ], tZ[:], bc_d["b_down"]); bufs_c.append(b_tZ)
        for ri, (wsrc, bsrc) in enumerate(((wsp_p_d, bsp_p_d), (wsp_s_d, bsp_s_d))):
            ld("sp", "const", [Rb[ri][1]], Rb[ri][0][:, 1, 0:512].rearrange("p (g j) -> p g j", g=4), wsrc.rearrange("g i j -> i g j"))
            ld("sp", "const", [Rb[ri][1]], Rb[ri][0][0:1, 1, 512:1024], bsrc)
        for k_, ud in enumerate((up_d, us_d, upre_d)):
            ld("sp", "const", [b_tC], tC[:, k_ * 128:(k_ + 1) * 128], ud)
        bufs_c.append(b_tC)
        const_ops = [o for o in P.ops if o.is_dma and o.semname == "const"]
        Rb[0][1].multi = False
        Rb[1][1].multi = False
        b_cd = Buf("const_done")
        bufs_c.append(b_cd)
        for b in bufs_c:
            b.full, b.partials, b.readers, b.prev_readers = const_ops[-1], list(const_ops), [], []
        ld("pool", "misc", [b_wg2], wg2[:], wg2_d, reads=[b_cd])
        ld("pool", "misc", [b_wg], wg[:], w_in[:, 1536:1552].rearrange("(k p) n -> p k n", p=128), reads=[b_cd])
        wpk = big[:, 0:8, :]
        wpv = big[:, 8:24, :].rearrange("p (k a) n -> p k (a n)", a=2)
        ld("pool", "misc", [b_big], wpk, w_in[:, 256:512].rearrange("(k p) n -> p k n", p=128), reads=[b_cd])
        ld("pool", "misc", [b_big], wpv, w_in[:, 512:1024].rearrange("(k p) n -> p k n", p=128), reads=[b_cd])
        misc_ops = [o for o in P.ops if o.is_dma and o.semname == "misc"]
        for b in (b_wg2, b_wg, b_big):
            b.full, b.partials, b.readers = misc_ops[-1], list(misc_ops), []

        tiles = []

        def wtile(src2d, r0, c0, ncols=512):
            return src2d[r0:r0 + 1024, c0:c0 + ncols].rearrange("(k p) n -> p k n", p=128)

        for j in range(12):
            tiles.append(wtile(w_ada, 0, 512 * j))

        def block_tiles():
            tl = [wtile(w_in, 0, 0), wtile(w_in, 0, 512), wtile(w_in, 0, 2064),
                  wtile(w_in, 0, 1024), wtile(w_in, 0, 1552),
                  wtile(w_o, 0, 0), wtile(w_o, 0, 512)]
            for j in range(8):
                tl.append(wtile(w_up, 0, 512 * j))
            for half in range(2):
                for g in range(4):
                    tl.append(wtile(w_down, 1024 * g, 512 * half))
            return tl
        NBT = 23
        wscr = nc.dram_tensor("wscr", [NBT, 128, 8 * 512], BF16, kind="Internal").ap()
        scr_b = [Buf("wscr%d" % j) for j in range(NBT)]
        nblocks_chain = NPB + (1 if DO_SAMPLE else 0)
        wstate = {"next_load": 0, "next_use": 0}

        def convert_weights():
            base = wstate["next_load"]
            for j, tsrc in enumerate(block_tiles()):
                slot = (base + j) % RING
                t, b = ring[slot]
                ld("pool", "ring%d" % slot, [b], t[:], tsrc)
                P.dma("pool", "ws%d" % slot, [b], [scr_b[j]],
                      lambda e, j=j, t=t: e.dma_start(out=wscr[j].rearrange("p (k n) -> p k n", k=8), in_=t[:]),
                      dur=6000.0, nbytes=1048576.0)
            wstate["conv_base"] = base
            for bi_ in range(nblocks_chain):
                for j in range(NBT):
                    tiles.append(("plain", wscr[j].rearrange("p (k n) -> p k n", k=8), j))

        def w_issue(upto):
            while wstate["next_load"] < min(upto, len(tiles)):
                i = wstate["next_load"]
                t, b = ring[i % RING]
                ent = tiles[i]
                if not isinstance(ent, tuple):
                    ld("pool", "ring%d" % (i % RING), [b], t[:], ent, reads=[b_cd])
                elif ent[0] == "plain":
                    ld("pool", "ring%d" % (i % RING), [b], t[:], ent[1], reads=[scr_b[ent[2]]])
                wstate["next_load"] += 1

        def w_acquire():
            i = wstate["next_use"]
            wstate["next_use"] += 1
            assert i < wstate["next_load"]
            return ring[i % RING]

        def w_release(n=1):
            w_issue(wstate["next_load"] + n)

        w_issue(RING)

        P.op("dve", [], [b_ones], lambda e: e.memset(ones_bf[:], 1.0))
        for k_, nm in enumerate(("p", "s", "pre")):
            cpy([b_tC], [Ubf[nm][1]], Ubf[nm][0][:, :], tC[:, k_ * 128:(k_ + 1) * 128])
        P.op("dve", [], [b_ones0], lambda e: e.memset(ones0[:], 0.0))
        P.op("dve", [b_ones0], [b_ones0], lambda e: e.memset(ones0[0:1, :], 1.0))
        b_qsel.multi = False
        P.op("dve", [], [b_qsel], lambda e: e.memset(qsel[:], 0.0))
        b_qsel.multi = True
        for st__ in ("p", "s"):
            P.op("dve", [], [bsp[st__][1]], lambda e, st__=st__: e.memset(bsp[st__][0][:], 0.0))
        ts([b_vecT], [b_gn], gn128[:], vecT[:, V_GN:V_GN + 1], float(np.sqrt(128.0)), None, ALU.mult)
        for st_, rb in (("p", Rb[0]), ("s", Rb[1])):
            t, b = SCT[st_]
            act([rb[1]], [b], t, rb[0][:, 0, :].rearrange("p (k m) -> p k m", k=8), AF.Silu)
        cpy([SCT["p"][1]], [b_SC5], SC5[:, :, 0:1], SCT["p"][0][:, :, 0:1])
        cpy([SCT["s"][1]], [b_SC5], SC5[:, :, 1:5], SCT["s"][0][:, :, 0:128:32])
        mt_slot = {0: (0, 0), 1: (0, 4), 2: (1, 0), 3: (1, 4), 6: (2, 0), 7: (2, 4), 8: (3, 0), 9: (3, 4)}
        vcol = {0: V_SH1, 1: V_SC1, 2: V_SH2, 3: V_SC2}
        for j in range(12):
            wt, wb = w_acquire()
            if j in (4, 5, 10, 11):
                gname = "G1" if j < 6 else "G2"
                bat, bab = (tA, b_tA) if j < 6 else (tB, b_tB)
                hf = j % 2 if j < 6 else (j - 10)
                for st_ in ("p", "s"):
                    pt, pb = bank()
                    for kc in range(8):
                        mm([SCT[st_][1], wb], [pb], pt[:, :], SCT[st_][0][:, kc, :], wt[:, kc, :],
                           start=(kc == 0), stop=(kc == 7))
                    gt, gbuf = modS[st_][gname]
                    stt([pb, bab], [gbuf], gt[:, hf * 512:(hf + 1) * 512], pt[:, :], 1.0,
                        bat[:, hf * 512:(hf + 1) * 512], ALU.add, ALU.add)
            else:
                v, c0 = mt_slot[j]
                for cc in range(4):
                    pt, pb = bank()
                    for kc in range(8):
                        mm([b_SC5, wb], [pb], pt[:, 0:5], wt[:, kc, cc * 128:(cc + 1) * 128], SC5[:, kc, :],
                           start=(kc == 0), stop=(kc == 7))
                    col = vcol[v] + c0 + cc
                    ts([pb, b_vecT], [b_MT], MT[:, v, c0 + cc, :], pt[:, 0:5], vecT[:, col:col + 1], None, ALU.add)
            w_release()
        convert_weights()
        w_issue(wstate["next_load"] + RING)
        for kc in range(8):
            for (ai, bi_, sc, sh, gcol, bcol) in ((0, 1, 1, 0, V_GIN, V_BIN), (2, 3, 3, 2, V_G1, V_B1)):
                ts([b_MT, b_vecT], [b_AB], AB[:, ai, kc, :], MT[:, sc, kc, :], 1.0,
                   vecT[:, gcol + kc:gcol + kc + 1], ALU.add, ALU.mult)
                ts([b_MT, b_vecT], [b_AB], AB[:, bi_, kc, :], MT[:, sc, kc, :], 1.0,
                   vecT[:, bcol + kc:bcol + kc + 1], ALU.add, ALU.mult)
                tt([b_AB, b_MT], [b_AB], AB[:, bi_, kc, :], AB[:, bi_, kc, :], MT[:, sh, kc, :], ALU.add)
        for st_ in ("p", "s"):
            for (gn_, cn_, bt, bb) in (("G1", "C1", yst, b_yst), ("G2", "C2", tZ, b_tZ)):
                gt, gbuf = modS[st_][gn_]
                ct, cbuf = modS[st_][cn_]
                tt([gbuf, bb], [b_tC], tC[:], gt[:], bt[:], ALU.mult)
                stt([cbuf, b_tC], [cbuf], ct[:], ct[:], ALPHA, tC[:], ALU.mult, ALU.add)
        for n in ("P1", "P2"):
            t, b = modP[n]
            ts([b], [b], t[:], t[:], ALPHA, None, ALU.mult)
        for ri, (st_, mk, mkb) in enumerate((("p", trilT, b_tril), ("s", masks, b_masks))):
            Rt_, Rb_ = Rb[ri]
            for g in range(4):
                pt, pb = bank()
                tr([Rb_, b_ident], [pb], pt[:, 0:128], Rt_[:, 1, g * 128:(g + 1) * 128])
                tt([pb, mkb], [wsT[st_][1]], wsT[st_][0][:, g, :], pt[:, 0:128], mk[:, :], ALU.mult)
            t, b = bsp[st_]
            src = Rt_[0:1, 1, 512:1024]
            cpy([Rb_], [b], t[0:1, 0, :], src)
            cpy([b], [Rb_], Rt_[0:1, 1, 0:512], t[0:1, 0, :])
            tt([Rb_], [b], t[0:1, 1, :], src, Rt_[0:1, 1, 0:512], ALU.subtract)

        stat2, b_stat2 = T("stat2", [128, 2, 6])
        mv2, b_mv2 = T("mv2", [128, 2])
        rs2, b_rs2 = T("rs2", [128, 2])
        Sos, b_Sos = tZ[:, :].rearrange("p (s a e) -> p s a e", s=4, a=2), b_tZ
        sph, b_sph = T("sph", [128, NT, 2, 256], BF16)
        SS = {"a": (stat, b_stat, mv, b_mv, rs, b_rs), "b": (stat2, b_stat2, mv2, b_mv2, rs2, b_rs2)}
        def layer_norm_rows(Rt, Rbuf, t_, out_ap, out_bufs, ss="a"):
            st, bst, mv_, bmv, rs_, brs = SS[ss]
            for hh in range(2):
                P.op("dve", [Rbuf], [bst], lambda e, hh=hh: e.bn_stats(out=st[:, hh, :], in_=Rt[:, t_, hh * 512:(hh + 1) * 512]), dur=700.0)
            P.op("dve", [bst], [bmv], lambda e: e.bn_aggr(out=mv_[:, :], in_=st[:, :, :].rearrange("p a b -> p (a b)")))
            rsqrt_(bmv, mv_[:, 1:2], LN_EPS, brs, rs_[:, 0:1])
            stt([bmv, brs], [brs], rs_[:, 1:2], mv_[:, 0:1], -1.0, rs_[:, 0:1], ALU.mult, ALU.mult)
            act([Rbuf, brs], out_bufs, out_ap, Rt[:, t_, :], AF.Identity, bias=rs_[:, 1:2], scale=rs_[:, 0:1])

        def transpose_affine(Rt, Rbuf, t_, dst, dstbuf, ai, groups):
            for _ in transpose_affine_g(Rt, Rbuf, t_, dst, dstbuf, ai, groups):
                pass

        def transpose_affine_g(Rt, Rbuf, t_, dst, dstbuf, ai, groups):
            for hb in range(2):
                pt, pb = bank()
                for k4 in range(4):
                    kc = hb * 4 + k4
                    tr([Rbuf, b_ident], [pb], pt[:, k4 * 128:(k4 + 1) * 128], Rt[:, t_, kc * 128:(kc + 1) * 128])
                for k4 in range(4):
                    kc = hb * 4 + k4
                    for (c0, c1, bi) in groups:
                        o = dst[:, kc, t_ * 128 + c0:t_ * 128 + c1]
                        i_ = pt[:, k4 * 128 + c0:k4 * 128 + c1]
                        if hb == 0:
                            act([pb, b_AB], [dstbuf], o, i_, AF.Identity, bias=AB[:, ai + 1, kc, bi:bi + 1],
                                scale=AB[:, ai, kc, bi:bi + 1])
                        else:
                            ts([pb, b_AB], [dstbuf], o, i_, AB[:, ai, kc, bi:bi + 1], AB[:, ai + 1, kc, bi:bi + 1],
                               ALU.mult, ALU.add)
                yield

        def gate_glr(ntl, H, Hbuf):
            pt, pb = bank()
            for kc in range(8):
                mm([b_wg, Hbuf], [pb], pt[0:16, 0:ntl * 128], wg[:, kc, :], H[:, kc, 0:ntl * 128],
                   start=(kc == 0), stop=(kc == 7))
            act([pb], [b_glrT], glrT[:, 0:ntl * 128], pt[0:16, 0:ntl * 128], AF.Copy)

        def gate_sp(t_):
            pt, pb = bank()
            mm([b_glrT, b_wg2], [pb], pt[:, 0:256], glrT[:, t_ * 128:(t_ + 1) * 128], wg2[:, :])
            tt([pb, b_bg2], [b_spt], spt[:, t_, :], pt[:, 0:256], bg2[:, :], ALU.add)
            act([b_spt], [b_spt], spt[:, t_, :], spt[:, t_, :], AF.Exp, scale=-1.0)
            act([b_spt], [b_spt], spt[:, t_, :], spt[:, t_, :], AF.Ln, bias=1.0)
            act([b_spt], [b_sph], sph[:, t_, 0, :], spt[:, t_, :], AF.Copy)
            tt([b_spt, b_sph], [b_sph], sph[:, t_, 1, :], spt[:, t_, :], sph[:, t_, 0, :], ALU.subtract)

        def state_update(pt, pb, col0, e_idx, snap_idx):
            for pr in range(2):
                ev = Et[:, pr, e_idx:e_idx + 1]
                ts([b_S, b_Et], [b_Se], Se[:, pr, :], S[:, pr, :], ev, None, ALU.mult)
                stt([pb, b_Et, b_Se], [b_S], S[:, pr, :], pt[:, col0 + pr * 128:col0 + (pr + 1) * 128], ev, Se[:, pr, :],
                    ALU.mult, ALU.add)
            if snap_idx is not None:
                act([b_S], [b_Sb], Sb[:, snap_idx, :, :], S[:, :, :], AF.Copy)

        Hs = [(hT, b_hT), (mixT, b_mixT), (h2T, b_h2T)]
        ENB = [(tA, b_tA), (tB, b_tB)]

        def preA(i):
            Rt, Rbuf = Rb[i % 2]
            H, Hbuf = Hs[i % 3]
            ld("sp", "R%d" % (i % 2), [Rbuf], Rt[:, 0:NT, :],
               xmain[i * NB:(i + 1) * NB, :].rearrange("(t p) d -> p t d", p=128))
            for t_ in range(NT):
                layer_norm_rows(Rt, Rbuf, t_, Rt[:, t_, :], [Rbuf], "b")
                yield
                for _ in transpose_affine_g(Rt, Rbuf, t_, H, Hbuf, 0, [(0, 128, 0)]):
                    yield

        def preB(i):
            H, Hbuf = Hs[i % 3]
            en, enb_ = ENB[i % 2]
            ut, ub = Ubf["pre"]
            gate_glr(NT, H, Hbuf)
            yield
            for t_ in range(NT):
                gate_sp(t_)
                yield
                pt, pb = bank()
                for hl in range(2):
                    mm([ub, b_sph], [pb], pt[:, 0:256], ut[:, :], sph[:, t_, hl, :], start=(hl == 0), stop=(hl == 1))
                act([pb], [enb_], en[:, t_ * 256:(t_ + 1) * 256], pt[:, 0:256], AF.Exp, scale=-1.0)
                pe_, peb = bank()
                for pr in range(2):
                    for hl in range(2):
                        mm([ub, b_sph], [peb], pe_[:, pr:pr + 1], sph[:, t_, hl, pr * 128:(pr + 1) * 128], ut[:, 127:128],
                           start=(hl == 0), stop=(hl == 1))
                sl = (i % 2) * 2 + t_
                act([peb], [b_Et], Et[:, :, sl:sl + 1], pe_[:, 0:2].rearrange("p (a b) -> p a b", b=1), AF.Exp)
                yield

        def preC(i):
            H, Hbuf = Hs[i % 3]
            en, enb_ = ENB[i % 2]
            for t_ in range(NT):
                pk, pkb = bank()
                for kc in range(8):
                    mm([Hbuf, b_big], [pkb], pk[:, 0:256], H[:, kc, t_ * 128:(t_ + 1) * 128], wpk[:, kc, :],
                       start=(kc == 0), stop=(kc == 7))
                tt([pkb, enb_], [b_ktm], ktm[:, t_, :], pk[:, 0:256], en[:, t_ * 256:(t_ + 1) * 256], ALU.mult)
                yield
                pv, pvb = bank()
                for kc in range(8):
                    mm([Hbuf, b_big], [pvb], pv[:, :], H[:, kc, t_ * 128:(t_ + 1) * 128], wpv[:, kc, :],
                       start=(kc == 0), stop=(kc == 7))
                act([pvb], [b_vbf], vbf[:, t_, :], pv[:, :], AF.Copy)
                yield
                pp, ppb = bank()
                for h in range(4):
                    pr, hf = h // 2, h % 2
                    mm([b_ktm, b_vbf], [ppb], pp[hf * 64:(hf + 1) * 64, pr * 128:(pr + 1) * 128],
                       ktm[:, t_, h * 64:(h + 1) * 64], vbf[:, t_, h * 128:(h + 1) * 128], tp=(0, hf * 64))
                sl = (i % 2) * 2 + t_
                state_update(pp, ppb, 0, sl, None)
                tt([b_Dt, b_Et], [b_Dt], Dt[:, :], Dt[:, :], Et[:, :, sl], ALU.mult)
                yield

        def round_robin(gens):
            gens = [g for g in gens if g is not None]
            while gens:
                alive = []
                for g in gens:
                    try:
                        next(g)
                        alive.append(g)
                    except StopIteration:
                        pass
                gens = alive

        class Ctx:
            pass

        def make_ctx(mode, blk, rbi):
            c = Ctx()
            c.sample = (mode == "sample")
            c.blk = blk
            c.ntl = 1 if c.sample else NT
            c.n = c.ntl * 128
            c.st = "s" if c.sample else "p"
            c.cs = 32 if c.sample else 64
            c.cpt = 128 // c.cs
            c.Rt, c.Rbuf = Rb[rbi]
            c.rbi = rbi
            c.src = xmain[2048:2176, :] if c.sample else xmain[blk * NB:(blk + 1) * NB, :]
            c.groups = [(32 * s_, 32 * s_ + 32, 1 + s_) for s_ in range(4)] if c.sample else [(0, 128, 0)]
            return c

        def head1(c):
            ld("sp", "R%d" % c.rbi, [c.Rbuf], c.Rt[:, 0:c.ntl, :], c.src.rearrange("(t p) d -> p t d", p=128))
            for t_ in range(c.ntl):
                layer_norm_rows(c.Rt, c.Rbuf, t_, c.Rt[:, t_, :], [c.Rbuf], "b")
                yield
                for _ in transpose_affine_g(c.Rt, c.Rbuf, t_, hT, b_hT, 0, c.groups):
                    yield
            gate_glr(c.ntl, hT, b_hT)
            yield
            ut, ub = Ubf[c.st]
            for t_ in range(c.ntl):
                gate_sp(t_)
                yield
                pt, pb = bank()
                for hl in range(2):
                    mm([ub, b_sph], [pb], pt[:, 0:256], ut[:, :], sph[:, t_, hl, :], start=(hl == 0), stop=(hl == 1))
                act([pb], [b_tC], tC[:, t_ * 256:(t_ + 1) * 256], pt[:, 0:256], AF.Exp, scale=-1.0)
                pt2, pb2 = bank()
                for pr in range(2):
                    for hl in range(2):
                        mm([ub, b_sph], [pb2], pt2[:, pr * 128:(pr + 1) * 128], sph[:, t_, hl, pr * 128:(pr + 1) * 128],
                           ut[:, :], start=(hl == 0), stop=(hl == 1))
                v3 = pt2[:, 0:256].rearrange("p (a b) -> p a b", a=2)
                act([pb2], [b_ebT], ebT[:, :, t_ * 128:(t_ + 1) * 128], v3, AF.Exp)
                act([pb2], [b_enbT], enbT[:, :, t_ * 128:(t_ + 1) * 128], v3, AF.Exp, scale=-1.0)
                for cl_ in range(c.cpt):
                    cend = cl_ * c.cs + c.cs - 1
                    cidx = t_ * c.cpt + cl_
                    act([pb2], [b_Et], Et[:, :, cidx:cidx + 1], v3[:, :, cend:cend + 1], AF.Exp)
                yield

        def head2(c):
            n, ntl = c.n, c.ntl
            wt, wb = w_acquire()
            for pr in range(2):
                pt, pb = bank()
                for kc in range(8):
                    mm([wb, b_hT], [pb], pt[:, 0:n], wt[:, kc, pr * 128:(pr + 1) * 128], hT[:, kc, 0:n],
                       start=(kc == 0), stop=(kc == 7))
                for hf in range(2):
                    hs_ = slice(hf * 64, (hf + 1) * 64)
                    stt([pb, b_ebT], [b_qsel], qsel[hs_, 2 * pr + hf, 0:n], pt[hs_, 0:n], 0.125, ebT[hs_, pr, 0:n],
                        ALU.mult, ALU.mult)
                pt, pb = bank()
                for kc in range(8):
                    mm([wb, b_hT], [pb], pt[:, 0:n], wt[:, kc, 256 + pr * 128:256 + (pr + 1) * 128], hT[:, kc, 0:n],
                       start=(kc == 0), stop=(kc == 7))
                tt([pb, b_enbT], [b_ktT], ktT[:, pr, 0:n], pt[:, 0:n], enbT[:, pr, 0:n], ALU.mult)
                yield
            for t_ in range(ntl):
                pt, pb = bank()
                for kc in range(8):
                    mm([wb, b_hT], [pb], pt[:, 0:256], hT[:, kc, t_ * 128:(t_ + 1) * 128], wt[:, kc, 256:512],
                       start=(kc == 0), stop=(kc == 7))
                tt([pb, b_tC], [b_ktm], ktm[:, t_, :], pt[:, 0:256], tC[:, t_ * 256:(t_ + 1) * 256], ALU.mult)
            w_release()
            yield
            wt, wb = w_acquire()
            for t_ in range(ntl):
                pt, pb = bank()
                for kc in range(8):
                    mm([wb, b_hT], [pb], pt[:, :], hT[:, kc, t_ * 128:(t_ + 1) * 128], wt[:, kc, :],
                       start=(kc == 0), stop=(kc == 7))
                act([pb], [b_vbf], vbf[:, t_, :], pt[:, :], AF.Copy)
                for cl_ in range(c.cpt):
                    act([pb, cmask[c.st][1]], [b_vsel], vsel[:, t_ * c.cpt + cl_, :], pt[:, :], AF.Copy,
                        scale=cmask[c.st][0][:, cl_:cl_ + 1])
                yield
            w_release()
            for _ in gla_early(c):
                yield
            wt, wb = w_acquire()
            st, bst, mv_, bmv, rs_, brs = SS["b"]
            for t_ in range(ntl):
                pt, pb = bank()
                for kc in range(8):
                    mm([wb, b_hT], [pb], pt[:, :], hT[:, kc, t_ * 128:(t_ + 1) * 128], wt[:, kc, :],
                       start=(kc == 0), stop=(kc == 7))
                act([pb], [b_tZ], tZ[:, 0:512], pt[:, :], AF.Gelu)
                P.op("dve", [b_tZ], [bst], lambda e: e.bn_stats(out=st[:, 0, :], in_=tZ[:, 0:512]), dur=700.0)
                P.op("dve", [bst], [bmv], lambda e: e.bn_aggr(out=mv_[:, :], in_=st[:, 0, 0:6]))
                rsqrt_(bmv, mv_[:, 1:2], LN_EPS, brs, rs_[:, 0:1])
                stt([bmv, brs], [brs], rs_[:, 1:2], mv_[:, 0:1], -1.0, rs_[:, 0:1], ALU.mult, ALU.mult)
                stt([b_gg, brs, b_gb], [b_tZ], tZ[:, 512:1024], gg[:, :], rs_[:, 1:2], gb[:, :], ALU.mult, ALU.add)
                stt([b_tZ, brs, b_gg], [b_tZ], tZ[:, 0:512], tZ[:, 0:512], rs_[:, 0:1], gg[:, :], ALU.mult, ALU.mult)
                if c.sample:
                    tt([b_tZ], [b_tZ], tZ[:, 512:1024], tZ[:, 0:512], tZ[:, 512:1024], ALU.add)
                    act([b_tZ], [b_zvn], zvn[:, t_, :], tZ[:, 512:1024], AF.Copy)
                    P.dma("sp", "st_zv", [b_tZ], [], lambda e: e.dma_start(out=zv_out, in_=tZ[:, 512:1024]))
                else:
                    tt([b_tZ], [b_zvn], zvn[:, t_, :], tZ[:, 0:512], tZ[:, 512:1024], ALU.add)
                yield
            w_release()

            for (func, dst, dbuf) in ((AF.Silu, srT, b_srT), (AF.Gelu, guT, b_guT)):
                wt, wb = w_acquire()
                for cc in range(4):
                    pt, pb = bank()
                    for kc in range(8):
                        mm([wb, b_hT], [pb], pt[:, 0:n], wt[:, kc, cc * 128:(cc + 1) * 128], hT[:, kc, 0:n],
                           start=(kc == 0), stop=(kc == 7))
                    act([pb], [dbuf], dst[:, cc, 0:n], pt[:, 0:n], func)
                    if cc % 2 == 1:
                        yield
                w_release()
        def gla_early(c):
            ntl, cs, cpt = c.ntl, c.cs, c.cpt
            nchunks = ntl * cpt
            if c.sample:
                act([b_S0s], [b_Sb], Sb[:, 0:4, :, :], S0s[:, :, :, :], AF.Copy)
            else:
                act([b_S], [b_Sb], Sb[:, 0, :, :], S[:, :, :], AF.Copy)
            Mk, Mkb = (masks, b_masks) if c.sample else (maskp, b_maskp)
            for t_ in range(ntl):
                tk = slice(t_ * 128, (t_ + 1) * 128)
                attm, b_attm = attm2[t_ % 2]
                pbanks = []
                for cl_ in range(cpt):
                    if cl_ % 2 == 0:
                        pbanks.append(bank())
                    pp, ppb = pbanks[-1]
                    for h in range(4):
                        pr, hf = h // 2, h % 2
                        c0 = (cl_ % 2) * 256 + pr * 128
                        mm([b_ktm, b_vsel], [ppb], pp[hf * 64:(hf + 1) * 64, c0:c0 + 128],
                           ktm[:, t_, h * 64:(h + 1) * 64], vsel[:, t_ * cpt + cl_, h * 128:(h + 1) * 128],
                           tp=(0, hf * 64))
                pa, pab = bank()
                for h in range(4):
                    pr = h // 2
                    mm([b_ktT, b_qsel], [pab], pa[:, h * 128:(h + 1) * 128], ktT[:, pr, tk], qsel[:, h, tk])
                for h in range(4):
                    tt([pab, Mkb], [b_attm], attm[:, h, :], pa[:, h * 128:(h + 1) * 128], Mk[:, :], ALU.mult)
                for cl_ in range(cpt):
                    cidx = t_ * cpt + cl_
                    pp, ppb = pbanks[cl_ // 2]
                    if c.sample:
                        for pr in range(2):
                            ev = Et[:, pr, cidx:cidx + 1]
                            ts([b_S0s, b_Et], [b_Se], Se[:, pr, :], S0s[:, cidx, pr, :], ev, None, ALU.mult)
                            c0 = (cl_ % 2) * 256 + pr * 128
                            stt([ppb, b_Et, b_Se], [b_Sos], Sos[:, cidx, pr, :], pp[:, c0:c0 + 128], ev, Se[:, pr, :],
                                ALU.mult, ALU.add)
                    else:
                        snap = cidx + 1 if cidx + 1 < nchunks else None
                        state_update(pp, ppb, (cl_ % 2) * 256, cidx, snap)
                yield
            if c.sample:
                P.dma("sp", "st_ss", [b_Sos], [], lambda e: e.dma_start(out=ss_out.rearrange("s p a e -> p s a e"), in_=Sos[:, :, :, :]))

        def gla_late(c):
            ntl, cs, cpt, st_ = c.ntl, c.cs, c.cpt, c.st
            for t_ in range(ntl):
                tk = slice(t_ * 128, (t_ + 1) * 128)
                attm, b_attm = attm2[t_ % 2]
                osq, b_osq = osq2[t_ % 2]
                tR, b_tR = (tC, b_tC) if t_ % 2 == 0 else (tA, b_tA)
                po, pob = bank()
                for h in range(4):
                    pr = h // 2
                    mm([b_vbf, b_attm], [pob], po[:, h * 128:(h + 1) * 128], vbf[:, t_, h * 128:(h + 1) * 128],
                       attm[:, h, :], start=True, stop=False)
                    for cl_ in range(cpt):
                        cidx = t_ * cpt + cl_
                        mm([b_Sb, b_qsel], [pob], po[:, h * 128 + cl_ * cs:h * 128 + (cl_ + 1) * cs],
                           Sb[:, cidx, pr, :], qsel[:, h, t_ * 128 + cl_ * cs:t_ * 128 + (cl_ + 1) * cs],
                           start=False, stop=(cl_ == cpt - 1))
                act([pob], [b_osq], osq[:, :], po[:, :], AF.Square)
                ps, psb = bank()
                mm([b_ones, b_osq], [psb], ps[:, :], ones_bf[:, :], osq[:, :])
                rsqrt_(psb, ps[:, :], 128.0 * RMS_EPS, b_tR, tR[:, 0:512])
                tt([pob, b_tR], [b_tR], tR[:, 512:1024], po[:, :], tR[:, 0:512], ALU.mult)
                stt([b_tR, b_gn, b_srT], [b_mixT], mixT[:, 0:4, tk], tR[:, 512:1024].rearrange("p (h i) -> p h i", h=4),
                    gn128[:, 0:1], srT[:, :, tk], ALU.mult, ALU.mult)
                pg, pgb = bank()
                wst, wsb = wsT[st_]
                bt_, bbuf = bsp[st_]
                for g in range(4):
                    o = pg[:, g * 128:(g + 1) * 128]
                    mm([b_zvn, wsb], [pgb], o, zvn[:, t_, g * 128:(g + 1) * 128], wst[:, g, :], start=True, stop=False)
                    mm([b_ones0, bbuf], [pgb], o, ones0[:, :], bt_[:, 0, g * 128:(g + 1) * 128], start=False, stop=False)
                    mm([b_ones0, bbuf], [pgb], o, ones0[:, :], bt_[:, 1, g * 128:(g + 1) * 128], start=False, stop=True)
                tt([pgb, b_guT], [b_mixT], mixT[:, 4:8, tk], pg[:, :].rearrange("p (g i) -> p g i", g=4), guT[:, :, tk], ALU.mult)

        def pull(gen, k):
            if gen is None:
                return
            for _ in range(k):
                try:
                    next(gen)
                except StopIteration:
                    return

        def drain(gen):
            if gen is None:
                return
            for _ in gen:
                pass

        def sub_matmuls(c, lhs, lhsbuf, nk, ntiles_per_half, gen, tile_major):
            ybanks = {}

            def group(t_, half, wts):
                pt, pb = bank(pin=True)
                ybanks[(t_, half)] = (pt, pb)
                k = 0
                for (wt, wb) in wts:
                    for kk in range(8):
                        mm([wb, lhsbuf], [pb], pt[:, :], lhs[:, k, t_ * 128:(t_ + 1) * 128], wt[:, kk, :],
                           start=(k == 0), stop=(k == nk - 1))
                        k += 1
            if tile_major:
                w2 = [[w_acquire() for _ in range(ntiles_per_half)] for _ in range(2)]
                for t_ in range(c.ntl):
                    for half in range(2):
                        group(t_, half, w2[half])
                w_release(2 * ntiles_per_half)
            else:
                for half in range(2):
                    wts = [w_acquire() for _ in range(ntiles_per_half)]
                    for t_ in range(c.ntl):
                        group(t_, half, wts)
                        pull(gen, 2)
                    w_release(ntiles_per_half)
            return ybanks

        def residual_ln(c, ybanks, t_, Gt, Gb, Pt, Pb_, Ct, Cb, final):
            Rt, Rbuf = c.Rt, c.Rbuf
            tt([Rbuf, Pb_], [b_tA], tA[:, :], Rt[:, t_, :], Pt[:, :], ALU.mult)
            tt([b_tA, Cb], [b_tA], tA[:, :], tA[:, :], Ct[:, :], ALU.add)
            for half in range(2):
                pt, pb = ybanks[(t_, half)]
                hs = slice(half * 512, (half + 1) * 512)
                tt([pb, Gb], [b_tB], tB[:, hs], pt[:, :], Gt[:, hs], ALU.mult)
                unpin(pb)
            tt([b_tA, b_tB], [Rbuf], Rt[:, t_, :], tA[:, :], tB[:, :], ALU.add)
            if not final:
                layer_norm_rows(Rt, Rbuf, t_, Rt[:, t_, :], [Rbuf], "a")
                transpose_affine(Rt, Rbuf, t_, h2T, b_h2T, 2, c.groups)
            else:
                layer_norm_rows(Rt, Rbuf, t_, tA[:, :], [b_tA], "a")
                tt([b_tA, modP["g2"][1]], [b_tB], tB[:, :], tA[:, :], modP["g2"][0][:, :], ALU.mult)
                tt([b_tB, modP["b2"][1]], [b_yst], yst[:, :], tB[:, :], modP["b2"][0][:, :], ALU.add)
                r0 = 2048 + t_ * 128 if c.sample else c.blk * NB + t_ * 128
                P.dma("sp", "yst", [b_yst], [], lambda e, r0=r0: e.dma_start(out=y_out[r0:r0 + 128, :], in_=yst[:, :]), nbytes=524288.0)

        def mid(c, g1):
            st_ = c.st
            gla_late(c)
            G1t, G1b = modS[st_]["G1"]; C1t, C1b = modS[st_]["C1"]
            P1t, P1b = modP["P1"]
            yb = sub_matmuls(c, mixT, b_mixT, 8, 1, None, True)
            for t_ in range(c.ntl):
                residual_ln(c, yb, t_, G1t, G1b, P1t, P1b, C1t, C1b, False)
            n = c.n
            for j in range(8):
                wt, wb = w_acquire()
                for cc in range(4):
                    ch = j * 4 + cc
                    pt, pb = bank()
                    for kc in range(8):
                        mm([wb, b_h2T], [pb], pt[:, 0:n], wt[:, kc, cc * 128:(cc + 1) * 128], h2T[:, kc, 0:n],
                           start=(kc == 0), stop=(kc == 7))
                    act([pb, b_vecT], [b_tZ], tZ[:, 0:n], pt[:, 0:n], AF.Relu, bias=vecT[:, V_BUP + ch:V_BUP + ch + 1])
                    tt([b_tZ], [b_big], big[:, ch, 0:n], tZ[:, 0:n], tZ[:, 0:n], ALU.mult)
                w_release()
                pull(g1, 1)

        def tail(c, nxt, g1):
            st_ = c.st
            G2t, G2b = modS[st_]["G2"]; C2t, C2b = modS[st_]["C2"]
            P2t, P2b = modP["P2"]
            yb = sub_matmuls(c, big, b_big, 32, 4, g1, False)
            drain(g1)
            g2 = head2(nxt) if nxt is not None else None
            for t_ in range(c.ntl):
                residual_ln(c, yb, t_, G2t, G2b, P2t, P2b, C2t, C2b, True)
                pull(g2, 7)
            drain(g2)

        P.op("dve", [], [b_S], lambda e: e.memset(S[:], 0.0))
        P.op("dve", [], [b_Dt], lambda e: e.memset(Dt[:], 1.0))
        for step in range(NPRE + 2):
            round_robin([preA(step) if step < NPRE else None,
                         preB(step - 1) if 0 <= step - 1 < NPRE else None,
                         preC(step - 2) if 0 <= step - 2 < NPRE else None])
        cc_in = nc.dram_tensor("cc_in", [128, 258], F32, kind="Internal").ap()
        cc_out = nc.dram_tensor("cc_out", [512, 258], F32, kind="Internal", addr_space="Local").ap()
        b_ccin, b_ccout = Buf("cc_in"), Buf("cc_out")
        P.dma("sp", "ccw", [b_S], [b_ccin], lambda e: e.dma_start(out=cc_in[:, 0:256], in_=S[:, :, :].rearrange("p a e -> p (a e)")))
        P.dma("sp", "ccw", [b_Dt], [b_ccin], lambda e: e.dma_start(out=cc_in[:, 256:258], in_=Dt[:, :]))
        if USE_CC:
            P.op("pool", [b_ccin], [b_ccout], lambda e: e.collective_compute(
                "AllGather", ALU.bypass, replica_groups=[[0, 1, 2, 3], [4, 5, 6, 7]], ins=[cc_in], outs=[cc_out]),
                dur=30000.0)
        else:
            for r_ in range(4):
                P.dma("sp", "ccw", [b_ccin], [b_ccout], lambda e, r_=r_: e.dma_start(out=cc_out[r_ * 128:(r_ + 1) * 128, :], in_=cc_in))
        P.op("dve", [b_ccin], [b_S], lambda e: e.memset(S[:], 0.0))
        for p_ in range(4):
            mcol = qmask[:, p_:p_ + 1]
            P.dma("sp", "ccr", [b_ccout], [b_Gt], lambda e, p_=p_: e.dma_start(out=Gt[:, :], in_=cc_out[p_ * 128:(p_ + 1) * 128, :]))
            ts([b_Gt, b_qmask], [b_Dp], Dp[:, :], Gt[:, 256:258], -1.0, mcol, ALU.add, ALU.mult)
            ts([b_Dp], [b_Dp], Dp[:, :], Dp[:, :], 1.0, None, ALU.add)
            for pr in range(2):
                ts([b_S, b_Dp], [b_Se], Se[:, pr, :], S[:, pr, :], Dp[:, pr:pr + 1], None, ALU.mult)
                stt([b_Gt, b_qmask, b_Se], [b_S], S[:, pr, :], Gt[:, pr * 128:(pr + 1) * 128], mcol, Se[:, pr, :],
                    ALU.mult, ALU.add)
        ctxs = []
        if DO_SAMPLE:
            ctxs.append(make_ctx("sample", 0, len(ctxs) % 2))
        for blk in range(NPB):
            ctxs.append(make_ctx("main", blk, len(ctxs) % 2))
        if ctxs:
            drain(head1(ctxs[0]))
            drain(head2(ctxs[0]))
        for idx, c in enumerate(ctxs):
            nxt = ctxs[idx + 1] if idx + 1 < len(ctxs) else None
            g1 = head1(nxt) if nxt is not None else None
            mid(c, g1)
            tail(c, nxt, g1)
        P.dma("sp", "st_sp", [b_S], [], lambda e: e.dma_start(out=sp_out, in_=S[:, :, :]))

        P.fixed = set(FIXED)
        if FILL:
            fb = banks[7][0]
            wrhs = wsT["p"][0][:, :, :].rearrange("p g i -> p (g i)")
            P.filler = {"fn": (lambda e: e.matmul(fb[:, :], ones_bf[:, :], wrhs, start=True, stop=True, skip_group_check=True)),
                        "dur": 270.0, "deps": [b_ones.full, wsT["p"][1].full], "t_min": 450000.0,
                        "min_gap": FILL_MIN_GAP, "frac": FILL_FRAC}
        P.codegen()


        with nc.Block() as block:
            @block.sync
            def _(e):
                P.replay("sp", e)

            @block.gpsimd
            def _(e):
                P.replay("pool", e)

            @block.tensor
            def _(e):
                P.replay("pe", e)

            @block.scalar
            def _(e):
                P.replay("act", e)

            @block.vector
            def _(e):
                P.replay("dve", e)
    return nc


_NC = None


def _consts():
    j = np.arange(128)[:, None]
    i = np.arange(128)[None, :]
    c = {}
    c["ident"] = np.eye(128, dtype=np.float32)
    c["maskp"] = ((j <= i) & (j // 64 == i // 64)).astype(np.float32)
    c["masks"] = ((j <= i) & (j // 32 == i // 32)).astype(np.float32)
    c["U_p"] = (c["maskp"] * np.float32(-1.0 / 16.0)).astype(np.float32)
    c["U_s"] = (c["masks"] * np.float32(-1.0 / 16.0)).astype(np.float32)
    c["U_pre"] = ((j <= i).astype(np.float32) * np.float32(-1.0 / 16.0)).astype(np.float32)
    c["trilT"] = (j <= i).astype(np.float32)
    p = np.arange(128)[:, None]
    cc = np.arange(4)[None, :]
    c["cmask_p"] = (p // 64 == cc).astype(np.float32)
    c["cmask_s"] = (p // 32 == cc).astype(np.float32)
    return c


def prepare(x_prompt, x_sample, state_gla, c_prompt, c_sample, ln_in_g, ln_in_b, w_ada, b_ada, w_in,
            w_gate2, b_gate2, gla_norm_g, gmlp_ln_g, gmlp_ln_b, w_spatial, b_spatial, w_o, b_o,
            ln1_g, ln1_b, w_up, b_up, w_down, b_down, ln2_g, ln2_b):
    f = lambda a: np.ascontiguousarray(np.asarray(a, dtype=np.float32))
    x_prompt, x_sample, state_gla = f(x_prompt), f(x_sample), f(state_gla)
    c_prompt, c_sample = f(c_prompt), f(c_sample)
    b_ada0 = f(b_ada)[0]
    consts = _consts()

    def chunksT(v):
        return f(v).reshape(-1, 128).T

    vecT = np.concatenate([
        chunksT(ln_in_g), chunksT(ln_in_b), chunksT(f(ln1_g)[0]), chunksT(f(ln1_b)[0]),
        chunksT(b_ada0[0:1024]), chunksT(b_ada0[1024:2048]), chunksT(b_ada0[3072:4096]), chunksT(b_ada0[4096:5120]),
        chunksT(f(b_up)[0]), chunksT(f(gla_norm_g)[0])], axis=1)
    bcast = lambda v: np.ascontiguousarray(np.broadcast_to(f(v).reshape(1, -1), (128, f(v).size)))
    shared = {
        "w_ada": f(w_ada)[0], "w_in": f(w_in)[0], "w_o": f(w_o)[0], "w_up": f(w_up)[0], "w_down": f(w_down)[0],
        "vecT": np.ascontiguousarray(vecT),
        "bc_g_in": bcast(ln_in_g), "bc_b_in": bcast(ln_in_b), "bc_b_o": bcast(f(b_o)[0]),
        "bc_g1": bcast(f(ln1_g)[0]), "bc_b1": bcast(f(ln1_b)[0]), "bc_b_down": bcast(f(b_down)[0]),
        "bc_g2": bcast(f(ln2_g)[0]), "bc_b2": bcast(f(ln2_b)[0]),
        "bc_ba_g1": bcast(b_ada0[2048:3072]), "bc_ba_g2": bcast(b_ada0[5120:6144]),
        "bc_bg2": bcast(f(b_gate2)[0]), "bc_gg": bcast(f(gmlp_ln_g)[0]), "bc_gb": bcast(f(gmlp_ln_b)[0]),
        "w_gate2": f(w_gate2)[0],
        "wsp_p": f(w_spatial)[0],
        "bsp_p": f(b_spatial)[0].reshape(1, 512),
        "bsp_s": np.ascontiguousarray(np.tile(f(b_spatial)[0][:, :32], (1, 4)).reshape(1, 512)),
    }
    wsp_s = np.zeros((4, 128, 128), np.float32)
    for s_ in range(4):
        wsp_s[:, 32 * s_:32 * s_ + 32, 32 * s_:32 * s_ + 32] = f(w_spatial)[0][:, :32, :32]
    shared["wsp_s"] = wsp_s
    shared.update(consts)

    in_maps = []
    for ci in range(8):
        b, q = ci // 4, ci % 4
        m = dict(shared)
        xs = x_sample[4 * ci:4 * ci + 4].reshape(128, D)
        m["xmain"] = np.ascontiguousarray(np.concatenate([x_prompt[b, 2048 * q:2048 * (q + 1)], xs], axis=0))
        m["qmask"] = np.ascontiguousarray(np.broadcast_to((np.arange(4) < q).astype(np.float32)[None, :], (128, 4)))
        ctp = c_prompt[b].reshape(8, 128).T
        m["cT_p"] = np.ascontiguousarray(np.broadcast_to(ctp[:, :, None], (128, 8, 128)))
        cs4 = c_sample[4 * ci:4 * ci + 4]
        cts = cs4.reshape(4, 8, 128).transpose(2, 1, 0)
        m["cT_s"] = np.ascontiguousarray(np.repeat(cts, 32, axis=2))
        st = state_gla[0, 4 * ci:4 * ci + 4]
        st = st.reshape(4, 2, 2, 64, 128).transpose(0, 2, 3, 1, 4)
        m["st0"] = np.ascontiguousarray(st.reshape(4, 128, 2, 128))
        in_maps.append(m)

    return in_maps


def kernel(**inputs):
    global _NC
    if _NC is None:
        _NC = build_program()
    in_maps = prepare(**inputs)
    res = run_bass_kernel_spmd(_NC, in_maps, core_ids=list(range(8)))
    return assemble(res.results)


def assemble(R):
    y_prompt = np.zeros((2, 8192, D), np.float32)
    y_sample = np.zeros((32, 32, D), np.float32)
    ns_p = np.zeros((1, 2, 4, 64, 128), np.float32)
    ns_s = np.zeros((1, 32, 4, 64, 128), np.float32)
    nv_s = np.zeros((1, 32, 32, 512), np.float32)

    def unstate(a):
        return a.reshape(2, 64, 2, 128).transpose(2, 0, 1, 3).reshape(4, 64, 128)
    for ci in range(8):
        b, q = ci // 4, ci % 4
        y = R[ci]["y"]
        y_prompt[b, 2048 * q:2048 * (q + 1)] = y[0:2048]
        y_sample[4 * ci:4 * ci + 4] = y[2048:2176].reshape(4, 32, D)
        if q == 3:
            ns_p[0, b] = unstate(R[ci]["s_out_p"])
        so = R[ci]["s_out_s"]
        for s_ in range(4):
            ns_s[0, 4 * ci + s_] = unstate(so[s_])
        nv_s[0, 4 * ci:4 * ci + 4] = R[ci]["zv_out"].reshape(4, 32, 512)
    return (y_prompt, y_sample, ns_p, ns_s, nv_s)
```

```python
import numpy as np
import concourse.bass as bass
import concourse.mybir as mybir
from concourse.bass_utils import run_bass_kernel_spmd

F32 = mybir.dt.float32
BF16 = mybir.dt.bfloat16
AF = mybir.ActivationFunctionType
ALU = mybir.AluOpType

D = 1024
NIN = 2576
DFF = 4096
NT = 2
NB = NT * 128
NPB = 2048 // NB
NPRE = NPB
ALPHA = 2.0 ** 0.25
LN_EPS = 1e-5
RMS_EPS = 1e-6
RING = 5
DO_SAMPLE = True
XLAT = 250.0
USE_CC = True
FIXED = ()
STOP = 99
V_GIN, V_BIN, V_G1, V_B1, V_SH1, V_SC1, V_SH2, V_SC2, V_BUP, V_GN = 0, 8, 16, 24, 32, 40, 48, 56, 64, 96
NV = 97


class Buf:
    def __init__(self, name, multi=False, excl=False):
        self.name = name
        self.excl = excl
        self.multi = multi
        self.full = None
        self.partials = []
        self.readers = []
        self.prev_readers = []


class Op:
    __slots__ = ("idx", "ek", "fn", "deps", "is_dma", "semname", "dur", "issue", "t_end", "pos", "nsucc", "users", "nbytes")

    def __init__(self, idx, ek, fn, deps, is_dma, semname, dur, issue):
        self.idx, self.ek, self.fn, self.deps = idx, ek, fn, deps
        self.is_dma, self.semname, self.dur, self.issue = is_dma, semname, dur, issue
        self.t_end = None
        self.pos = None
        self.users = []
        self.nbytes = 0.0


class Eng:
    def __init__(self, key, sem):
        self.key = key
        self.sem = sem
        self.cmds = []


class Prog:
    def __init__(self, nc, sems):
        self.nc = nc
        self.eng = {k: Eng(k, sems[k]) for k in ("pe", "act", "dve", "pool", "sp")}
        self.dsem = {}
        self.ops = []

    def _deps(self, reads, writes):
        deps = set()
        for b in reads:
            if b.full is not None:
                deps.add(b.full)
            deps.update(b.partials)
            if b.excl:
                deps.update(b.readers)
        for b in writes:
            if b.full is not None:
                deps.add(b.full)
            deps.update(b.readers)
            if b.multi:
                deps.update(b.prev_readers)
            else:
                deps.update(b.partials)
        return deps

    def _record(self, op, reads, writes):
        for b in reads:
            b.readers.append(op)
        for b in writes:
            if b.multi:
                if b.readers:
                    b.partials = [op]
                    b.prev_readers = b.readers
                    b.readers = []
                else:
                    b.partials.append(op)
            else:
                b.full = op
                b.partials = []
                b.readers = []
                b.prev_readers = []
        self.ops.append(op)

    def op(self, ek, reads, writes, fn, dur=150.0):
        op = Op(len(self.ops), ek, fn, self._deps(reads, writes), False, None, dur, dur)
        self._record(op, reads, writes)
        return op

    def dma(self, qk, semname, reads, writes, fn, dur=2000.0, nbytes=65536.0):
        issue = 1000.0 if qk == "pool" else 80.0
        op = Op(len(self.ops), qk, fn, self._deps(reads, writes), True, semname, dur, issue)
        op.nbytes = nbytes
        self._record(op, reads, writes)
        return op

    def schedule(self):
        ops = self.ops
        for o in ops:
            o.nsucc = len(o.deps)
            for d in o.deps:
                d.users.append(o)
        fixed = getattr(self, "fixed", set())
        nxt_fixed = {k: 0 for k in self.eng}
        per_eng = {k: [o for o in ops if o.ek == k] for k in self.eng}
        ready = {k: [] for k in self.eng}
        free = {k: 0.0 for k in self.eng}
        order = {k: [] for k in self.eng}
        rt = {}
        for o in ops:
            if o.nsucc == 0:
                ready[o.ek].append(o)
                rt[o] = 0.0
        n_done = 0
        dma_free = 0.0
        while n_done < len(ops):
            best = None
            for k, lst in ready.items():
                if not lst:
                    continue
                f = free[k]
                cand = None
                if k in fixed:
                    lst = [o for o in lst if o is per_eng[k][nxt_fixed[k]]]
                    if not lst:
                        continue
                for o in lst:
                    st = max(f, rt[o])
                    key = (st, o.idx) if rt[o] > f else (f, o.idx)
                    if cand is None or key < cand[0]:
                        cand = (key, o, st)
                if best is None or cand[0] < best[0]:
                    best = cand
            key, o, st = best
            k = o.ek
            ready[k].remove(o)
            nxt_fixed[k] += 1
            free[k] = st + o.issue
            if o.is_dma:
                t0 = max(st + o.issue, dma_free)
                dma_free = t0 + o.nbytes / 300.0
                o.t_end = dma_free + 2000.0
            else:
                o.t_end = st + o.dur
            order[k].append(o)
            n_done += 1
            for u in o.users:
                u.nsucc -= 1
                r = max(rt.get(u, 0.0), o.t_end + (XLAT if u.ek != o.ek else 0.0))
                rt[u] = r
                if u.nsucc == 0:
                    ready[u.ek].append(u)
        self.makespan = max(o.t_end for o in ops)
        return order

    def codegen(self):
        order = self.schedule()
        cnt = {k: 0 for k in self.eng}
        dcnt = {n: 0 for n in self.dsem}
        for k, lst in order.items():
            for o in lst:
                if o.is_dma:
                    dcnt[o.semname] += 1
                    o.pos = (self.dsem[o.semname], 16 * dcnt[o.semname], "dma:" + o.semname)
                else:
                    cnt[k] += 1
                    o.pos = (self.eng[k].sem, cnt[k], k)
        for k, lst in order.items():
            E = self.eng[k]
            waited = {}
            for o in lst:
                need = {}
                for d in o.deps:
                    sem, val, key = d.pos
                    if key == k and k == "pe":
                        continue
                    if key not in need or need[key][1] < val:
                        need[key] = (sem, val)
                for key, (sem, val) in need.items():
                    if waited.get(key, 0) < val:
                        E.cmds.append(("wait", sem, val))
                        waited[key] = val
                if o.is_dma:
                    E.cmds.append(("ins", o.fn, self.dsem[o.semname], 16))
                else:
                    E.cmds.append(("ins", o.fn, E.sem, 1))
        E = self.eng["sp"]
        for name, n in dcnt.items():
            if n > 0:
                E.cmds.append(("wait", self.dsem[name], 16 * n))

    def replay(self, ek, handle):
        for c in self.eng[ek].cmds:
            if c[0] == "wait":
                handle.wait_ge(c[1], c[2])
            else:
                ins = c[1](handle)
                ins.then_inc(c[2], c[3])


def build_program():
    nc = bass.Bass("TRN2", target_bir_lowering=False)

    def din(name, shape):
        return nc.dram_tensor(name, list(shape), F32, kind="ExternalInput").ap()

    def dout(name, shape):
        return nc.dram_tensor(name, list(shape), F32, kind="ExternalOutput").ap()

    xmain = din("xmain", [2176, D])
    qmask_d = din("qmask", [128, 4])
    cT_p = din("cT_p", [128, 8, 128])
    cT_s = din("cT_s", [128, 8, 128])
    st0 = din("st0", [4, 128, 2, 128])
    w_ada = din("w_ada", [D, 6 * D])
    w_in = din("w_in", [D, NIN])
    w_o = din("w_o", [D, D])
    w_up = din("w_up", [D, DFF])
    w_down = din("w_down", [DFF, D])
    vecT_d = din("vecT", [128, NV])
    bc_names = ["g_in", "b_in", "b_o", "g1", "b1", "b_down", "g2", "b2", "ba_g1", "ba_g2"]
    bc_d = {n: din("bc_" + n, [128, D]) for n in bc_names}
    bg2_d = din("bc_bg2", [128, 256])
    gg_d = din("bc_gg", [128, 512])
    gb_d = din("bc_gb", [128, 512])
    wg2_d = din("w_gate2", [16, 256])
    wsp_p_d = din("wsp_p", [4, 128, 128])
    wsp_s_d = din("wsp_s", [4, 128, 128])
    bsp_p_d = din("bsp_p", [1, 512])
    bsp_s_d = din("bsp_s", [1, 512])
    ident_d = din("ident", [128, 128])
    maskp_d = din("maskp", [128, 128])
    masks_d = din("masks", [128, 128])
    up_d = din("U_p", [128, 128])
    us_d = din("U_s", [128, 128])
    upre_d = din("U_pre", [128, 128])
    tril_d = din("trilT", [128, 128])
    cmask_p_d = din("cmask_p", [128, 4])
    cmask_s_d = din("cmask_s", [128, 4])

    y_out = dout("y", [2176, D])
    sp_out = dout("s_out_p", [128, 2, 128])
    ss_out = dout("s_out_s", [4, 128, 2, 128])
    zv_out = dout("zv_out", [128, 512])

    from contextlib import ExitStack
    es = ExitStack()

    def sb(name, shape, dt=F32):
        return es.enter_context(nc.sbuf_tensor("sb_" + name, list(shape), dt))

    def sem(name):
        return es.enter_context(nc.semaphore(name))

    with es:
        sems = {k: sem("s_" + k) for k in ("pe", "act", "dve", "pool", "sp")}
        P = Prog(nc, sems)
        for n in ["const", "R0", "R1", "yst", "misc", "st_zv", "st_ss", "st_sp", "m_a", "m_b", "m_c", "ccw", "ccr"] + ["ring%d" % i for i in range(RING)] + ["ws%d" % i for i in range(RING)]:
            P.dsem[n] = sem("d_" + n)

        MULTI = {"hT", "mixT", "h2T", "big", "srT", "guT", "qsel", "ktT"}

        def T(name, shape, dt=F32):
            t = sb(name, shape, dt)
            return t, Buf(name, multi=(name in MULTI))

        ident, b_ident = T("ident", [128, 128])
        maskp, b_maskp = T("maskp", [128, 128])
        masks, b_masks = T("masks", [128, 128])
        trilT, b_tril = T("trilT", [128, 128])
        vecT, b_vecT = T("vecT", [128, NV])
        qmask, b_qmask = T("qmask", [128, 4])
        Dt, b_Dt = T("Dt", [128, 2])
        Dp, b_Dp = T("Dp", [128, 2])
        Gt, b_Gt = T("Gt", [128, 258])
        ones_bf, b_ones = T("ones_bf", [128, 128], BF16)
        gn128, b_gn = T("gn128", [128, 1])
        bg2, b_bg2 = T("bg2", [128, 256])
        gg, b_gg = T("gg", [128, 512])
        gb, b_gb = T("gb", [128, 512])
        wg2, b_wg2 = T("wg2", [16, 256], BF16)
        wg, b_wg = T("wg", [128, 8, 16], BF16)
        wsT = {}
        for st_ in ("p", "s"):
            wsT[st_] = T("wsT_" + st_, [128, 4, 128], BF16)
        bsp = {}
        for st_ in ("p", "s"):
            bsp[st_] = T("bsp_" + st_, [128, 2, 512], BF16)
        SC5, b_SC5 = T("SC5", [128, 8, 5], BF16)
        MT, b_MT = T("MT", [128, 4, 8, 5])
        AB, b_AB = T("AB", [128, 4, 8, 5])
        modP = {n: T("mod_" + n, [128, D]) for n in ("P1", "P2", "g2", "b2")}
        modS = {st_: {n: T("mod_%s_%s" % (n, st_), [128, D]) for n in ("G1", "C1", "G2", "C2")}
                for st_ in ("p", "s")}
        ring = [T("ring%d" % i, [128, 8, 512], BF16) for i in range(RING)]
        Rb = [T("R%d" % i, [128, NT, D]) for i in range(2)]
        hT, b_hT = T("hT", [128, 8, NB], BF16)
        SCT = {"p": (hT[:, :, 0:128], b_hT), "s": (hT[:, :, 128:256], b_hT)}
        mixT, b_mixT = T("mixT", [128, 8, NB], BF16)
        h2T, b_h2T = T("h2T", [128, 8, NB], BF16)
        glrT, b_glrT = T("glrT", [16, NB], BF16)
        spt, b_spt = T("sp", [128, NT, 256])
        ebT, b_ebT = T("ebT", [128, 2, NB])
        enbT, b_enbT = T("enbT", [128, 2, NB])
        Et, b_Et = T("Et", [128, 2, 8])
        ktT, b_ktT = T("ktT", [128, 2, NB], BF16)
        ktm, b_ktm = T("ktm", [128, NT, 256], BF16)
        vbf, b_vbf = T("vbf", [128, NT, 512], BF16)
        srT, b_srT = T("srT", [128, 4, NB], BF16)
        guT, b_guT = T("guT", [128, 4, NB], BF16)
        zvn, b_zvn = T("zvn", [128, NT, 512], BF16)
        attm2 = [T("attm%d" % i, [128, 4, 128], BF16) for i in range(2)]
        osq2 = [T("osq%d" % i, [128, 512], BF16) for i in range(2)]
        qsel, b_qsel = T("qsel", [128, 4, NB], BF16)
        vsel, b_vsel = T("vsel", [128, 4, 512], BF16)
        ones0, b_ones0 = T("ones0", [128, 128], BF16)
        cmask = {"p": T("cmask_p", [128, 4]), "s": T("cmask_s", [128, 4])}
        S, b_S = T("S", [128, 2, 128])
        Se, b_Se = T("Se", [128, 2, 128])
        Sb, b_Sb = T("Sb", [128, 4, 2, 128], BF16)
        S0s, b_S0s = T("S0s", [128, 4, 2, 128])
        stat, b_stat = T("stat", [128, 2, 6])
        mv, b_mv = T("mv", [128, 2])
        rs, b_rs = T("rs", [128, 2])
        tA, b_tA = T("tA", [128, D])
        enbtm, b_enbtm = tA[:, 768:1024], b_tA
        tB, b_tB = T("tB", [128, D])
        tC, b_tC = T("tC", [128, D])
        tZ, b_tZ = T("tZ", [128, D])
        Ubf = {}
        for k_, nm in enumerate(("p", "s", "pre")):
            Ubf[nm] = T("Ubf_" + nm, [128, 128], BF16)
        big, b_big = T("big", [128, 32, NB], BF16)
        yst, b_yst = T("yst", [128, D])

        banks = []
        for i in range(8):
            t = es.enter_context(nc.psum_tensor("bank%d" % i, [128, 512], F32))
            banks.append((t, Buf("bank%d" % i, excl=True)))
        bank_ctr = [0]

        pinned = set()

        def bank(pin=False):
            for _ in range(8):
                i = bank_ctr[0] % 8
                bank_ctr[0] += 1
                if i not in pinned:
                    if pin:
                        pinned.add(i)
                    return banks[i]
            raise RuntimeError("all PSUM banks pinned")

        def unpin(pbuf):
            for i, (t, b) in enumerate(banks):
                if b is pbuf:
                    pinned.discard(i)


        def fsz(ap):
            n = 1
            for d in ap.shape[1:]:
                n *= int(d)
            return n

        def act(reads, writes, out, in_, func, bias=None, scale=None):
            kw = {}
            if bias is not None:
                kw["bias"] = bias
            if scale is not None:
                kw["scale"] = scale
            P.op("act", reads, writes, lambda e: e.activation(out=out, in_=in_, func=func, **kw),
                 dur=250.0 + fsz(in_) / 1.1)

        def tt(reads, writes, out, in0, in1, op, eng="dve"):
            P.op(eng, reads, writes, lambda e: e.tensor_tensor(out=out, in0=in0, in1=in1, op=op),
                 dur=120.0 + fsz(in0) / 0.9)

        def ts(reads, writes, out, in0, s1, s2, op0, op1=None, eng="dve"):
            d = 120.0 + fsz(in0) / 0.9
            if op1 is None:
                P.op(eng, reads, writes, lambda e: e.tensor_scalar(out=out, in0=in0, scalar1=s1, scalar2=None, op0=op0), dur=d)
            else:
                P.op(eng, reads, writes, lambda e: e.tensor_scalar(out=out, in0=in0, scalar1=s1, scalar2=s2, op0=op0, op1=op1), dur=d)

        def stt(reads, writes, out, in0, scalar, in1, op0, op1, eng="dve"):
            P.op(eng, reads, writes, lambda e: e.scalar_tensor_tensor(out=out, in0=in0, scalar=scalar, in1=in1, op0=op0, op1=op1),
                 dur=120.0 + fsz(in0) / 0.9)

        def cpy(reads, writes, out, in_, eng="dve"):
            P.op(eng, reads, writes, lambda e: e.tensor_copy(out=out, in_=in_), dur=120.0 + fsz(in_) / 0.9)

        def rsqrt_(sbuf_, src, eps, dbuf, dst):
            act([sbuf_], [dbuf], dst, src, AF.Ln, bias=float(eps))
            act([dbuf], [dbuf], dst, dst, AF.Exp, scale=-0.5)

        def mm(reads, writes, out, lhsT, rhs, start=True, stop=True, tp=None):
            kw = {}
            if tp is not None:
                kw["tile_position"] = tp
            n = max(64, fsz(rhs))
            d = n / 2.0 * (4.0 if rhs.dtype == F32 else 1.0)
            P.op("pe", reads, writes, lambda e: e.matmul(out, lhsT, rhs, start=start, stop=stop,
                                                         skip_group_check=True, **kw), dur=d)

        def tr(reads, writes, out, in_):
            P.op("pe", reads, writes, lambda e: e.transpose(out, in_, ident[:, :]), dur=215.0)

        def ld(q, semname, writes, out, in_, reads=()):
            nbytes = (4.0 if in_.dtype == F32 else 2.0) * int(out.shape[0]) * fsz(out)
            return P.dma(q, semname, list(reads), writes, lambda e: e.dma_start(out=out, in_=in_), nbytes=nbytes)

        Rb[0][1].multi = True
        Rb[1][1].multi = True
        cl = [(ident, b_ident, ident_d), (maskp, b_maskp, maskp_d), (masks, b_masks, masks_d),
              (trilT, b_tril, tril_d),
              (vecT, b_vecT, vecT_d), (qmask, b_qmask, qmask_d), (bg2, b_bg2, bg2_d), (gg, b_gg, gg_d),
              (gb, b_gb, gb_d), (cmask["p"][0], cmask["p"][1], cmask_p_d), (cmask["s"][0], cmask["s"][1], cmask_s_d)]
        bufs_c = []
        for t, b, d in cl:
            ld("sp", "const", [b], t[:], d)
            bufs_c.append(b)
        for n, src in (("P1", "g_in"), ("P2", "g1"), ("g2", "g2"), ("b2", "b2")):
            t, b = modP[n]
            ld("sp", "const", [b], t[:], bc_d[src])
            bufs_c.append(b)
        for st_ in ("p", "s"):
            t, b = modS[st_]["C1"]
            ld("sp", "const", [b], t[:], bc_d["b_in"])
            bufs_c.append(b)
            t, b = modS[st_]["C2"]
            ld("sp", "const", [b], t[:], bc_d["b1"])
            bufs_c.append(b)
        ld("sp", "const", [b_tA], tA[:], bc_d["ba_g1"]); bufs_c.append(b_tA)
        ld("sp", "const", [b_tB], tB[:], bc_d["ba_g2"]); bufs_c.append(b_tB)
        ld("sp", "const", [Rb[0][1]], Rb[0][0][:, 0, :].rearrange("p (k m) -> p k m", k=8), cT_p); bufs_c.append(Rb[0][1])
        ld("sp", "const", [Rb[1][1]], Rb[1][0][:, 0, :].rearrange("p (k m) -> p k m", k=8), cT_s); bufs_c.append(Rb[1][1])
        ld("sp", "const", [b_S0s], S0s[:], st0.rearrange("s p a e -> p s a e")); bufs_c.append(b_S0s)
        ld("sp", "const", [b_yst], yst[:], bc_d["b_o"]); bufs_c.append(b_yst)
        ld("sp", "const", [b_tZ], tZ[:], bc_d["b_down"]); bufs_c.append(b_tZ)
        for ri, (wsrc, bsrc) in enumerate(((wsp_p_d, bsp_p_d), (wsp_s_d, bsp_s_d))):
            ld("sp", "const", [Rb[ri][1]], Rb[ri][0][:, 1, 0:512].rearrange("p (g j) -> p g j", g=4), wsrc.rearrange("g i j -> i g j"))
            ld("sp", "const", [Rb[ri][1]], Rb[ri][0][0:1, 1, 512:1024], bsrc)
        for k_, ud in enumerate((up_d, us_d, upre_d)):
            ld("sp", "const", [b_tC], tC[:, k_ * 128:(k_ + 1) * 128], ud)
        bufs_c.append(b_tC)
        const_ops = [o for o in P.ops if o.is_dma and o.semname == "const"]
        Rb[0][1].multi = False
        Rb[1][1].multi = False
        b_cd = Buf("const_done")
        bufs_c.append(b_cd)
        for b in bufs_c:
            b.full, b.partials, b.readers, b.prev_readers = const_ops[-1], list(const_ops), [], []
        ld("pool", "misc", [b_wg2], wg2[:], wg2_d, reads=[b_cd])
        ld("pool", "misc", [b_wg], wg[:], w_in[:, 1536:1552].rearrange("(k p) n -> p k n", p=128), reads=[b_cd])
        wpk = big[:, 0:8, :]
        wpv = big[:, 8:24, :].rearrange("p (k a) n -> p k (a n)", a=2)
        ld("pool", "misc", [b_big], wpk, w_in[:, 256:512].rearrange("(k p) n -> p k n", p=128), reads=[b_cd])
        ld("pool", "misc", [b_big], wpv, w_in[:, 512:1024].rearrange("(k p) n -> p k n", p=128), reads=[b_cd])
        misc_ops = [o for o in P.ops if o.is_dma and o.semname == "misc"]
        for b in (b_wg2, b_wg, b_big):
            b.full, b.partials, b.readers = misc_ops[-1], list(misc_ops), []

        tiles = []

        def wtile(src2d, r0, c0, ncols=512):
            return src2d[r0:r0 + 1024, c0:c0 + ncols].rearrange("(k p) n -> p k n", p=128)

        for j in range(12):
            tiles.append(wtile(w_ada, 0, 512 * j))

        def block_tiles():
            tl = [wtile(w_in, 0, 0), wtile(w_in, 0, 512), wtile(w_in, 0, 2064),
                  wtile(w_in, 0, 1024), wtile(w_in, 0, 1552),
                  wtile(w_o, 0, 0), wtile(w_o, 0, 512)]
            for j in range(8):
                tl.append(wtile(w_up, 0, 512 * j))
            for half in range(2):
                for g in range(4):
                    tl.append(wtile(w_down, 1024 * g, 512 * half))
            return tl
        NBT = 23
        wscr = nc.dram_tensor("wscr", [NBT, 128, 8 * 512], BF16, kind="Internal").ap()
        scr_b = [Buf("wscr%d" % j) for j in range(NBT)]
        nblocks_chain = NPB + (1 if DO_SAMPLE else 0)
        wstate = {"next_load": 0, "next_use": 0}

        def convert_weights():
            base = wstate["next_load"]
            for j, tsrc in enumerate(block_tiles()):
                slot = (base + j) % RING
                t, b = ring[slot]
                ld("pool", "ring%d" % slot, [b], t[:], tsrc)
                P.dma("pool", "ws%d" % slot, [b], [scr_b[j]],
                      lambda e, j=j, t=t: e.dma_start(out=wscr[j].rearrange("p (k n) -> p k n", k=8), in_=t[:]),
                      dur=6000.0, nbytes=1048576.0)
            wstate["conv_base"] = base
            for bi_ in range(nblocks_chain):
                for j in range(NBT):
                    tiles.append(("plain", wscr[j].rearrange("p (k n) -> p k n", k=8), j))

        def w_issue(upto):
            while wstate["next_load"] < min(upto, len(tiles)):
                i = wstate["next_load"]
                t, b = ring[i % RING]
                ent = tiles[i]
                if not isinstance(ent, tuple):
                    ld("pool", "ring%d" % (i % RING), [b], t[:], ent, reads=[b_cd])
                elif ent[0] == "plain":
                    ld("pool", "ring%d" % (i % RING), [b], t[:], ent[1], reads=[scr_b[ent[2]]])
                wstate["next_load"] += 1

        def w_acquire():
            i = wstate["next_use"]
            wstate["next_use"] += 1
            assert i < wstate["next_load"]
            return ring[i % RING]

        def w_release(n=1):
            w_issue(wstate["next_load"] + n)

        w_issue(RING)

        P.op("dve", [], [b_ones], lambda e: e.memset(ones_bf[:], 1.0))
        for k_, nm in enumerate(("p", "s", "pre")):
            cpy([b_tC], [Ubf[nm][1]], Ubf[nm][0][:, :], tC[:, k_ * 128:(k_ + 1) * 128])
        P.op("dve", [], [b_ones0], lambda e: e.memset(ones0[:], 0.0))
        P.op("dve", [b_ones0], [b_ones0], lambda e: e.memset(ones0[0:1, :], 1.0))
        b_qsel.multi = False
        P.op("dve", [], [b_qsel], lambda e: e.memset(qsel[:], 0.0))
        b_qsel.multi = True
        for st__ in ("p", "s"):
            P.op("dve", [], [bsp[st__][1]], lambda e, st__=st__: e.memset(bsp[st__][0][:], 0.0))
        ts([b_vecT], [b_gn], gn128[:], vecT[:, V_GN:V_GN + 1], float(np.sqrt(128.0)), None, ALU.mult)
        for st_, rb in (("p", Rb[0]), ("s", Rb[1])):
            t, b = SCT[st_]
            act([rb[1]], [b], t, rb[0][:, 0, :].rearrange("p (k m) -> p k m", k=8), AF.Silu)
        cpy([SCT["p"][1]], [b_SC5], SC5[:, :, 0:1], SCT["p"][0][:, :, 0:1])
        cpy([SCT["s"][1]], [b_SC5], SC5[:, :, 1:5], SCT["s"][0][:, :, 0:128:32])
        mt_slot = {0: (0, 0), 1: (0, 4), 2: (1, 0), 3: (1, 4), 6: (2, 0), 7: (2, 4), 8: (3, 0), 9: (3, 4)}
        vcol = {0: V_SH1, 1: V_SC1, 2: V_SH2, 3: V_SC2}
        for j in range(12):
            wt, wb = w_acquire()
            if j in (4, 5, 10, 11):
                gname = "G1" if j < 6 else "G2"
                bat, bab = (tA, b_tA) if j < 6 else (tB, b_tB)
                hf = j % 2 if j < 6 else (j - 10)
                for st_ in ("p", "s"):
                    pt, pb = bank()
                    for kc in range(8):
                        mm([SCT[st_][1], wb], [pb], pt[:, :], SCT[st_][0][:, kc, :], wt[:, kc, :],
                           start=(kc == 0), stop=(kc == 7))
                    gt, gbuf = modS[st_][gname]
                    stt([pb, bab], [gbuf], gt[:, hf * 512:(hf + 1) * 512], pt[:, :], 1.0,
                        bat[:, hf * 512:(hf + 1) * 512], ALU.add, ALU.add)
            else:
                v, c0 = mt_slot[j]
                for cc in range(4):
                    pt, pb = bank()
                    for kc in range(8):
                        mm([b_SC5, wb], [pb], pt[:, 0:5], wt[:, kc, cc * 128:(cc + 1) * 128], SC5[:, kc, :],
                           start=(kc == 0), stop=(kc == 7))
                    col = vcol[v] + c0 + cc
                    ts([pb, b_vecT], [b_MT], MT[:, v, c0 + cc, :], pt[:, 0:5], vecT[:, col:col + 1], None, ALU.add)
            w_release()
        convert_weights()
        w_issue(wstate["next_load"] + RING)
        for kc in range(8):
            for (ai, bi_, sc, sh, gcol, bcol) in ((0, 1, 1, 0, V_GIN, V_BIN), (2, 3, 3, 2, V_G1, V_B1)):
                ts([b_MT, b_vecT], [b_AB], AB[:, ai, kc, :], MT[:, sc, kc, :], 1.0,
                   vecT[:, gcol + kc:gcol + kc + 1], ALU.add, ALU.mult)
                ts([b_MT, b_vecT], [b_AB], AB[:, bi_, kc, :], MT[:, sc, kc, :], 1.0,
                   vecT[:, bcol + kc:bcol + kc + 1], ALU.add, ALU.mult)
                tt([b_AB, b_MT], [b_AB], AB[:, bi_, kc, :], AB[:, bi_, kc, :], MT[:, sh, kc, :], ALU.add)
        for st_ in ("p", "s"):
            for (gn_, cn_, bt, bb) in (("G1", "C1", yst, b_yst), ("G2", "C2", tZ, b_tZ)):
                gt, gbuf = modS[st_][gn_]
                ct, cbuf = modS[st_][cn_]
                tt([gbuf, bb], [b_tC], tC[:], gt[:], bt[:], ALU.mult)
                stt([cbuf, b_tC], [cbuf], ct[:], ct[:], ALPHA, tC[:], ALU.mult, ALU.add)
        for n in ("P1", "P2"):
            t, b = modP[n]
            ts([b], [b], t[:], t[:], ALPHA, None, ALU.mult)
        for ri, (st_, mk, mkb) in enumerate((("p", trilT, b_tril), ("s", masks, b_masks))):
            Rt_, Rb_ = Rb[ri]
            for g in range(4):
                pt, pb = bank()
                tr([Rb_, b_ident], [pb], pt[:, 0:128], Rt_[:, 1, g * 128:(g + 1) * 128])
                tt([pb, mkb], [wsT[st_][1]], wsT[st_][0][:, g, :], pt[:, 0:128], mk[:, :], ALU.mult)
            t, b = bsp[st_]
            src = Rt_[0:1, 1, 512:1024]
            cpy([Rb_], [b], t[0:1, 0, :], src)
            cpy([b], [Rb_], Rt_[0:1, 1, 0:512], t[0:1, 0, :])
            tt([Rb_], [b], t[0:1, 1, :], src, Rt_[0:1, 1, 0:512], ALU.subtract)

        stat2, b_stat2 = T("stat2", [128, 2, 6])
        mv2, b_mv2 = T("mv2", [128, 2])
        rs2, b_rs2 = T("rs2", [128, 2])
        Sos, b_Sos = tZ[:, :].rearrange("p (s a e) -> p s a e", s=4, a=2), b_tZ
        sph, b_sph = T("sph", [128, NT, 2, 256], BF16)
        SS = {"a": (stat, b_stat, mv, b_mv, rs, b_rs), "b": (stat2, b_stat2, mv2, b_mv2, rs2, b_rs2)}
        def layer_norm_rows(Rt, Rbuf, t_, out_ap, out_bufs, ss="a"):
            st, bst, mv_, bmv, rs_, brs = SS[ss]
            for hh in range(2):
                P.op("dve", [Rbuf], [bst], lambda e, hh=hh: e.bn_stats(out=st[:, hh, :], in_=Rt[:, t_, hh * 512:(hh + 1) * 512]), dur=700.0)
            P.op("dve", [bst], [bmv], lambda e: e.bn_aggr(out=mv_[:, :], in_=st[:, :, :].rearrange("p a b -> p (a b)")))
            rsqrt_(bmv, mv_[:, 1:2], LN_EPS, brs, rs_[:, 0:1])
            stt([bmv, brs], [brs], rs_[:, 1:2], mv_[:, 0:1], -1.0, rs_[:, 0:1], ALU.mult, ALU.mult)
            act([Rbuf, brs], out_bufs, out_ap, Rt[:, t_, :], AF.Identity, bias=rs_[:, 1:2], scale=rs_[:, 0:1])

        def transpose_affine(Rt, Rbuf, t_, dst, dstbuf, ai, groups):
            for _ in transpose_affine_g(Rt, Rbuf, t_, dst, dstbuf, ai, groups):
                pass

        def transpose_affine_g(Rt, Rbuf, t_, dst, dstbuf, ai, groups):
            for hb in range(2):
                pt, pb = bank()
                for k4 in range(4):
                    kc = hb * 4 + k4
                    tr([Rbuf, b_ident], [pb], pt[:, k4 * 128:(k4 + 1) * 128], Rt[:, t_, kc * 128:(kc + 1) * 128])
                for k4 in range(4):
                    kc = hb * 4 + k4
                    for (c0, c1, bi) in groups:
                        o = dst[:, kc, t_ * 128 + c0:t_ * 128 + c1]
                        i_ = pt[:, k4 * 128 + c0:k4 * 128 + c1]
                        if hb == 0:
                            act([pb, b_AB], [dstbuf], o, i_, AF.Identity, bias=AB[:, ai + 1, kc, bi:bi + 1],
                                scale=AB[:, ai, kc, bi:bi + 1])
                        else:
                            ts([pb, b_AB], [dstbuf], o, i_, AB[:, ai, kc, bi:bi + 1], AB[:, ai + 1, kc, bi:bi + 1],
                               ALU.mult, ALU.add)
                yield

        def gate_glr(ntl, H, Hbuf):
            pt, pb = bank()
            for kc in range(8):
                mm([b_wg, Hbuf], [pb], pt[0:16, 0:ntl * 128], wg[:, kc, :], H[:, kc, 0:ntl * 128],
                   start=(kc == 0), stop=(kc == 7))
            act([pb], [b_glrT], glrT[:, 0:ntl * 128], pt[0:16, 0:ntl * 128], AF.Copy)

        def gate_sp(t_):
            pt, pb = bank()
            mm([b_glrT, b_wg2], [pb], pt[:, 0:256], glrT[:, t_ * 128:(t_ + 1) * 128], wg2[:, :])
            tt([pb, b_bg2], [b_spt], spt[:, t_, :], pt[:, 0:256], bg2[:, :], ALU.add)
            act([b_spt], [b_spt], spt[:, t_, :], spt[:, t_, :], AF.Exp, scale=-1.0)
            act([b_spt], [b_spt], spt[:, t_, :], spt[:, t_, :], AF.Ln, bias=1.0)
            act([b_spt], [b_sph], sph[:, t_, 0, :], spt[:, t_, :], AF.Copy)
            tt([b_spt, b_sph], [b_sph], sph[:, t_, 1, :], spt[:, t_, :], sph[:, t_, 0, :], ALU.subtract)

        def state_update(pt, pb, col0, e_idx, snap_idx):
            for pr in range(2):
                ev = Et[:, pr, e_idx:e_idx + 1]
                ts([b_S, b_Et], [b_Se], Se[:, pr, :], S[:, pr, :], ev, None, ALU.mult)
                stt([pb, b_Et, b_Se], [b_S], S[:, pr, :], pt[:, col0 + pr * 128:col0 + (pr + 1) * 128], ev, Se[:, pr, :],
                    ALU.mult, ALU.add)
            if snap_idx is not None:
                act([b_S], [b_Sb], Sb[:, snap_idx, :, :], S[:, :, :], AF.Copy)

        Hs = [(hT, b_hT), (mixT, b_mixT), (h2T, b_h2T)]
        ENB = [(tA, b_tA), (tB, b_tB)]

        def preA(i):
            Rt, Rbuf = Rb[i % 2]
            H, Hbuf = Hs[i % 3]
            ld("sp", "R%d" % (i % 2), [Rbuf], Rt[:, 0:NT, :],
               xmain[i * NB:(i + 1) * NB, :].rearrange("(t p) d -> p t d", p=128))
            for t_ in range(NT):
                layer_norm_rows(Rt, Rbuf, t_, Rt[:, t_, :], [Rbuf], "b")
                yield
                for _ in transpose_affine_g(Rt, Rbuf, t_, H, Hbuf, 0, [(0, 128, 0)]):
                    yield

        def preB(i):
            H, Hbuf = Hs[i % 3]
            en, enb_ = ENB[i % 2]
            ut, ub = Ubf["pre"]
            gate_glr(NT, H, Hbuf)
            yield
            for t_ in range(NT):
                gate_sp(t_)
                yield
                pt, pb = bank()
                for hl in range(2):
                    mm([ub, b_sph], [pb], pt[:, 0:256], ut[:, :], sph[:, t_, hl, :], start=(hl == 0), stop=(hl == 1))
                act([pb], [enb_], en[:, t_ * 256:(t_ + 1) * 256], pt[:, 0:256], AF.Exp, scale=-1.0)
                pe_, peb = bank()
                for pr in range(2):
                    for hl in range(2):
                        mm([ub, b_sph], [peb], pe_[:, pr:pr + 1], sph[:, t_, hl, pr * 128:(pr + 1) * 128], ut[:, 127:128],
                           start=(hl == 0), stop=(hl == 1))
                sl = (i % 2) * 2 + t_
                act([peb], [b_Et], Et[:, :, sl:sl + 1], pe_[:, 0:2].rearrange("p (a b) -> p a b", b=1), AF.Exp)
                yield

        def preC(i):
            H, Hbuf = Hs[i % 3]
            en, enb_ = ENB[i % 2]
            for t_ in range(NT):
                pk, pkb = bank()
                for kc in range(8):
                    mm([Hbuf, b_big], [pkb], pk[:, 0:256], H[:, kc, t_ * 128:(t_ + 1) * 128], wpk[:, kc, :],
                       start=(kc == 0), stop=(kc == 7))
                tt([pkb, enb_], [b_ktm], ktm[:, t_, :], pk[:, 0:256], en[:, t_ * 256:(t_ + 1) * 256], ALU.mult)
                yield
                pv, pvb = bank()
                for kc in range(8):
                    mm([Hbuf, b_big], [pvb], pv[:, :], H[:, kc, t_ * 128:(t_ + 1) * 128], wpv[:, kc, :],
                       start=(kc == 0), stop=(kc == 7))
                act([pvb], [b_vbf], vbf[:, t_, :], pv[:, :], AF.Copy)
                yield
                pp, ppb = bank()
                for h in range(4):
                    pr, hf = h // 2, h % 2
                    mm([b_ktm, b_vbf], [ppb], pp[hf * 64:(hf + 1) * 64, pr * 128:(pr + 1) * 128],
                       ktm[:, t_, h * 64:(h + 1) * 64], vbf[:, t_, h * 128:(h + 1) * 128], tp=(0, hf * 64))
                sl = (i % 2) * 2 + t_
                state_update(pp, ppb, 0, sl, None)
                tt([b_Dt, b_Et], [b_Dt], Dt[:, :], Dt[:, :], Et[:, :, sl], ALU.mult)
                yield

        def round_robin(gens):
            gens = [g for g in gens if g is not None]
            while gens:
                alive = []
                for g in gens:
                    try:
                        next(g)
                        alive.append(g)
                    except StopIteration:
                        pass
                gens = alive

        class Ctx:
            pass

        def make_ctx(mode, blk, rbi):
            c = Ctx()
            c.sample = (mode == "sample")
            c.blk = blk
            c.ntl = 1 if c.sample else NT
            c.n = c.ntl * 128
            c.st = "s" if c.sample else "p"
            c.cs = 32 if c.sample else 64
            c.cpt = 128 // c.cs
            c.Rt, c.Rbuf = Rb[rbi]
            c.rbi = rbi
            c.src = xmain[2048:2176, :] if c.sample else xmain[blk * NB:(blk + 1) * NB, :]
            c.groups = [(32 * s_, 32 * s_ + 32, 1 + s_) for s_ in range(4)] if c.sample else [(0, 128, 0)]
            return c

        def head1(c):
            ld("sp", "R%d" % c.rbi, [c.Rbuf], c.Rt[:, 0:c.ntl, :], c.src.rearrange("(t p) d -> p t d", p=128))
            for t_ in range(c.ntl):
                layer_norm_rows(c.Rt, c.Rbuf, t_, c.Rt[:, t_, :], [c.Rbuf], "b")
                yield
                for _ in transpose_affine_g(c.Rt, c.Rbuf, t_, hT, b_hT, 0, c.groups):
                    yield
            gate_glr(c.ntl, hT, b_hT)
            yield
            ut, ub = Ubf[c.st]
            for t_ in range(c.ntl):
                gate_sp(t_)
                yield
                pt, pb = bank()
                for hl in range(2):
                    mm([ub, b_sph], [pb], pt[:, 0:256], ut[:, :], sph[:, t_, hl, :], start=(hl == 0), stop=(hl == 1))
                act([pb], [b_tC], tC[:, t_ * 256:(t_ + 1) * 256], pt[:, 0:256], AF.Exp, scale=-1.0)
                pt2, pb2 = bank()
                for pr in range(2):
                    for hl in range(2):
                        mm([ub, b_sph], [pb2], pt2[:, pr * 128:(pr + 1) * 128], sph[:, t_, hl, pr * 128:(pr + 1) * 128],
                           ut[:, :], start=(hl == 0), stop=(hl == 1))
                v3 = pt2[:, 0:256].rearrange("p (a b) -> p a b", a=2)
                act([pb2], [b_ebT], ebT[:, :, t_ * 128:(t_ + 1) * 128], v3, AF.Exp)
                act([pb2], [b_enbT], enbT[:, :, t_ * 128:(t_ + 1) * 128], v3, AF.Exp, scale=-1.0)
                for cl_ in range(c.cpt):
                    cend = cl_ * c.cs + c.cs - 1
                    cidx = t_ * c.cpt + cl_
                    act([pb2], [b_Et], Et[:, :, cidx:cidx + 1], v3[:, :, cend:cend + 1], AF.Exp)
                yield

        def head2(c):
            n, ntl = c.n, c.ntl
            wt, wb = w_acquire()
            for pr in range(2):
                pt, pb = bank()
                for kc in range(8):
                    mm([wb, b_hT], [pb], pt[:, 0:n], wt[:, kc, pr * 128:(pr + 1) * 128], hT[:, kc, 0:n],
                       start=(kc == 0), stop=(kc == 7))
                for hf in range(2):
                    hs_ = slice(hf * 64, (hf + 1) * 64)
                    stt([pb, b_ebT], [b_qsel], qsel[hs_, 2 * pr + hf, 0:n], pt[hs_, 0:n], 0.125, ebT[hs_, pr, 0:n],
                        ALU.mult, ALU.mult)
                pt, pb = bank()
                for kc in range(8):
                    mm([wb, b_hT], [pb], pt[:, 0:n], wt[:, kc, 256 + pr * 128:256 + (pr + 1) * 128], hT[:, kc, 0:n],
                       start=(kc == 0), stop=(kc == 7))
                tt([pb, b_enbT], [b_ktT], ktT[:, pr, 0:n], pt[:, 0:n], enbT[:, pr, 0:n], ALU.mult)
                yield
            for t_ in range(ntl):
                pt, pb = bank()
                for kc in range(8):
                    mm([wb, b_hT], [pb], pt[:, 0:256], hT[:, kc, t_ * 128:(t_ + 1) * 128], wt[:, kc, 256:512],
                       start=(kc == 0), stop=(kc == 7))
                tt([pb, b_tC], [b_ktm], ktm[:, t_, :], pt[:, 0:256], tC[:, t_ * 256:(t_ + 1) * 256], ALU.mult)
            w_release()
            yield
            wt, wb = w_acquire()
            for t_ in range(ntl):
                pt, pb = bank()
                for kc in range(8):
                    mm([wb, b_hT], [pb], pt[:, :], hT[:, kc, t_ * 128:(t_ + 1) * 128], wt[:, kc, :],
                       start=(kc == 0), stop=(kc == 7))
                act([pb], [b_vbf], vbf[:, t_, :], pt[:, :], AF.Copy)
                for cl_ in range(c.cpt):
                    act([pb, cmask[c.st][1]], [b_vsel], vsel[:, t_ * c.cpt + cl_, :], pt[:, :], AF.Copy,
                        scale=cmask[c.st][0][:, cl_:cl_ + 1])
                yield
            w_release()
            for _ in gla_early(c):
                yield
            wt, wb = w_acquire()
            st, bst, mv_, bmv, rs_, brs = SS["b"]
            for t_ in range(ntl):
                pt, pb = bank()
                for kc in range(8):
                    mm([wb, b_hT], [pb], pt[:, :], hT[:, kc, t_ * 128:(t_ + 1) * 128], wt[:, kc, :],
                       start=(kc == 0), stop=(kc == 7))
                act([pb], [b_tZ], tZ[:, 0:512], pt[:, :], AF.Gelu)
                P.op("dve", [b_tZ], [bst], lambda e: e.bn_stats(out=st[:, 0, :], in_=tZ[:, 0:512]), dur=700.0)
                P.op("dve", [bst], [bmv], lambda e: e.bn_aggr(out=mv_[:, :], in_=st[:, 0, 0:6]))
                rsqrt_(bmv, mv_[:, 1:2], LN_EPS, brs, rs_[:, 0:1])
                stt([bmv, brs], [brs], rs_[:, 1:2], mv_[:, 0:1], -1.0, rs_[:, 0:1], ALU.mult, ALU.mult)
                stt([b_gg, brs, b_gb], [b_tZ], tZ[:, 512:1024], gg[:, :], rs_[:, 1:2], gb[:, :], ALU.mult, ALU.add)
                stt([b_tZ, brs, b_gg], [b_tZ], tZ[:, 0:512], tZ[:, 0:512], rs_[:, 0:1], gg[:, :], ALU.mult, ALU.mult)
                if c.sample:
                    tt([b_tZ], [b_tZ], tZ[:, 512:1024], tZ[:, 0:512], tZ[:, 512:1024], ALU.add)
                    act([b_tZ], [b_zvn], zvn[:, t_, :], tZ[:, 512:1024], AF.Copy)
                    P.dma("sp", "st_zv", [b_tZ], [], lambda e: e.dma_start(out=zv_out, in_=tZ[:, 512:1024]))
                else:
                    tt([b_tZ], [b_zvn], zvn[:, t_, :], tZ[:, 0:512], tZ[:, 512:1024], ALU.add)
                yield
            w_release()

            for (func, dst, dbuf) in ((AF.Silu, srT, b_srT), (AF.Gelu, guT, b_guT)):
                wt, wb = w_acquire()
                for cc in range(4):
                    pt, pb = bank()
                    for kc in range(8):
                        mm([wb, b_hT], [pb], pt[:, 0:n], wt[:, kc, cc * 128:(cc + 1) * 128], hT[:, kc, 0:n],
                           start=(kc == 0), stop=(kc == 7))
                    act([pb], [dbuf], dst[:, cc, 0:n], pt[:, 0:n], func)
                    if cc % 2 == 1:
                        yield
                w_release()
        def gla_early(c):
            ntl, cs, cpt = c.ntl, c.cs, c.cpt
            nchunks = ntl * cpt
            if c.sample:
                act([b_S0s], [b_Sb], Sb[:, 0:4, :, :], S0s[:, :, :, :], AF.Copy)
            else:
                act([b_S], [b_Sb], Sb[:, 0, :, :], S[:, :, :], AF.Copy)
            Mk, Mkb = (masks, b_masks) if c.sample else (maskp, b_maskp)
            for t_ in range(ntl):
                tk = slice(t_ * 128, (t_ + 1) * 128)
                attm, b_attm = attm2[t_ % 2]
                pbanks = []
                for cl_ in range(cpt):
                    if cl_ % 2 == 0:
                        pbanks.append(bank())
                    pp, ppb = pbanks[-1]
                    for h in range(4):
                        pr, hf = h // 2, h % 2
                        c0 = (cl_ % 2) * 256 + pr * 128
                        mm([b_ktm, b_vsel], [ppb], pp[hf * 64:(hf + 1) * 64, c0:c0 + 128],
                           ktm[:, t_, h * 64:(h + 1) * 64], vsel[:, t_ * cpt + cl_, h * 128:(h + 1) * 128],
                           tp=(0, hf * 64))
                pa, pab = bank()
                for h in range(4):
                    pr = h // 2
                    mm([b_ktT, b_qsel], [pab], pa[:, h * 128:(h + 1) * 128], ktT[:, pr, tk], qsel[:, h, tk])
                for h in range(4):
                    tt([pab, Mkb], [b_attm], attm[:, h, :], pa[:, h * 128:(h + 1) * 128], Mk[:, :], ALU.mult)
                for cl_ in range(cpt):
                    cidx = t_ * cpt + cl_
                    pp, ppb = pbanks[cl_ // 2]
                    if c.sample:
                        for pr in range(2):
                            ev = Et[:, pr, cidx:cidx + 1]
                            ts([b_S0s, b_Et], [b_Se], Se[:, pr, :], S0s[:, cidx, pr, :], ev, None, ALU.mult)
                            c0 = (cl_ % 2) * 256 + pr * 128
                            stt([ppb, b_Et, b_Se], [b_Sos], Sos[:, cidx, pr, :], pp[:, c0:c0 + 128], ev, Se[:, pr, :],
                                ALU.mult, ALU.add)
                    else:
                        snap = cidx + 1 if cidx + 1 < nchunks else None
                        state_update(pp, ppb, (cl_ % 2) * 256, cidx, snap)
                yield
            if c.sample:
                P.dma("sp", "st_ss", [b_Sos], [], lambda e: e.dma_start(out=ss_out.rearrange("s p a e -> p s a e"), in_=Sos[:, :, :, :]))

        def gla_late(c):
            ntl, cs, cpt, st_ = c.ntl, c.cs, c.cpt, c.st
            for t_ in range(ntl):
                tk = slice(t_ * 128, (t_ + 1) * 128)
                attm, b_attm = attm2[t_ % 2]
                osq, b_osq = osq2[t_ % 2]
                tR, b_tR = (tC, b_tC) if t_ % 2 == 0 else (tA, b_tA)
                po, pob = bank()
                for h in range(4):
                    pr = h // 2
                    mm([b_vbf, b_attm], [pob], po[:, h * 128:(h + 1) * 128], vbf[:, t_, h * 128:(h + 1) * 128],
                       attm[:, h, :], start=True, stop=False)
                    for cl_ in range(cpt):
                        cidx = t_ * cpt + cl_
                        mm([b_Sb, b_qsel], [pob], po[:, h * 128 + cl_ * cs:h * 128 + (cl_ + 1) * cs],
                           Sb[:, cidx, pr, :], qsel[:, h, t_ * 128 + cl_ * cs:t_ * 128 + (cl_ + 1) * cs],
                           start=False, stop=(cl_ == cpt - 1))
                act([pob], [b_osq], osq[:, :], po[:, :], AF.Square)
                ps, psb = bank()
                mm([b_ones, b_osq], [psb], ps[:, :], ones_bf[:, :], osq[:, :])
                rsqrt_(psb, ps[:, :], 128.0 * RMS_EPS, b_tR, tR[:, 0:512])
                tt([pob, b_tR], [b_tR], tR[:, 512:1024], po[:, :], tR[:, 0:512], ALU.mult)
                stt([b_tR, b_gn, b_srT], [b_mixT], mixT[:, 0:4, tk], tR[:, 512:1024].rearrange("p (h i) -> p h i", h=4),
                    gn128[:, 0:1], srT[:, :, tk], ALU.mult, ALU.mult)
                pg, pgb = bank()
                wst, wsb = wsT[st_]
                bt_, bbuf = bsp[st_]
                for g in range(4):
                    o = pg[:, g * 128:(g + 1) * 128]
                    mm([b_zvn, wsb], [pgb], o, zvn[:, t_, g * 128:(g + 1) * 128], wst[:, g, :], start=True, stop=False)
                    mm([b_ones0, bbuf], [pgb], o, ones0[:, :], bt_[:, 0, g * 128:(g + 1) * 128], start=False, stop=False)
                    mm([b_ones0, bbuf], [pgb], o, ones0[:, :], bt_[:, 1, g * 128:(g + 1) * 128], start=False, stop=True)
                tt([pgb, b_guT], [b_mixT], mixT[:, 4:8, tk], pg[:, :].rearrange("p (g i) -> p g i", g=4), guT[:, :, tk], ALU.mult)

        def pull(gen, k):
            if gen is None:
                return
            for _ in range(k):
                try:
                    next(gen)
                except StopIteration:
                    return

        def drain(gen):
            if gen is None:
                return
            for _ in gen:
                pass

        def sub_matmuls(c, lhs, lhsbuf, nk, ntiles_per_half, gen, tile_major):
            ybanks = {}

            def group(t_, half, wts):
                pt, pb = bank(pin=True)
                ybanks[(t_, half)] = (pt, pb)
                k = 0
                for (wt, wb) in wts:
                    for kk in range(8):
                        mm([wb, lhsbuf], [pb], pt[:, :], lhs[:, k, t_ * 128:(t_ + 1) * 128], wt[:, kk, :],
                           start=(k == 0), stop=(k == nk - 1))
                        k += 1
            if tile_major:
                w2 = [[w_acquire() for _ in range(ntiles_per_half)] for _ in range(2)]
                for t_ in range(c.ntl):
                    for half in range(2):
                        group(t_, half, w2[half])
                w_release(2 * ntiles_per_half)
            else:
                for half in range(2):
                    wts = [w_acquire() for _ in range(ntiles_per_half)]
                    for t_ in range(c.ntl):
                        group(t_, half, wts)
                        pull(gen, 2)
                    w_release(ntiles_per_half)
            return ybanks

        def residual_ln(c, ybanks, t_, Gt, Gb, Pt, Pb_, Ct, Cb, final):
            Rt, Rbuf = c.Rt, c.Rbuf
            tt([Rbuf, Pb_], [b_tA], tA[:, :], Rt[:, t_, :], Pt[:, :], ALU.mult)
            tt([b_tA, Cb], [b_tA], tA[:, :], tA[:, :], Ct[:, :], ALU.add)
            for half in range(2):
                pt, pb = ybanks[(t_, half)]
                hs = slice(half * 512, (half + 1) * 512)
                tt([pb, Gb], [b_tB], tB[:, hs], pt[:, :], Gt[:, hs], ALU.mult)
                unpin(pb)
            tt([b_tA, b_tB], [Rbuf], Rt[:, t_, :], tA[:, :], tB[:, :], ALU.add)
            if not final:
                layer_norm_rows(Rt, Rbuf, t_, Rt[:, t_, :], [Rbuf], "a")
                transpose_affine(Rt, Rbuf, t_, h2T, b_h2T, 2, c.groups)
            else:
                layer_norm_rows(Rt, Rbuf, t_, tA[:, :], [b_tA], "a")
                tt([b_tA, modP["g2"][1]], [b_tB], tB[:, :], tA[:, :], modP["g2"][0][:, :], ALU.mult)
                tt([b_tB, modP["b2"][1]], [b_yst], yst[:, :], tB[:, :], modP["b2"][0][:, :], ALU.add)
                r0 = 2048 + t_ * 128 if c.sample else c.blk * NB + t_ * 128
                P.dma("sp", "yst", [b_yst], [], lambda e, r0=r0: e.dma_start(out=y_out[r0:r0 + 128, :], in_=yst[:, :]), nbytes=524288.0)

        def mid(c, g1):
            st_ = c.st
            gla_late(c)
            G1t, G1b = modS[st_]["G1"]; C1t, C1b = modS[st_]["C1"]
            P1t, P1b = modP["P1"]
            yb = sub_matmuls(c, mixT, b_mixT, 8, 1, None, True)
            for t_ in range(c.ntl):
                residual_ln(c, yb, t_, G1t, G1b, P1t, P1b, C1t, C1b, False)
            n = c.n
            for j in range(8):
                wt, wb = w_acquire()
                for cc in range(4):
                    ch = j * 4 + cc
                    pt, pb = bank()
                    for kc in range(8):
                        mm([wb, b_h2T], [pb], pt[:, 0:n], wt[:, kc, cc * 128:(cc + 1) * 128], h2T[:, kc, 0:n],
                           start=(kc == 0), stop=(kc == 7))
                    act([pb, b_vecT], [b_tZ], tZ[:, 0:n], pt[:, 0:n], AF.Relu, bias=vecT[:, V_BUP + ch:V_BUP + ch + 1])
                    tt([b_tZ], [b_big], big[:, ch, 0:n], tZ[:, 0:n], tZ[:, 0:n], ALU.mult)
                w_release()
                pull(g1, 1)

        def tail(c, nxt, g1):
            st_ = c.st
            G2t, G2b = modS[st_]["G2"]; C2t, C2b = modS[st_]["C2"]
            P2t, P2b = modP["P2"]
            yb = sub_matmuls(c, big, b_big, 32, 4, g1, False)
            drain(g1)
            g2 = head2(nxt) if nxt is not None else None
            for t_ in range(c.ntl):
                residual_ln(c, yb, t_, G2t, G2b, P2t, P2b, C2t, C2b, True)
                pull(g2, 7)
            drain(g2)

        P.op("dve", [], [b_S], lambda e: e.memset(S[:], 0.0))
        P.op("dve", [], [b_Dt], lambda e: e.memset(Dt[:], 1.0))
        for step in range(NPRE + 2):
            round_robin([preA(step) if step < NPRE else None,
                         preB(step - 1) if 0 <= step - 1 < NPRE else None,
                         preC(step - 2) if 0 <= step - 2 < NPRE else None])
        cc_in = nc.dram_tensor("cc_in", [128, 258], F32, kind="Internal").ap()
        cc_out = nc.dram_tensor("cc_out", [512, 258], F32, kind="Internal", addr_space="Local").ap()
        b_ccin, b_ccout = Buf("cc_in"), Buf("cc_out")
        P.dma("sp", "ccw", [b_S], [b_ccin], lambda e: e.dma_start(out=cc_in[:, 0:256], in_=S[:, :, :].rearrange("p a e -> p (a e)")))
        P.dma("sp", "ccw", [b_Dt], [b_ccin], lambda e: e.dma_start(out=cc_in[:, 256:258], in_=Dt[:, :]))
        if USE_CC:
            P.op("pool", [b_ccin], [b_ccout], lambda e: e.collective_compute(
                "AllGather", ALU.bypass, replica_groups=[[0, 1, 2, 3], [4, 5, 6, 7]], ins=[cc_in], outs=[cc_out]),
                dur=30000.0)
        else:
            for r_ in range(4):
                P.dma("sp", "ccw", [b_ccin], [b_ccout], lambda e, r_=r_: e.dma_start(out=cc_out[r_ * 128:(r_ + 1) * 128, :], in_=cc_in))
        P.op("dve", [b_ccin], [b_S], lambda e: e.memset(S[:], 0.0))
        for p_ in range(4):
            mcol = qmask[:, p_:p_ + 1]
            P.dma("sp", "ccr", [b_ccout], [b_Gt], lambda e, p_=p_: e.dma_start(out=Gt[:, :], in_=cc_out[p_ * 128:(p_ + 1) * 128, :]))
            ts([b_Gt, b_qmask], [b_Dp], Dp[:, :], Gt[:, 256:258], -1.0, mcol, ALU.add, ALU.mult)
            ts([b_Dp], [b_Dp], Dp[:, :], Dp[:, :], 1.0, None, ALU.add)
            for pr in range(2):
                ts([b_S, b_Dp], [b_Se], Se[:, pr, :], S[:, pr, :], Dp[:, pr:pr + 1], None, ALU.mult)
                stt([b_Gt, b_qmask, b_Se], [b_S], S[:, pr, :], Gt[:, pr * 128:(pr + 1) * 128], mcol, Se[:, pr, :],
                    ALU.mult, ALU.add)
        ctxs = []
        for blk in range(NPB):
            ctxs.append(make_ctx("main", blk, len(ctxs) % 2))
        if DO_SAMPLE:
            ctxs.append(make_ctx("sample", 0, len(ctxs) % 2))
        if ctxs:
            drain(head1(ctxs[0]))
            drain(head2(ctxs[0]))
        for idx, c in enumerate(ctxs):
            nxt = ctxs[idx + 1] if idx + 1 < len(ctxs) else None
            g1 = head1(nxt) if nxt is not None else None
            mid(c, g1)
            tail(c, nxt, g1)
        P.dma("sp", "st_sp", [b_S], [], lambda e: e.dma_start(out=sp_out, in_=S[:, :, :]))

        P.fixed = set(FIXED)
        P.codegen()

        with nc.Block() as block:
            @block.sync
            def _(e):
                P.replay("sp", e)

            @block.gpsimd
            def _(e):
                P.replay("pool", e)

            @block.tensor
            def _(e):
                P.replay("pe", e)

            @block.scalar
            def _(e):
                P.replay("act", e)

            @block.vector
            def _(e):
                P.replay("dve", e)
    return nc


_NC = None


def _consts():
    j = np.arange(128)[:, None]
    i = np.arange(128)[None, :]
    c = {}
    c["ident"] = np.eye(128, dtype=np.float32)
    c["maskp"] = ((j <= i) & (j // 64 == i // 64)).astype(np.float32)
    c["masks"] = ((j <= i) & (j // 32 == i // 32)).astype(np.float32)
    c["U_p"] = (c["maskp"] * np.float32(-1.0 / 16.0)).astype(np.float32)
    c["U_s"] = (c["masks"] * np.float32(-1.0 / 16.0)).astype(np.float32)
    c["U_pre"] = ((j <= i).astype(np.float32) * np.float32(-1.0 / 16.0)).astype(np.float32)
    c["trilT"] = (j <= i).astype(np.float32)
    p = np.arange(128)[:, None]
    cc = np.arange(4)[None, :]
    c["cmask_p"] = (p // 64 == cc).astype(np.float32)
    c["cmask_s"] = (p // 32 == cc).astype(np.float32)
    return c


def prepare(x_prompt, x_sample, state_gla, c_prompt, c_sample, ln_in_g, ln_in_b, w_ada, b_ada, w_in,
            w_gate2, b_gate2, gla_norm_g, gmlp_ln_g, gmlp_ln_b, w_spatial, b_spatial, w_o, b_o,
            ln1_g, ln1_b, w_up, b_up, w_down, b_down, ln2_g, ln2_b):
    f = lambda a: np.ascontiguousarray(np.asarray(a, dtype=np.float32))
    x_prompt, x_sample, state_gla = f(x_prompt), f(x_sample), f(state_gla)
    c_prompt, c_sample = f(c_prompt), f(c_sample)
    b_ada0 = f(b_ada)[0]
    consts = _consts()

    def chunksT(v):
        return f(v).reshape(-1, 128).T

    vecT = np.concatenate([
        chunksT(ln_in_g), chunksT(ln_in_b), chunksT(f(ln1_g)[0]), chunksT(f(ln1_b)[0]),
        chunksT(b_ada0[0:1024]), chunksT(b_ada0[1024:2048]), chunksT(b_ada0[3072:4096]), chunksT(b_ada0[4096:5120]),
        chunksT(f(b_up)[0]), chunksT(f(gla_norm_g)[0])], axis=1)
    bcast = lambda v: np.ascontiguousarray(np.broadcast_to(f(v).reshape(1, -1), (128, f(v).size)))
    shared = {
        "w_ada": f(w_ada)[0], "w_in": f(w_in)[0], "w_o": f(w_o)[0], "w_up": f(w_up)[0], "w_down": f(w_down)[0],
        "vecT": np.ascontiguousarray(vecT),
        "bc_g_in": bcast(ln_in_g), "bc_b_in": bcast(ln_in_b), "bc_b_o": bcast(f(b_o)[0]),
        "bc_g1": bcast(f(ln1_g)[0]), "bc_b1": bcast(f(ln1_b)[0]), "bc_b_down": bcast(f(b_down)[0]),
        "bc_g2": bcast(f(ln2_g)[0]), "bc_b2": bcast(f(ln2_b)[0]),
        "bc_ba_g1": bcast(b_ada0[2048:3072]), "bc_ba_g2": bcast(b_ada0[5120:6144]),
        "bc_bg2": bcast(f(b_gate2)[0]), "bc_gg": bcast(f(gmlp_ln_g)[0]), "bc_gb": bcast(f(gmlp_ln_b)[0]),
        "w_gate2": f(w_gate2)[0],
        "wsp_p": f(w_spatial)[0],
        "bsp_p": f(b_spatial)[0].reshape(1, 512),
        "bsp_s": np.ascontiguousarray(np.tile(f(b_spatial)[0][:, :32], (1, 4)).reshape(1, 512)),
    }
    wsp_s = np.zeros((4, 128, 128), np.float32)
    for s_ in range(4):
        wsp_s[:, 32 * s_:32 * s_ + 32, 32 * s_:32 * s_ + 32] = f(w_spatial)[0][:, :32, :32]
    shared["wsp_s"] = wsp_s
    shared.update(consts)

    in_maps = []
    for ci in range(8):
        b, q = ci // 4, ci % 4
        m = dict(shared)
        xs = x_sample[4 * ci:4 * ci + 4].reshape(128, D)
        m["xmain"] = np.ascontiguousarray(np.concatenate([x_prompt[b, 2048 * q:2048 * (q + 1)], xs], axis=0))
        m["qmask"] = np.ascontiguousarray(np.broadcast_to((np.arange(4) < q).astype(np.float32)[None, :], (128, 4)))
        ctp = c_prompt[b].reshape(8, 128).T
        m["cT_p"] = np.ascontiguousarray(np.broadcast_to(ctp[:, :, None], (128, 8, 128)))
        cs4 = c_sample[4 * ci:4 * ci + 4]
        cts = cs4.reshape(4, 8, 128).transpose(2, 1, 0)
        m["cT_s"] = np.ascontiguousarray(np.repeat(cts, 32, axis=2))
        st = state_gla[0, 4 * ci:4 * ci + 4]
        st = st.reshape(4, 2, 2, 64, 128).transpose(0, 2, 3, 1, 4)
        m["st0"] = np.ascontiguousarray(st.reshape(4, 128, 2, 128))
        in_maps.append(m)

    return in_maps


def kernel(**inputs):
    global _NC
    if _NC is None:
        _NC = build_program()
    in_maps = prepare(**inputs)
    res = run_bass_kernel_spmd(_NC, in_maps, core_ids=list(range(8)))
    return assemble(res.results)


def assemble(R):
    y_prompt = np.zeros((2, 8192, D), np.float32)
    y_sample = np.zeros((32, 32, D), np.float32)
    ns_p = np.zeros((1, 2, 4, 64, 128), np.float32)
    ns_s = np.zeros((1, 32, 4, 64, 128), np.float32)
    nv_s = np.zeros((1, 32, 32, 512), np.float32)

    def unstate(a):
        return a.reshape(2, 64, 2, 128).transpose(2, 0, 1, 3).reshape(4, 64, 128)
    for ci in range(8):
        b, q = ci // 4, ci % 4
        y = R[ci]["y"]
        y_prompt[b, 2048 * q:2048 * (q + 1)] = y[0:2048]
        y_sample[4 * ci:4 * ci + 4] = y[2048:2176].reshape(4, 32, D)
        if q == 3:
            ns_p[0, b] = unstate(R[ci]["s_out_p"])
        so = R[ci]["s_out_s"]
        for s_ in range(4):
            ns_s[0, 4 * ci + s_] = unstate(so[s_])
        nv_s[0, 4 * ci:4 * ci + 4] = R[ci]["zv_out"].reshape(4, 32, 512)
    return (y_prompt, y_sample, ns_p, ns_s, nv_s)
```
